# Optimizing a Trainium2 kernel written in Bass

```python
import math
import jax, jax.numpy as jnp
from jax import lax
import numpy as np

D_MODEL = 1024
BATCH = 8
SEQ = 4096
DEPTH = 2
DEC_BATCH = 8
DEC_SEQ = 64
PAST_LEN = 2048

CHUNK = 64
N_HEADS = 8
N_KV_HEADS = 2
HEAD_DIM = 64
GROUP = N_HEADS // N_KV_HEADS
ATT_WIDTH = N_HEADS * HEAD_DIM
KV_WIDTH = N_KV_HEADS * HEAD_DIM
WINDOW = 128
WINDOW_CHUNKS = WINDOW // CHUNK
BAND = (WINDOW_CHUNKS + 1) * CHUNK
CONV_WIDTH = 512
CONV_K = 3
NUM_BUCKETS = 32
MAX_DISTANCE = 128
N_EXPERTS = 32
TOP_K = 4
D_FF = 1024
SWIGLU_LIMIT = 7.0
SWIGLU_ALPHA = 1.702
MOE_BLOCK = 256
LN_EPS = 1e-5
NEG_INF = -1e30
DEEPNORM_ALPHA = (2 * DEPTH) ** 0.25
DEEPNORM_BETA = (8 * DEPTH) ** -0.25
IN_SIZES = (ATT_WIDTH, KV_WIDTH, KV_WIDTH, CONV_WIDTH, CONV_WIDTH, CONV_WIDTH, D_MODEL, D_MODEL)
IN_WIDTH = sum(IN_SIZES)
IN_SPLITS = tuple(int(s) for s in np.cumsum(IN_SIZES)[:-1])

kernel_name = "chunk_causal_swa_shortconv_moe_step"


def _layernorm(x, g, b):
    xf = x.astype(jnp.float32)
    mu = jnp.mean(xf, axis=-1, keepdims=True)
    var = jnp.mean(jnp.square(xf - mu), axis=-1, keepdims=True)
    return ((xf - mu) * lax.rsqrt(var + LN_EPS)).astype(x.dtype) * g + b


def _rel_bucket(rel):
    half = NUM_BUCKETS // 2
    max_exact = half // 2
    n = jnp.abs(rel)
    n_f = jnp.maximum(n, 1).astype(jnp.float32)
    large = max_exact + (jnp.log(n_f / max_exact) / math.log(MAX_DISTANCE / max_exact)
                         * (half - max_exact)).astype(jnp.int32)
    large = jnp.minimum(large, half - 1)
    return jnp.where(rel > 0, half, 0) + jnp.where(n < max_exact, n, large)


def _attn_bias(rel, rel_table):
    q_len, k_len = rel.shape
    b = rel_table[_rel_bucket(rel)]
    return jnp.transpose(b, (2, 0, 1)).reshape(N_KV_HEADS, GROUP, q_len, k_len)


def _sink_softmax(s, sink):
    sink_b = jnp.broadcast_to(sink.astype(jnp.float32)[:, :, None, None], s.shape[:-1] + (1,))
    p = jax.nn.softmax(jnp.concatenate([s, sink_b], axis=-1), axis=-1)
    return p[..., :-1]


def _attn_prompt(q, k, v, sink, rel_table):
    b, t, _ = q.shape
    nc = t // CHUNK
    q = q.reshape(b, nc, CHUNK, N_KV_HEADS, GROUP, HEAD_DIM)
    k = k.reshape(b, nc, CHUNK, N_KV_HEADS, HEAD_DIM)
    v = v.reshape(b, nc, CHUNK, N_KV_HEADS, HEAD_DIM)
    pad = ((0, 0), (WINDOW_CHUNKS, 0), (0, 0), (0, 0), (0, 0))
    kp, vp = jnp.pad(k, pad), jnp.pad(v, pad)
    kb = jnp.concatenate([kp[:, w:w + nc] for w in range(WINDOW_CHUNKS + 1)], axis=2)
    vb = jnp.concatenate([vp[:, w:w + nc] for w in range(WINDOW_CHUNKS + 1)], axis=2)
    rel = jnp.arange(BAND)[None, :] - WINDOW - jnp.arange(CHUNK)[:, None]
    bias = _attn_bias(rel, rel_table)
    valid = (jnp.arange(nc)[:, None] * CHUNK + jnp.arange(BAND)[None, :] - WINDOW) >= 0
    s = jnp.einsum('bnqkgd,bnskd->bnkgqs', q, kb, preferred_element_type=jnp.float32) * (HEAD_DIM ** -0.5)
    s = jnp.where(valid[None, :, None, None, None, :], s + bias[None, None], NEG_INF)
    p = _sink_softmax(s, sink.reshape(N_KV_HEADS, GROUP))
    o = jnp.einsum('bnkgqs,bnskd->bnqkgd', p.astype(vb.dtype), vb)
    return o.reshape(b, t, ATT_WIDTH)


def _attn_sample(q, k_new, v_new, k_cache, v_cache, sink, rel_table):
    b, s_len, _ = q.shape
    w = k_cache.shape[1]
    k_all = jnp.concatenate([k_cache, k_new.reshape(b, s_len, N_KV_HEADS, HEAD_DIM)], axis=1)
    v_all = jnp.concatenate([v_cache, v_new.reshape(b, s_len, N_KV_HEADS, HEAD_DIM)], axis=1)
    rel = jnp.arange(w + s_len)[None, :] - w - jnp.arange(s_len)[:, None]
    bias = _attn_bias(rel, rel_table)
    qh = q.reshape(b, s_len, N_KV_HEADS, GROUP, HEAD_DIM)
    s = jnp.einsum('bqkgd,bskd->bkgqs', qh, k_all, preferred_element_type=jnp.float32) * (HEAD_DIM ** -0.5) + bias[None]
    p = _sink_softmax(s, sink.reshape(N_KV_HEADS, GROUP))
    o = jnp.einsum('bkgqs,bskd->bqkgd', p.astype(v_all.dtype), v_all)
    return o.reshape(b, s_len, ATT_WIDTH), k_all[:, -w:], v_all[:, -w:]


def _short_conv(u, hist, conv_w):
    full = jnp.concatenate([hist, u], axis=1)
    y = lax.conv_general_dilated(full, conv_w.astype(u.dtype)[:, None, :], window_strides=(1,),
                                 padding='VALID', dimension_numbers=('NWC', 'WIO', 'NWC'),
                                 feature_group_count=CONV_WIDTH)
    return y, full[:, -(CONV_K - 1):]


def _moe(h, w_router, b_router, w_gu, b_gu, w_down, b_down):
    n, d = h.shape
    logits = (h @ w_router + b_router).astype(jnp.float32)
    top_val, top_idx = lax.top_k(logits, TOP_K)
    gates = jax.nn.softmax(top_val, axis=-1)
    nk = n * TOP_K
    flat_e = top_idx.reshape(nk)
    order = jnp.argsort(flat_e)
    e_sorted = flat_e[order]
    tok = order // TOP_K
    counts = jnp.bincount(flat_e, length=N_EXPERTS)
    padded = (counts + MOE_BLOCK - 1) // MOE_BLOCK * MOE_BLOCK
    pad_end = jnp.cumsum(padded)
    pad_start = pad_end - padded
    start = jnp.cumsum(counts) - counts
    dest = pad_start[e_sorted] + jnp.arange(nk) - start[e_sorted]
    n_blocks = -(-nk // MOE_BLOCK) + N_EXPERTS
    xs = jnp.zeros((n_blocks * MOE_BLOCK, d), h.dtype).at[dest].set(h[tok])
    block_e = jnp.minimum(jnp.searchsorted(pad_end, jnp.arange(n_blocks) * MOE_BLOCK, side='right'),
                          N_EXPERTS - 1)

    def expert_block(args):
        xb, e = args
        gu = xb @ w_gu[e] + b_gu[e]
        g, lin = jnp.split(gu, 2, axis=-1)
        g = jnp.minimum(g, SWIGLU_LIMIT)
        lin = jnp.clip(lin, -SWIGLU_LIMIT, SWIGLU_LIMIT)
        a = g * jax.nn.sigmoid(SWIGLU_ALPHA * g) * (lin + 1)
        return a @ w_down[e] + b_down[e]

    ys = lax.map(expert_block, (xs.reshape(n_blocks, MOE_BLOCK, d), block_e)).reshape(-1, d)
    contrib = ys[dest] * gates.reshape(nk)[order][:, None].astype(ys.dtype)
    return jax.ops.segment_sum(contrib, tok, num_segments=n)


def _layer(x, c, hist_k, hist_v, hist_u, rel_table, w_ada, b_ada, w_in, b_in, sink, conv_w,
           w_oa, w_ob, w_o, ln1_g, ln1_b, w_router, b_router, w_gu, b_gu, w_down, b_down, ln2_g, ln2_b):
    b, t, d = x.shape
    mod = (jax.nn.silu(c) @ w_ada + b_ada)[:, None, :]
    sh1, sc1, g1, sh2, sc2, g2 = jnp.split(mod, 6, axis=-1)
    h = x * (1 + sc1) + sh1
    z = h @ w_in + b_in
    q, k, v, cb, cc, cx, ga, gb = jnp.split(z, IN_SPLITS, axis=-1)
    if hist_k is None:
        ya = _attn_prompt(q, k, v, sink, rel_table)
        new_k = k.reshape(b, t, N_KV_HEADS, HEAD_DIM)[:, -WINDOW:]
        new_v = v.reshape(b, t, N_KV_HEADS, HEAD_DIM)[:, -WINDOW:]
        hist_u = jnp.zeros((b, CONV_K - 1, CONV_WIDTH), cx.dtype)
    else:
        ya, new_k, new_v = _attn_sample(q, k, v, hist_k, hist_v, sink, rel_table)
    yc, new_u = _short_conv(cc * cx, hist_u, conv_w)
    yb = cb * yc
    mix = (jax.nn.sigmoid(ga) * (ya @ w_oa) + jax.nn.sigmoid(gb) * (yb @ w_ob)) @ w_o
    x = _layernorm(DEEPNORM_ALPHA * x + (1 + g1) * mix, ln1_g, ln1_b)
    h = x * (1 + sc2) + sh2
    ff = _moe(h.reshape(b * t, d), w_router, b_router, w_gu, b_gu, w_down, b_down).reshape(b, t, d)
    x = _layernorm(DEEPNORM_ALPHA * x + (1 + g2) * ff, ln2_g, ln2_b)
    return x, new_k, new_v, new_u


def setup_inputs(seed: int = 0) -> dict:
    key = jax.random.key(seed)
    ks = jax.random.split(key, 32)

    def nrm(k, shape, s):
        return jax.random.normal(k, shape, jnp.float32) * s

    w_cache = min(WINDOW, PAST_LEN)
    col_scale = jnp.concatenate([jnp.full((n,), DEEPNORM_BETA if i == 2 else 1.0, jnp.float32)
                                 for i, n in enumerate(IN_SIZES)])
    return {
        "x_prompt": nrm(ks[0], (BATCH, SEQ, D_MODEL), 1.0),
        "x_sample": nrm(ks[1], (DEC_BATCH, DEC_SEQ, D_MODEL), 1.0),
        "c_prompt": nrm(ks[2], (BATCH, D_MODEL), 1.0),
        "c_sample": nrm(ks[3], (DEC_BATCH, D_MODEL), 1.0),
        "cache_k": nrm(ks[4], (DEPTH, DEC_BATCH, w_cache, N_KV_HEADS, HEAD_DIM), 1.0),
        "cache_v": nrm(ks[5], (DEPTH, DEC_BATCH, w_cache, N_KV_HEADS, HEAD_DIM), 1.0),
        "state_conv": nrm(ks[6], (DEPTH, DEC_BATCH, CONV_K - 1, CONV_WIDTH), 1.0),
        "rel_table": nrm(ks[7], (NUM_BUCKETS, N_HEADS), 0.5),
        "ln0_g": 1.0 + nrm(ks[8], (D_MODEL,), 0.01),
        "ln0_b": nrm(ks[9], (D_MODEL,), 0.01),
        "w_ada": nrm(ks[10], (DEPTH, D_MODEL, 6 * D_MODEL), 0.1 * D_MODEL ** -0.5),
        "b_ada": nrm(ks[11], (DEPTH, 6 * D_MODEL), 0.01),
        "w_in": nrm(ks[12], (DEPTH, D_MODEL, IN_WIDTH), D_MODEL ** -0.5) * col_scale,
        "b_in": nrm(ks[13], (DEPTH, IN_WIDTH), 0.01),
        "sinks": nrm(ks[14], (DEPTH, N_HEADS), 1.0),
        "conv_w": nrm(ks[15], (DEPTH, CONV_K, CONV_WIDTH), CONV_K ** -0.5),
        "w_oa": nrm(ks[16], (DEPTH, ATT_WIDTH, D_MODEL), DEEPNORM_BETA * ATT_WIDTH ** -0.5),
        "w_ob": nrm(ks[17], (DEPTH, CONV_WIDTH, D_MODEL), DEEPNORM_BETA * CONV_WIDTH ** -0.5),
        "w_o": nrm(ks[18], (DEPTH, D_MODEL, D_MODEL), DEEPNORM_BETA * D_MODEL ** -0.5),
        "ln1_g": 1.0 + nrm(ks[19], (DEPTH, D_MODEL), 0.01),
        "ln1_b": nrm(ks[20], (DEPTH, D_MODEL), 0.01),
        "w_router": nrm(ks[21], (DEPTH, D_MODEL, N_EXPERTS), D_MODEL ** -0.5),
        "b_router": nrm(ks[22], (DEPTH, N_EXPERTS), 0.01),
        "w_gu": nrm(ks[23], (DEPTH, N_EXPERTS, D_MODEL, 2 * D_FF), DEEPNORM_BETA * D_MODEL ** -0.5),
        "b_gu": nrm(ks[24], (DEPTH, N_EXPERTS, 2 * D_FF), 0.01),
        "w_down": nrm(ks[25], (DEPTH, N_EXPERTS, D_FF, D_MODEL), DEEPNORM_BETA * D_FF ** -0.5),
        "b_down": nrm(ks[26], (DEPTH, N_EXPERTS, D_MODEL), 0.01),
        "ln2_g": 1.0 + nrm(ks[27], (DEPTH, D_MODEL), 0.01),
        "ln2_b": nrm(ks[28], (DEPTH, D_MODEL), 0.01),
    }


def reference(x_prompt, x_sample, c_prompt, c_sample, cache_k, cache_v, state_conv, rel_table,
              ln0_g, ln0_b, w_ada, b_ada, w_in, b_in, sinks, conv_w, w_oa, w_ob, w_o, ln1_g, ln1_b,
              w_router, b_router, w_gu, b_gu, w_down, b_down, ln2_g, ln2_b):
    y_prompt = _layernorm(x_prompt, ln0_g, ln0_b)
    y_sample = _layernorm(x_sample, ln0_g, ln0_b)
    kp, vp, up, ksm, vsm, usm = [], [], [], [], [], []
    for l in range(DEPTH):
        lw = (w_ada[l], b_ada[l], w_in[l], b_in[l], sinks[l], conv_w[l], w_oa[l], w_ob[l], w_o[l],
              ln1_g[l], ln1_b[l], w_router[l], b_router[l], w_gu[l], b_gu[l], w_down[l], b_down[l],
              ln2_g[l], ln2_b[l])
        y_prompt, k1, v1, u1 = _layer(y_prompt, c_prompt, None, None, None, rel_table, *lw)
        y_sample, k2, v2, u2 = _layer(y_sample, c_sample, cache_k[l], cache_v[l], state_conv[l], rel_table, *lw)
        kp.append(k1); vp.append(v1); up.append(u1)
        ksm.append(k2); vsm.append(v2); usm.append(u2)
    return (y_prompt, y_sample, jnp.stack(kp), jnp.stack(vp), jnp.stack(up),
            jnp.stack(ksm), jnp.stack(vsm), jnp.stack(usm))
```

```python
import math
from contextlib import ExitStack

import numpy as np
import concourse.bass as bass
import concourse.mybir as mybir
from concourse.bass_utils import run_bass_kernel_spmd

F32 = mybir.dt.float32
BF16 = mybir.dt.bfloat16
I32 = mybir.dt.int32
U32 = mybir.dt.uint32
AF = mybir.ActivationFunctionType
ALU = mybir.AluOpType
AX = mybir.AxisListType

D = 1024
DEPTH = 2
NEXP = 32
DFF = 1024
INW = 4352
LN_EPS = 1e-5
ALPHA = float((2 * DEPTH) ** 0.25)
NCORES = 8
SEQ = 4096
DEC = 64

ENGINES = ("tensor", "vector", "scalar", "gpsimd", "sync")


class Sem:
    def __init__(self, handle, name):
        self.h = handle
        self.name = name
        self.count = 0


class Buf:
    __slots__ = ("name", "w", "r")

    def __init__(self, name):
        self.name = name
        self.w = {}
        self.r = {}


class Op:
    __slots__ = ("eng", "fn", "deps", "seq", "kind", "sem", "ndma", "val")


class Prog:
    def __init__(self, nc, same_engine_sync=True):
        self.nc = nc
        self.ops = {e: [] for e in ENGINES}
        self.nseq = {e: 0 for e in ENGINES}
        self.same_engine_sync = same_engine_sync
        self.milestones = {e: set() for e in ENGINES}
        self.all_ops = []
        self.ALL = Buf("ALL")

    @staticmethod
    def _merge(dst, key, tok, hard):
        old = dst.get(key)
        if old is None:
            dst[key] = (tok, hard)
        else:
            t = tok if tok[2] > old[0][2] else old[0]
            dst[key] = (t, hard or old[1])

    def op(self, eng, fn, reads=(), writes=(), dma_sem=None, ndma=1, barrier=False):
        o = Op()
        o.eng = eng
        o.fn = fn
        o.kind = 'd' if dma_sem is not None else 'c'
        o.sem = dma_sem
        o.ndma = ndma
        reads = list(reads)
        writes = list(writes)
        if barrier:
            writes.append(self.ALL)
        else:
            reads.append(self.ALL)
        deps = {}
        for b in reads:
            for k, t in b.w.items():
                self._merge(deps, k, t, True)
        for b in writes:
            for k, t in b.w.items():
                self._merge(deps, k, t, True)
            for k, t in b.r.items():
                self._merge(deps, k, t, b is not self.ALL)
        if dma_sem is not None and dma_sem.count > 0:
            self._merge(deps, ('d', id(dma_sem)), ('d', dma_sem, dma_sem.count), True)
        o.deps = deps
        o.seq = self.nseq[eng]
        self.nseq[eng] += 1
        if o.kind == 'd':
            dma_sem.count += 16 * ndma
            o.val = dma_sem.count
            tok = ('d', dma_sem, o.val)
            key = ('d', id(dma_sem))
        else:
            tok = ('c', eng, o.seq)
            key = ('c', eng)
        for b in writes:
            b.w = {key: tok}
            b.r = {}
        for b in reads:
            if b in writes:
                continue
            old = b.r.get(key)
            if old is None or old[2] < tok[2]:
                b.r[key] = tok
        self.ops[eng].append(o)
        self.all_ops.append(o)
        return o

    def barrier(self):
        for e in ENGINES:
            self.op(e, lambda en: None, barrier=True)

    def _skip(self, o, tok, hard):
        e2 = tok[1]
        if e2 != o.eng:
            return False
        if e2 == "tensor":
            return True
        if e2 == "sync":
            return True
        return (not hard) or (not self.same_engine_sync)

    def emit(self, block, esems):
        for o in self.all_ops:
            for key, (tok, hard) in o.deps.items():
                if tok[0] != 'c' or self._skip(o, tok, hard):
                    continue
                self.milestones[tok[1]].add(tok[2])
        rank = {e: {s: i + 1 for i, s in enumerate(sorted(st))} for e, st in self.milestones.items()}

        def make(eng):
            def body(e):
                waited = {}
                for o in self.ops[eng]:
                    for key, (tok, hard) in o.deps.items():
                        if tok[0] == 'c':
                            if self._skip(o, tok, hard):
                                continue
                            semh = esems[tok[1]]
                            val = rank[tok[1]][tok[2]]
                            wk = ('c', tok[1])
                        else:
                            semh = tok[1].h
                            val = tok[2]
                            wk = ('d', id(tok[1]))
                        if waited.get(wk, 0) >= val:
                            continue
                        waited[wk] = val
                        e.wait_ge(semh, val)
                    res = o.fn(e)
                    if o.kind == 'd':
                        if not isinstance(res, (list, tuple)):
                            res = [res]
                        assert len(res) == o.ndma, (len(res), o.ndma)
                        for ins in res:
                            ins.then_inc(o.sem.h, 16)
                    elif o.seq in self.milestones[eng]:
                        if isinstance(res, (list, tuple)):
                            res = res[-1]
                        if res is None:
                            res = e.nop()
                        res.then_inc(esems[eng], 1)
            return body

        for eng in ENGINES:
            if self.ops[eng]:
                getattr(block, eng)(make(eng))


class TB:
    def __init__(self, t, name):
        self.t = t
        self.b = Buf(name)

    def __getitem__(self, k):
        return self.t[k]


def _rel_bucket_np(rel):
    half = 16
    max_exact = 8
    n = np.abs(rel)
    n_f = np.maximum(n, 1).astype(np.float32)
    large = max_exact + (np.log(n_f / np.float32(max_exact)) / np.float32(math.log(128 / max_exact))
                         * np.float32(half - max_exact)).astype(np.int32)
    large = np.minimum(large, half - 1)
    return np.where(rel > 0, half, 0) + np.where(n < max_exact, n, large)


def host_consts(C):
    q = np.arange(128)[:, None]
    kk = np.arange(256)[None, :]
    rel = kk - 128 - q
    bk = _rel_bucket_np(rel)
    oht = np.zeros((32, 256, 128), np.float32)
    for b in range(32):
        oht[b] = (bk == b).T.astype(np.float32)
    valid = np.where(q < 64, kk < 192, kk >= 64)
    mask = np.where(valid, 0.0, -1e30).astype(np.float32)
    ident = np.eye(128, dtype=np.float32)
    ustrict = (np.arange(128)[:, None] < np.arange(128)[None, :]).astype(np.float32)
    iotac = np.tile((np.arange(32) * C).astype(np.float32)[None, :], (128, 1))
    return {"c_oht": oht.reshape(32, 256 * 128), "c_mask": mask, "c_ident": ident,
            "c_ustrict": ustrict, "c_iotac": iotac}


def build(NT, C, dbg=False):
    TP = NT * 128
    TT = TP + DEC
    NTILE = NT + 1
    NSLOT = NEXP * C
    CT = C // 128
    BIG = 1.0e6
    assert C % 128 == 0

    nc = bass.Bass("TRN2", target_bir_lowering=False)

    def din(name, shape, dt=F32):
        return nc.dram_tensor(name, list(shape), dt, kind="ExternalInput").ap()

    def dout(name, shape, dt=F32):
        return nc.dram_tensor(name, list(shape), dt, kind="ExternalOutput").ap()

    def dscr(name, shape, dt=F32):
        return nc.dram_tensor(name, list(shape), dt, kind="Internal").ap()

    xp = din("xp", [TP, D]); xsm = din("xsm", [DEC, D]); cvec = din("cvec", [2, D])
    ck = din("ck", [DEPTH, 128, 128]); cv = din("cv", [DEPTH, 128, 128]); sconv = din("sconv", [DEPTH, 2, 512])
    rel_table = din("rel_table", [32, 8]); ln0_g = din("ln0_g", [1, D]); ln0_b = din("ln0_b", [1, D])
    w_ada = din("w_ada", [DEPTH, D, 6 * D]); b_ada = din("b_ada", [DEPTH, 6 * D])
    w_in = din("w_in", [DEPTH, D, INW]); b_in = din("b_in", [DEPTH, INW])
    sinks = din("sinks", [DEPTH, 8]); conv_w = din("conv_w", [DEPTH, 3 * 512])
    w_oa = din("w_oa", [DEPTH, 512, D]); w_ob = din("w_ob", [DEPTH, 512, D]); w_o = din("w_o", [DEPTH, D, D])
    ln1_g = din("ln1_g", [DEPTH, D]); ln1_b = din("ln1_b", [DEPTH, D])
    w_router = din("w_router", [DEPTH, D, NEXP]); b_router = din("b_router", [DEPTH, NEXP])
    w_gu = din("w_gu", [DEPTH, NEXP, D, 2 * DFF]); b_gu = din("b_gu", [DEPTH, NEXP, 2 * DFF])
    w_down = din("w_down", [DEPTH, NEXP, DFF, D]); b_down = din("b_down", [DEPTH, NEXP, D])
    ln2_g = din("ln2_g", [DEPTH, D]); ln2_b = din("ln2_b", [DEPTH, D])
    c_oht = din("c_oht", [32, 256 * 128]); c_mask = din("c_mask", [128, 256]); c_ident = din("c_ident", [128, 128])
    c_ustrict = din("c_ustrict", [128, 128]); c_iotac = din("c_iotac", [128, 32])

    y_p = dout("y_p", [TP, D]); y_s = dout("y_s", [DEC, D])
    nk_p = dout("nk_p", [DEPTH, 128, 128]); nv_p = dout("nv_p", [DEPTH, 128, 128]); nu_p = dout("nu_p", [DEPTH, 2, 512])
    cnt_out = dout("cnt_out", [DEPTH, 128, 32])
    nk_s = dout("nk_s", [DEPTH, 128, 128]); nv_s = dout("nv_s", [DEPTH, 128, 128]); nu_s = dout("nu_s", [DEPTH, 2, 512])

    if dbg:
        dbgX0 = dout("dbgX0", [TT, D]); dbgM = dout("dbgM", [DEPTH, 2, 6 * D]); dbgX1 = dout("dbgX1", [TT, D]); dbgB = dout("dbgB", [128, 8 * 256])
        dbgD = dout("dbgD", [128, NTILE * 4], I32); dbgG = dout("dbgG", [128, NTILE * 4]); dbgYS = dout("dbgYS", [NSLOT, D]); dbgXS = dout("dbgXS", [NSLOT, D], BF16)
    X = dscr("X", [TT, D]); XS = dscr("XS", [NSLOT, D], BF16); YS = dscr("YS", [NSLOT, D])
    MODS = dscr("MODS", [DEPTH, 2, 6 * D])

    st = ExitStack()
    E = st.enter_context
    nsem = [0]

    def sem(name):
        nsem[0] += 1
        return Sem(E(nc.semaphore(name)), name)

    esems = {e: E(nc.semaphore("es_" + e)) for e in ENGINES}
    P = Prog(nc)

    uniq = [0]

    def sb(name, shape, dt=F32, stack=None):
        uniq[0] += 1
        t = (stack or st).enter_context(nc.sbuf_tensor(f"{name}_{uniq[0]}", list(shape), dt))
        return TB(t, name)


    _bc = {}

    def bcreg(e):
        if "r" not in _bc:
            r = e.alloc_register("bcreg")
            e.reg_mov(r, NSLOT - 1)
            _bc["r"] = r
        return _bc["r"]

    def rw(lst):
        return [x.b for x in lst]

    def OP(eng, fn, r=(), w=(), sem=None, ndma=1):
        return P.op(eng, fn, reads=rw(r), writes=rw(w), dma_sem=sem, ndma=ndma)

    def Vop(fn, r=(), w=()): return OP("vector", fn, r, w)
    def Aop(fn, r=(), w=()): return OP("scalar", fn, r, w)
    def Gop(fn, r=(), w=()): return OP("gpsimd", fn, r, w)
    def Top(fn, r=(), w=()): return OP("tensor", fn, r, w)

    dXt = [TB(None, f"dX{i}") for i in range(NTILE)]; dXS = TB(XS, "dXS"); dYS = TB(YS, "dYS"); dMODS = TB(MODS, "dMODS")

    PS = [E(nc.psum_tensor(f"ps{i}", [128, 1024], F32)) for i in range(4)]
    BK = [TB(PS[k // 2][:, (k % 2) * 512:(k % 2 + 1) * 512], f"bank{k}") for k in range(8)]

    def bank(k):
        return PS[k // 2][:, (k % 2) * 512:(k % 2 + 1) * 512]

    def bank_bf(k):
        return bank(k).bitcast(BF16)

    ident_f = sb("ident_f", [128, 128]); ident_b = sb("ident_b", [128, 128], BF16)
    ustrict = sb("ustrict", [128, 128], BF16); ones_b = sb("ones_b", [128, 128], BF16)
    iotac = sb("iotac", [128, 32]); negh = sb("negh", [128, 1])
    biasT = sb("biasT", [128, 8, 256])
    DEST = sb("DEST", [128, NTILE, 4], I32); GATES = sb("GATES", [128, NTILE, 4])
    cnt = sb("cnt", [128, 32])
    cld = sem("cld")

    def ld_const():
        OP("sync", lambda e: [e.dma_start(out=ident_f[:], in_=c_ident), e.dma_start(out=iotac[:], in_=c_iotac)],
           w=[ident_f, iotac], sem=cld, ndma=2)
        Vop(lambda e: e.tensor_copy(out=ident_b[:], in_=ident_f[:]), [ident_f], [ident_b])
        Vop(lambda e: e.memset(ones_b[:], 1.0), [], [ones_b])
        Vop(lambda e: e.memset(negh[:], -0.5), [], [negh])
        Vop(lambda e: e.memset(DEST[:, :, :], 1000000), [], [DEST])
    ld_const()


    def ln_tile(xin, xout, stt, mv, g_t, b_t, T):
        for c in range(2):
            Vop(lambda e, c=c: e.bn_stats(out=stt[:T, c * 6:(c + 1) * 6], in_=xin[:T, c * 512:(c + 1) * 512]), [xin], [stt])
        Vop(lambda e: e.bn_aggr(out=mv[:T, 0:2], in_=stt[:T, :]), [stt], [mv])
        Vop(lambda e: e.tensor_scalar_add(out=mv[:T, 2:3], in0=mv[:T, 1:2], scalar1=LN_EPS), [mv], [mv])
        Gop(lambda e: e.tensor_tensor(out=mv[:T, 2:3], in0=mv[:T, 2:3], in1=negh[:T, :], op=ALU.pow), [mv, negh], [mv])
        Vop(lambda e: e.scalar_tensor_tensor(out=mv[:T, 3:4], in0=mv[:T, 0:1], scalar=-1.0, in1=mv[:T, 2:3],
                                             op0=ALU.mult, op1=ALU.mult), [mv], [mv])
        Aop(lambda e: e.activation(out=xout[:T, :], in_=xin[:T, :], func=AF.Identity, bias=mv[:T, 3:4], scale=mv[:T, 2:3]),
            [xin, mv], [xout])
        Vop(lambda e: e.tensor_tensor(out=xout[:T, :], in0=xout[:T, :], in1=g_t[:T, :], op=ALU.mult), [xout, g_t], [xout])
        Vop(lambda e: e.tensor_tensor(out=xout[:T, :], in0=xout[:T, :], in1=b_t[:T, :], op=ALU.add), [xout, b_t], [xout])

    def ln_stats(xin, stt, mv, T):
        for c in range(2):
            Vop(lambda e, c=c: e.bn_stats(out=stt[:T, c * 6:(c + 1) * 6], in_=xin[:T, c * 512:(c + 1) * 512]), [xin], [stt])
        Vop(lambda e: e.bn_aggr(out=mv[:T, 0:2], in_=stt[:T, :]), [stt], [mv])
        Vop(lambda e: e.tensor_scalar_add(out=mv[:T, 2:3], in0=mv[:T, 1:2], scalar1=LN_EPS), [mv], [mv])
        Gop(lambda e: e.tensor_tensor(out=mv[:T, 2:3], in0=mv[:T, 2:3], in1=negh[:T, :], op=ALU.pow), [mv, negh], [mv])
        Vop(lambda e: e.scalar_tensor_tensor(out=mv[:T, 3:4], in0=mv[:T, 0:1], scalar=-1.0, in1=mv[:T, 2:3],
                                             op0=ALU.mult, op1=ALU.mult), [mv], [mv])

    def ln_apply(xin, xout, mv, g_t, b_t, T):
        Aop(lambda e: e.activation(out=xout[:T, :], in_=xin[:T, :], func=AF.Identity, bias=mv[:T, 3:4], scale=mv[:T, 2:3]),
            [xin, mv], [xout])
        Vop(lambda e: e.tensor_tensor(out=xout[:T, :], in0=xout[:T, :], in1=g_t[:T, :], op=ALU.mult), [xout, g_t], [xout])
        Vop(lambda e: e.tensor_tensor(out=xout[:T, :], in0=xout[:T, :], in1=b_t[:T, :], op=ALU.add), [xout, b_t], [xout])

    def cast_load(dst, dst_fn, src_fn, ncols, sem_, reads=()):
        pcs = [(c, min(c + 2048, ncols)) for c in range(0, ncols, 2048)]
        OP("gpsimd", lambda e: [e.dma_start(out=dst_fn(a, b), in_=src_fn(a, b)) for (a, b) in pcs],
           r=list(reads), w=[dst], sem=sem_, ndma=len(pcs))

    with ExitStack() as s0:
        oht = sb("oht", [32, 256 * 128], BF16, s0)
        mask_sb = sb("mask_sb", [128, 256], F32, s0)
        OP("sync", lambda e: e.dma_start(out=mask_sb[:], in_=c_mask), w=[mask_sb], sem=cld)
        ust_f = sb("ust_f", [128, 128], F32, s0)
        tab = sb("tab", [32, 8], F32, s0); tab_hi = sb("tab_hi", [32, 8], BF16, s0)
        tab_d = sb("tab_d", [32, 8], F32, s0); tab_lo = sb("tab_lo", [32, 8], BF16, s0)
        s_a = sem("setup_a"); s_b = sem("setup_b")
        cast_load(oht, lambda a, b: oht[:, a:b], lambda a, b: c_oht[:, a:b], 256 * 128, s_a)
        OP("sync", lambda e: [e.dma_start(out=tab[:], in_=rel_table), e.dma_start(out=ust_f[:], in_=c_ustrict)],
           w=[tab, ust_f], sem=s_b, ndma=2)
        Vop(lambda e: e.tensor_copy(out=ustrict[:], in_=ust_f[:]), [ust_f], [ustrict])
        Vop(lambda e: e.tensor_copy(out=tab_hi[:], in_=tab[:]), [tab], [tab_hi])
        Vop(lambda e: e.tensor_sub(out=tab_d[:], in0=tab[:], in1=tab_hi[:]), [tab, tab_hi], [tab_d])
        Vop(lambda e: e.tensor_copy(out=tab_lo[:], in_=tab_d[:]), [tab_d], [tab_lo])
        for half in range(2):
            bks = [BK[4 * half + j] for j in range(2)]

            def mm_bias(e, half=half):
                last = None
                for kl in range(128):
                    kk = half * 128 + kl
                    o = PS[2 * half][:, kl * 8:(kl + 1) * 8]
                    e.matmul(o, lhsT=oht[:, kk * 128:(kk + 1) * 128], rhs=tab_hi[:], start=True, stop=False)
                    last = e.matmul(o, lhsT=oht[:, kk * 128:(kk + 1) * 128], rhs=tab_lo[:], start=False, stop=True)
                return last
            Top(mm_bias, [oht, tab_hi, tab_lo], bks)
            for h in range(8):
                def ev(e, half=half, h=h):
                    src = PS[2 * half][:, :].rearrange("q (kk h) -> q h kk", h=8)
                    return e.tensor_tensor(out=biasT[:, h, half * 128:(half + 1) * 128], in0=src[:, h, :],
                                           in1=mask_sb[:, half * 128:(half + 1) * 128], op=ALU.add)
                Vop(ev, bks + [mask_sb], [biasT])

    P.barrier()
    with ExitStack() as s0:
        cT = sb("cT", [128, 8, 2], F32, s0); cth = sb("cth", [128, 8, 2], F32, s0)
        scT = sb("scT", [128, 8, 2], BF16, s0)
        wada = [sb(f"wada{j}", [128, 8, 2048], BF16, s0) for j in range(2)]
        wsem = [sem(f"wada{j}") for j in range(2)]
        bada = sb("bada", [2, 6 * D], F32, s0)
        modsb = sb("modsb", [2, 6 * D], F32, s0)
        s_c = sem("setup_c"); s_m = sem("setup_m")

        def ld_c(e):
            with nc.allow_non_contiguous_dma(reason="tiny transposed load of c"):
                return [e.dma_start(out=cT[:, :, g], in_=cvec[g].rearrange("(kc p) -> p kc", p=128)) for g in range(2)]
        OP("sync", ld_c, w=[cT], sem=s_c, ndma=2)
        Aop(lambda e: e.activation(out=cth[:], in_=cT[:], func=AF.Tanh, scale=0.5), [cT], [cth])
        Vop(lambda e: e.tensor_scalar(out=cth[:], in0=cth[:], scalar1=0.5, scalar2=0.5, op0=ALU.mult, op1=ALU.add), [cth], [cth])
        Vop(lambda e: e.tensor_tensor(out=scT[:], in0=cth[:], in1=cT[:], op=ALU.mult), [cth, cT], [scT])
        it = 0
        for l in range(DEPTH):
            OP("sync", lambda e, l=l: [e.dma_start(out=bada[g:g + 1, :], in_=b_ada[l:l + 1, :]) for g in range(2)],
               w=[bada], sem=s_c, ndma=2)
            for pc in range(3):
                j = it % 2
                it += 1
                OP("gpsimd", lambda e, l=l, pc=pc, j=j: e.dma_start(
                    out=wada[j][:], in_=w_ada[l][:, pc * 2048:(pc + 1) * 2048].rearrange("(kc p) f -> p kc f", p=128)),
                   w=[wada[j]], sem=wsem[j])
                for n in range(4):
                    bk = BK[n % 2]

                    def mm(e, j=j, n=n):
                        last = None
                        for kc in range(8):
                            last = e.matmul(bank(n % 2)[0:2, :], lhsT=scT[:, kc, :], rhs=wada[j][:, kc, n * 512:(n + 1) * 512],
                                            start=(kc == 0), stop=(kc == 7))
                        return last
                    Top(mm, [scT, wada[j]], [bk])
                    c0 = pc * 2048 + n * 512
                    Vop(lambda e, n=n, c0=c0: e.tensor_tensor(out=modsb[:, c0:c0 + 512], in0=bank(n % 2)[0:2, :],
                                                              in1=bada[:, c0:c0 + 512], op=ALU.add), [bk, bada], [modsb])
            Vop(lambda e: e.tensor_scalar_add(out=modsb[:, D:2 * D], in0=modsb[:, D:2 * D], scalar1=1.0), [modsb], [modsb])
            Vop(lambda e: e.tensor_scalar(out=modsb[:, 2 * D:3 * D], in0=modsb[:, 2 * D:3 * D], scalar1=1.0, scalar2=0.5,
                                          op0=ALU.add, op1=ALU.mult), [modsb], [modsb])
            Vop(lambda e: e.tensor_scalar_add(out=modsb[:, 4 * D:6 * D], in0=modsb[:, 4 * D:6 * D], scalar1=1.0), [modsb], [modsb])
            OP("sync", lambda e, l=l: e.dma_start(out=MODS[l], in_=modsb[:]), r=[modsb], w=[dMODS], sem=s_m)

    P.barrier()
    with ExitStack() as s0:
        s_c = sem("setup_c2")
        g0 = sb("g0", [128, D], F32, s0); b0 = sb("b0", [128, D], F32, s0)
        OP("sync", lambda e: [e.dma_start(out=g0[:], in_=ln0_g.partition_broadcast(128)),
                              e.dma_start(out=b0[:], in_=ln0_b.partition_broadcast(128))], w=[g0, b0], sem=s_c, ndma=2)
        xi = [sb(f"xi{j}", [128, D], F32, s0) for j in range(3)]
        xo = [sb(f"xo{j}", [128, D], F32, s0) for j in range(2)]
        xis = [sem(f"xi{j}") for j in range(3)]; xos = [sem(f"xo{j}") for j in range(2)]
        stt = [sb(f"stt{j}", [128, 12], F32, s0) for j in range(2)]
        mv = [sb(f"mv{j}", [128, 4], F32, s0) for j in range(2)]

        def tl(i):
            return 128 if i < NT else DEC

        def ld0(i):
            T = tl(i)
            src = xp[i * 128:(i + 1) * 128, :] if i < NT else xsm
            OP("sync", lambda e: e.dma_start(out=xi[i % 3][:T], in_=src), w=[xi[i % 3]], sem=xis[i % 3])
        ld0(0)
        if NTILE > 1:
            ld0(1)
        ln_stats(xi[0], stt[0], mv[0], tl(0))
        for i in range(NTILE):
            T = tl(i)
            if i + 2 < NTILE:
                ld0(i + 2)
            if i + 1 < NTILE:
                ln_stats(xi[(i + 1) % 3], stt[(i + 1) % 2], mv[(i + 1) % 2], tl(i + 1))
            ln_apply(xi[i % 3], xo[i % 2], mv[i % 2], g0, b0, T)
            OP("sync", lambda e, T=T, i=i: e.dma_start(out=X[i * 128:i * 128 + T, :], in_=xo[i % 2][:T]),
               r=[xo[i % 2]], w=[dXt[i]], sem=xos[i % 2])
    P.barrier()


    dsem = sem("dbg")
    if dbg:
        OP("sync", lambda e: [e.dma_start(out=dbgX0, in_=X), e.dma_start(out=dbgM, in_=MODS), e.dma_start(out=dbgB, in_=biasT[:].rearrange("p h k -> p (h k)"))], sem=dsem, ndma=3)
        P.barrier()
    SEGS = [(0, 512), (512, 768), (768, 1280), (1280, 1792), (1792, 2304),
            (2304, 2816), (2816, 3328), (3328, 3840), (3840, 4352)]

    def shift_dma(e, dst, src_fn, sh, T):
        n = T - sh
        a = (n // 16) * 16
        if a == n:
            a -= 16
        return [e.dma_start(out=dst[sh:sh + a, :], in_=src_fn(0, a)),
                e.dma_start(out=dst[sh + a:T, :], in_=src_fn(a, n))]

    def phase_mix(l):
        with ExitStack() as sp:
            win = sb("win", [128, 8, INW], BF16, sp)
            woa = sb("woa", [128, 4, D], BF16, sp); wob = sb("wob", [128, 4, D], BF16, sp)
            wo = sb("wo", [128, 8, D], BF16, sp); wr = sb("wr", [128, 8, NEXP], BF16, sp)
            bin33 = sb("bin33", [33, INW], BF16, sp); br33 = sb("br33", [33, NEXP], BF16, sp)
            ones33 = sb("ones33", [33, 128], BF16, sp)
            wsm = sem(f"mw{l}"); csm = sem(f"mc{l}")
            cast_load(win, lambda a, b: win[:, :, a:b], lambda a, b: w_in[l][:, a:b].rearrange("(kc p) f -> p kc f", p=128), INW, wsm)
            cast_load(woa, lambda a, b: woa[:, :, a:b], lambda a, b: w_oa[l][:, a:b].rearrange("(kc p) f -> p kc f", p=128), D, wsm)
            cast_load(wob, lambda a, b: wob[:, :, a:b], lambda a, b: w_ob[l][:, a:b].rearrange("(kc p) f -> p kc f", p=128), D, wsm)
            cast_load(wo, lambda a, b: wo[:, :, a:b], lambda a, b: w_o[l][:, a:b].rearrange("(kc p) f -> p kc f", p=128), D, wsm)
            cast_load(wr, lambda a, b: wr[:, :, a:b], lambda a, b: w_router[l][:, a:b].rearrange("(kc p) f -> p kc f", p=128), NEXP, wsm)
            Vop(lambda e: e.memset(ones33[:], 1.0), [], [ones33])
            with ExitStack() as sq:
                bst = sb("bst", [33, INW + NEXP], F32, sq)
                b33 = sb("b33", [33, INW + NEXP], BF16, sq)
                Vop(lambda e: e.memset(bst[:], 0.0), [], [bst])
                OP("sync", lambda e: [e.dma_start(out=bst[r:r + 1, 0:INW], in_=b_in[l:l + 1, :]) for r in (0, 32)]
                   + [e.dma_start(out=bst[r:r + 1, INW:INW + NEXP], in_=b_router[l:l + 1, :]) for r in (0, 32)],
                   r=[], w=[bst], sem=csm, ndma=4)
                Vop(lambda e: e.tensor_copy(out=b33[:], in_=bst[:]), [bst], [b33])
                Vop(lambda e: e.tensor_sub(out=bst[:], in0=bst[:], in1=b33[:]), [bst, b33], [bst])
                Vop(lambda e: e.tensor_copy(out=b33[32:33, :], in_=bst[32:33, :]), [bst], [b33])
                Vop(lambda e: e.tensor_copy(out=bin33[:], in_=b33[:, 0:INW]), [b33], [bin33])
                Vop(lambda e: e.tensor_copy(out=br33[:], in_=b33[:, INW:INW + NEXP]), [b33], [br33])
            P.barrier()
            SC2 = sb("SC2", [128, D], F32, sp); SH2 = sb("SH2", [128, D], F32, sp); G1 = sb("G1", [128, D], F32, sp)
            l1g = sb("l1g", [128, D], F32, sp); l1b = sb("l1b", [128, D], F32, sp)
            cw = sb("cw", [128, 1536], F32, sp); sinkb = sb("sinkb", [128, 8], F32, sp)
            SC1c = sb("SC1c", [128, 8], F32, sp); SH1c = sb("SH1c", [128, 8], F32, sp)
            OP("sync", lambda e: [e.dma_start(out=l1g[:], in_=ln1_g[l:l + 1, :].partition_broadcast(128)),
                                  e.dma_start(out=l1b[:], in_=ln1_b[l:l + 1, :].partition_broadcast(128)),
                                  e.dma_start(out=cw[:], in_=conv_w[l:l + 1, :].partition_broadcast(128)),
                                  e.dma_start(out=sinkb[:], in_=sinks[l:l + 1, :].partition_broadcast(128))],
               w=[l1g, l1b, cw, sinkb], sem=csm, ndma=4)

            def load_mods(g):
                def f(e):
                    r = [e.dma_start(out=SC2[:], in_=MODS[l, g:g + 1, 4 * D:5 * D].partition_broadcast(128)),
                         e.dma_start(out=SH2[:], in_=MODS[l, g:g + 1, 3 * D:4 * D].partition_broadcast(128)),
                         e.dma_start(out=G1[:], in_=MODS[l, g:g + 1, 2 * D:3 * D].partition_broadcast(128))]
                    with nc.allow_non_contiguous_dma(reason="tiny per-partition mod vectors"):
                        r.append(e.dma_start(out=SC1c[:], in_=MODS[l, g, D:2 * D].rearrange("(kc p) -> p kc", p=128)))
                        r.append(e.dma_start(out=SH1c[:], in_=MODS[l, g, 0:D].rearrange("(kc p) -> p kc", p=128)))
                    return r
                OP("sync", f, r=[dMODS], w=[SC2, SH2, G1, SC1c, SH1c], sem=csm, ndma=5)
            load_mods(0)

            SC1s = sb("SC1s", [128, 8], F32, sp); SH1s = sb("SH1s", [128, 8], F32, sp)

            def ld_s(e):
                with nc.allow_non_contiguous_dma(reason="tiny per-partition mod vectors"):
                    return [e.dma_start(out=SC1s[:], in_=MODS[l, 1, D:2 * D].rearrange("(kc p) -> p kc", p=128)),
                            e.dma_start(out=SH1s[:], in_=MODS[l, 1, 0:D].rearrange("(kc p) -> p kc", p=128))]
            OP("sync", ld_s, r=[dMODS], w=[SC1s, SH1s], sem=csm, ndma=2)
            xin2 = [sb(f"xin{j}", [128, D], F32, sp) for j in range(2)]
            xsem2 = [sem(f"mx{l}{j}") for j in range(2)]; xosem = sem(f"mxo{l}")
            hT2 = [sb(f"hT{j}", [128, 8, 128], BF16, sp) for j in range(2)]
            q_bf = sb("q_bf", [128, 512], BF16, sp); kv_f = sb("kv_f", [128, 256], F32, sp)
            k_bf = sb("k_bf", [128, 128], BF16, sp)
            kT2 = sb("kT2", [128, 3, 128], BF16, sp); v2 = sb("v2", [128, 3, 128], BF16, sp)
            qT = sb("qT", [128, 4, 128], BF16, sp)
            S_sb = sb("S_sb", [128, 4, 256], F32, sp); Eb = sb("Eb", [128, 4, 256], BF16, sp)
            PT = sb("PT", [128, 4, 2, 128], BF16, sp)
            sm = sb("sm", [128, 16], F32, sp)
            rden = sb("rden", [128, 8], F32, sp)
            ya_bf = sb("ya_bf", [128, 512], BF16, sp); yaT = sb("yaT", [128, 4, 128], BF16, sp)
            cb = sb("cb", [128, 512], F32, sp); cc = sb("cc", [128, 512], F32, sp)
            u2 = sb("u2", [128, 3, 512], F32, sp); ush1 = sb("ush1", [128, 512], F32, sp); ush2 = sb("ush2", [128, 512], F32, sp)
            ct = sb("ct", [128, 512], F32, sp)
            yb_bf = sb("yb_bf", [128, 512], BF16, sp); ybT = sb("ybT", [128, 4, 128], BF16, sp)
            ta = sb("ta", [128, 512], F32, sp); t1 = sb("t1", [128, D], F32, sp)
            pre_bf = sb("pre_bf", [128, D], BF16, sp); preT = sb("preT", [128, 8, 128], BF16, sp)
            h2b = pre_bf; h2T = preT
            xo = sb("xo_m", [128, D], F32, sp)
            stt = sb("stt_m", [128, 12], F32, sp); mv = sb("mv_m", [128, 4], F32, sp)
            lg = sb("lg", [128, 32], F32, sp); top8 = sb("top8", [128, 8], F32, sp)
            rt = sb("rt", [128, 16], F32, sp); Mbf = sb("Mbf", [128, 32], BF16, sp)
            pos = sb("pos", [128, 32], F32, sp); oh = sb("oh", [128, 32], F32, sp); destf = sb("destf", [128, 4], F32, sp)
            ck_bf = sb("ck_bf", [128, 128], BF16, sp)
            ckf = kv_f
            shs = sem(f"msh{l}"); scs = [sem(f"msc{l}{k}") for k in range(4)]; kos = sem(f"mko{l}"); cks = sem(f"mck{l}")
            Vop(lambda e: e.memset(cnt[:], 0.0), [], [cnt])
            Vop(lambda e: e.memset(pre_bf[:], 0.0), [], [pre_bf])
            zsem = sem(f"mz{l}")
            NZ = 8
            rows = NSLOT // NZ
            OP("sync", lambda e: [e.dma_start(out=XS[z * rows:(z + 1) * rows, :].rearrange("(p r) d -> p r d", p=128),
                                              in_=pre_bf[:, None, :].to_broadcast([128, rows // 128, D])) for z in range(NZ)],
               r=[pre_bf], w=[dXS], sem=zsem, ndma=NZ)
            if l == 0:
                print("phase M sbuf remaining", nc.sbuf_bytes_remaining)

            def tinfo(i):
                T = 128 if i < NT else DEC
                return T, (i == NT), i % 3, (i - 1) % 3, i * 128

            ya2 = [ya_bf, sb("ya_bf1", [128, 512], BF16, sp)]
            yb2 = [yb_bf, sb("yb_bf1", [128, 512], BF16, sp)]
            ct2 = cc

            def zseg(si, bk, T, hT):
                c0, c1 = SEGS[si]

                def f(e):
                    for kc in range(8):
                        e.matmul(bank(bk)[:T, 0:c1 - c0], lhsT=hT[:, kc, :T], rhs=win[:, kc, c0:c1], start=(kc == 0), stop=False)
                    return e.matmul(bank(bk)[:T, 0:c1 - c0], lhsT=ones33[:, :T], rhs=bin33[:, c0:c1], start=False, stop=True)
                Top(f, [hT, win, ones33, bin33], [BK[bk]])

            def load_x(i):
                T, samp, cur, prv, r0 = tinfo(i)
                xin = xin2[i % 2]
                OP("sync", lambda e: e.dma_start(out=xin[:T], in_=X[r0:r0 + T, :]), r=[dXt[i]], w=[xin], sem=xsem2[i % 2])

            def steps1(i):
                T, samp, cur, prv, r0 = tinfo(i)
                xin = xin2[i % 2]; hT = hT2[i % 2]
                ya_o = ya2[i % 2]; yb_o = yb2[i % 2]
                sc_, sh_ = (SC1s, SH1s) if samp else (SC1c, SH1c)
                k0 = 128 if i == 0 else 0
                k1 = 128 + T
                has_prev = (i > 0)
                if samp:
                    prv = (i + 1) % 3
                S = []

                def a0():
                    def xT_f(e):
                        last = None
                        for kc in range(8):
                            last = e.transpose(PS[0][:, kc * 128:kc * 128 + T], xin[:T, kc * 128:(kc + 1) * 128], ident_f[:T, :T])
                        return last
                    Top(xT_f, [xin, ident_f], [BK[0], BK[1]])
                    for kc in range(8):
                        Aop(lambda e, kc=kc: e.activation(out=hT[:, kc, :T], in_=PS[0][:, kc * 128:kc * 128 + T], func=AF.Identity,
                                                          bias=sh_[:, kc:kc + 1], scale=sc_[:, kc:kc + 1]),
                            [BK[0], BK[1], sc_, sh_], [hT])
                S.append(a0)

                def a1():
                    zseg(0, 0, T, hT)
                    Vop(lambda e: e.tensor_copy(out=q_bf[:T, :].rearrange("t (g kv d) -> t kv g d", g=4, kv=2, d=64),
                                                in_=bank(0)[:T, :].rearrange("t (kv g d) -> t kv g d", kv=2, g=4, d=64)),
                        [BK[0]], [q_bf])
                S.append(a1)

                def a2():
                    zseg(1, 1, T, hT)
                    Vop(lambda e: e.tensor_copy(out=kv_f[:T, :], in_=bank(1)[:T, 0:256]), [BK[1]], [kv_f])
                    Aop(lambda e: e.copy(out=k_bf[:T, :], in_=kv_f[:T, 0:128]), [kv_f], [k_bf])
                    Aop(lambda e: e.copy(out=v2[:T, cur, :], in_=kv_f[:T, 128:256]), [kv_f], [v2])
                    if i == NT - 1:
                        OP("sync", lambda e: [e.dma_start(out=nk_p[l], in_=kv_f[:, 0:128]), e.dma_start(out=nv_p[l], in_=kv_f[:, 128:256])],
                           r=[kv_f], sem=kos, ndma=2)
                    if samp:
                        OP("sync", lambda e: [e.dma_start(out=nk_s[l, 64:128, :], in_=kv_f[:64, 0:128]),
                                              e.dma_start(out=nv_s[l, 64:128, :], in_=kv_f[:64, 128:256])],
                           r=[kv_f], sem=kos, ndma=2)
                S.append(a2)

                def a3():
                    zseg(2, 0, T, hT)
                    Vop(lambda e: e.tensor_copy(out=cb[:T, :], in_=bank(0)[:T, :]), [BK[0]], [cb])
                S.append(a3)

                def a4():
                    zseg(3, 1, T, hT)
                    Aop(lambda e: e.copy(out=cc[:T, :], in_=bank(1)[:T, :]), [BK[1]], [cc])
                S.append(a4)

                def a5():
                    zseg(4, 0, T, hT)
                    Vop(lambda e: e.tensor_tensor(out=u2[:T, cur, :], in0=bank(0)[:T, :], in1=cc[:T, :], op=ALU.mult), [BK[0], cc], [u2])
                    if i == 0:
                        Vop(lambda e: e.memset(ush1[0:1, :], 0.0), [], [ush1])
                        Vop(lambda e: e.memset(ush2[0:2, :], 0.0), [], [ush2])
                        hist_f = None
                    elif samp:
                        hist_f = lambda e: [e.dma_start(out=ush1[0:1, :], in_=sconv[l, 1:2, :]), e.dma_start(out=ush2[0:2, :], in_=sconv[l, 0:2, :])]
                    else:
                        pu = (i - 1) % 3
                        hist_f = lambda e: [e.dma_start(out=ush1[0:1, :], in_=u2[127:128, pu, :]),
                                            e.dma_start(out=ush2[0:2, :], in_=u2[126:128, pu, :])]

                    def shf(e):
                        r = shift_dma(e, ush1, lambda a, b: u2[a:b, cur, :], 1, T) + shift_dma(e, ush2, lambda a, b: u2[a:b, cur, :], 2, T)
                        if hist_f is not None:
                            r += hist_f(e)
                        return r
                    OP("sync", shf, r=[u2], w=[ush1, ush2], sem=shs, ndma=4 + (0 if hist_f is None else 2))
                    if i == NT - 1:
                        OP("sync", lambda e: e.dma_start(out=nu_p[l], in_=u2[126:128, cur, :]), r=[u2], sem=kos)
                    if samp:
                        OP("sync", lambda e: e.dma_start(out=nu_s[l], in_=u2[62:64, cur, :]), r=[u2], sem=kos)
                S.append(a5)

                def b0():
                    def qkT(e):
                        for g in range(4):
                            e.transpose(bank_bf(4)[:, g * 128:g * 128 + T], q_bf[:T, g * 128:(g + 1) * 128], ident_b[:T, :T])
                        return e.transpose(bank_bf(4)[:, 512:512 + T], k_bf[:T, :], ident_b[:T, :T])
                    Top(qkT, [q_bf, k_bf, ident_b], [BK[4]])
                    Vop(lambda e: e.tensor_copy(out=qT[:, :, :T], in_=bank_bf(4)[:, 0:512].rearrange("p (g t) -> p g t", g=4)[:, :, :T]),
                        [BK[4]], [qT])
                    Aop(lambda e: e.copy(out=kT2[:, cur, :T], in_=bank_bf(4)[:, 512:512 + T]), [BK[4]], [kT2])
                    if samp:
                        ck32 = lambda: S_sb[:, 0, :]
                        OP("sync", lambda e: [e.dma_start(out=ck32()[:, 0:128], in_=ck[l]), e.dma_start(out=S_sb[:, 1, 0:128], in_=cv[l])],
                           w=[S_sb], sem=cks, ndma=2)
                        Vop(lambda e: e.tensor_copy(out=ck_bf[:], in_=S_sb[:, 0, 0:128]), [S_sb], [ck_bf])
                        Vop(lambda e: e.tensor_copy(out=v2[:, prv, :], in_=S_sb[:, 1, 0:128]), [S_sb], [v2])
                        Top(lambda e: e.transpose(bank_bf(4)[:, 0:128], ck_bf[:], ident_b[:]), [ck_bf, ident_b], [BK[4]])
                        Vop(lambda e: e.tensor_copy(out=kT2[:, prv, :], in_=bank_bf(4)[:, 0:128]), [BK[4]], [kT2])
                        OP("sync", lambda e: [e.dma_start(out=nk_s[l, 0:64, :], in_=S_sb[64:128, 0, 0:128]),
                                              e.dma_start(out=nv_s[l, 0:64, :], in_=S_sb[64:128, 1, 0:128])],
                           r=[S_sb], sem=kos, ndma=2)
                S.append(b0)

                for kvh in range(2):
                    pr = slice(kvh * 64, (kvh + 1) * 64)

                    def b_s(kvh=kvh, pr=pr):
                        def smm(e):
                            last = None
                            for g in range(4):
                                if has_prev:
                                    e.matmul(PS[1][:T, g * 256:g * 256 + 128], lhsT=qT[pr, g, :T], rhs=kT2[pr, prv, :], start=True, stop=True)
                                last = e.matmul(PS[1][:T, g * 256 + 128:g * 256 + 128 + T], lhsT=qT[pr, g, :T], rhs=kT2[pr, cur, :T],
                                                start=True, stop=True)
                            return last
                        Top(smm, [qT, kT2], [BK[2], BK[3]])
                        S3 = lambda: PS[1][:, :].rearrange("q (g k) -> q g k", g=4)
                        Vop(lambda e: e.scalar_tensor_tensor(
                            out=S_sb[:T, :, k0:k1], in0=S3()[:T, :, k0:k1], scalar=0.125, in1=biasT[:T, kvh * 4:kvh * 4 + 4, k0:k1],
                            op0=ALU.mult, op1=ALU.add), [BK[2], BK[3], biasT], [S_sb])
                    S.append(b_s)

                    def b_m(kvh=kvh):
                        Vop(lambda e: e.reduce_max(out=sm[:T, 0:4], in_=S_sb[:T, :, k0:k1], axis=AX.X), [S_sb], [sm])
                        Vop(lambda e: e.tensor_tensor(out=sm[:T, 0:4], in0=sm[:T, 0:4], in1=sinkb[:T, kvh * 4:kvh * 4 + 4], op=ALU.max),
                            [sm, sinkb], [sm])
                        Vop(lambda e: e.tensor_scalar_mul(out=sm[:T, 4:8], in0=sm[:T, 0:4], scalar1=-1.0), [sm], [sm])
                        for g in range(4):
                            Aop(lambda e, g=g: e.activation(out=Eb[:T, g, k0:k1], in_=S_sb[:T, g, k0:k1], func=AF.Exp,
                                                            bias=sm[:T, 4 + g:5 + g], scale=1.0, accum_out=sm[:T, 8 + g:9 + g]),
                                [S_sb, sm], [Eb, sm])
                    S.append(b_m)

                    def b_d(kvh=kvh):
                        Vop(lambda e: e.tensor_tensor(out=sm[:T, 12:16], in0=sinkb[:T, kvh * 4:kvh * 4 + 4], in1=sm[:T, 4:8], op=ALU.add),
                            [sm, sinkb], [sm])
                        Aop(lambda e: e.activation(out=sm[:T, 12:16], in_=sm[:T, 12:16], func=AF.Exp), [sm], [sm])
                        Vop(lambda e: e.tensor_tensor(out=sm[:T, 12:16], in0=sm[:T, 12:16], in1=sm[:T, 8:12], op=ALU.add), [sm], [sm])
                        Vop(lambda e: e.reciprocal(out=rden[:T, kvh * 4:kvh * 4 + 4], in_=sm[:T, 12:16]), [sm], [rden])
                    S.append(b_d)

                    def b_p(kvh=kvh):
                        def ptT(e):
                            last = None
                            for g in range(4):
                                if has_prev:
                                    e.transpose(bank_bf(4)[:, (g * 2) * 128:(g * 2) * 128 + T], Eb[:T, g, 0:128], ident_b[:T, :T])
                                last = e.transpose(bank_bf(4)[:T, (g * 2 + 1) * 128:(g * 2 + 1) * 128 + T], Eb[:T, g, 128:128 + T], ident_b[:T, :T])
                            return last
                        Top(ptT, [Eb, ident_b], [BK[4]])
                        if has_prev:
                            Aop(lambda e: e.copy(out=PT[:, :, 0, :T],
                                                 in_=bank_bf(4)[:, :].rearrange("p (g b t) -> p g b t", g=4, b=2)[:, :, 0, :T]), [BK[4]], [PT])
                        Vop(lambda e: e.tensor_copy(out=PT[:T, :, 1, :T],
                                                    in_=bank_bf(4)[:, :].rearrange("p (g b t) -> p g b t", g=4, b=2)[:T, :, 1, :T]), [BK[4]], [PT])
                    S.append(b_p)

                    def b_o(kvh=kvh):
                        def omm(e):
                            last = None
                            for g in range(4):
                                h = kvh * 4 + g
                                o = bank(5)[:T, h * 64:(h + 1) * 64]
                                if has_prev:
                                    e.matmul(o, lhsT=PT[:, g, 0, :T], rhs=v2[:, prv, kvh * 64:(kvh + 1) * 64], start=True, stop=False)
                                last = e.matmul(o, lhsT=PT[:T, g, 1, :T], rhs=v2[:T, cur, kvh * 64:(kvh + 1) * 64], start=(not has_prev), stop=True)
                            return last
                        Top(omm, [PT, v2], [BK[5]])
                    S.append(b_o)
                    if kvh == 0:
                        def b_c1():
                            Vop(lambda e: e.tensor_tensor(out=ct[:T, :], in0=u2[:T, cur, :], in1=cw[:T, 1024:1536], op=ALU.mult), [u2, cw], [ct])
                            Vop(lambda e: e.tensor_tensor(out=ct2[:T, :], in0=ush1[:T, :], in1=cw[:T, 512:1024], op=ALU.mult), [ush1, cw], [ct2])
                            Vop(lambda e: e.tensor_tensor(out=ct[:T, :], in0=ct[:T, :], in1=ct2[:T, :], op=ALU.add), [ct, ct2], [ct])
                        S.append(b_c1)

                        def b_c2():
                            Vop(lambda e: e.tensor_tensor(out=ct2[:T, :], in0=ush2[:T, :], in1=cw[:T, 0:512], op=ALU.mult), [ush2, cw], [ct2])
                            Vop(lambda e: e.tensor_tensor(out=ct[:T, :], in0=ct[:T, :], in1=ct2[:T, :], op=ALU.add), [ct, ct2], [ct])
                            Vop(lambda e: e.tensor_tensor(out=yb_o[:T, :], in0=ct[:T, :], in1=cb[:T, :], op=ALU.mult), [ct, cb], [yb_o])
                        S.append(b_c2)

                def b_n():
                    Vop(lambda e: e.tensor_tensor(out=ya_o[:T, :].rearrange("t (h d) -> t h d", h=8),
                                                  in0=bank(5)[:T, :].rearrange("t (h d) -> t h d", h=8),
                                                  in1=rden[:T, :].unsqueeze(2).to_broadcast([T, 8, 64]), op=ALU.mult), [BK[5], rden], [ya_o])
                S.append(b_n)
                return S

            def steps2(i):
                T, samp, cur, prv, r0 = tinfo(i)
                xin = xin2[i % 2]; hT = hT2[i % 2]
                ya_i = ya2[i % 2]; yb_i = yb2[i % 2]
                S = []

                def tr(src, nch):
                    def f(e):
                        last = None
                        for c in range(nch):
                            last = e.transpose(bank_bf(6)[:, c * 128:c * 128 + T], src[:T, c * 128:(c + 1) * 128], ident_b[:T, :T])
                        return last
                    return f

                def proj(srcT, w, nkc):
                    def f(e):
                        last = None
                        for n in range(2):
                            for c in range(nkc):
                                last = e.matmul(PS[3][:T, n * 512:(n + 1) * 512], lhsT=srcT[:, c, :T], rhs=w[:, c, n * 512:(n + 1) * 512],
                                                start=(c == 0), stop=(c == nkc - 1))
                        return last
                    return f

                def c0():
                    if samp:
                        load_mods(1)
                    Top(tr(ya_i, 4), [ya_i, ident_b], [BK[6]])
                    Aop(lambda e: e.copy(out=yaT[:, :, :T], in_=bank_bf(6)[:, 0:512].rearrange("p (c t) -> p c t", c=4)[:, :, :T]), [BK[6]], [yaT])
                S.append(c0)

                def c1():
                    Top(tr(yb_i, 4), [yb_i, ident_b], [BK[6]])
                    Vop(lambda e: e.tensor_copy(out=ybT[:, :, :T], in_=bank_bf(6)[:, 0:512].rearrange("p (c t) -> p c t", c=4)[:, :, :T]), [BK[6]], [ybT])
                S.append(c1)
                S.append(lambda: Top(proj(yaT, woa, 4), [yaT, woa], [BK[6], BK[7]]))
                for n in range(2):
                    def c2(n=n):
                        zseg(5 + n, n, T, hT)
                        Aop(lambda e: e.activation(out=ta[:T, :], in_=bank(n)[:T, :], func=AF.Tanh, scale=0.5), [BK[n]], [ta])
                        Vop(lambda e: e.scalar_tensor_tensor(out=t1[:T, n * 512:(n + 1) * 512], in0=ta[:T, :], scalar=1.0,
                                                             in1=PS[3][:T, n * 512:(n + 1) * 512], op0=ALU.add, op1=ALU.mult),
                            [ta, BK[6 + n]], [t1])
                    S.append(c2)
                S.append(lambda: Top(proj(ybT, wob, 4), [ybT, wob], [BK[6], BK[7]]))
                for n in range(2):
                    def c3(n=n):
                        zseg(7 + n, n, T, hT)
                        Aop(lambda e: e.activation(out=ta[:T, :], in_=bank(n)[:T, :], func=AF.Tanh, scale=0.5), [BK[n]], [ta])
                        Vop(lambda e: e.scalar_tensor_tensor(out=ta[:T, :], in0=ta[:T, :], scalar=1.0,
                                                             in1=PS[3][:T, n * 512:(n + 1) * 512], op0=ALU.add, op1=ALU.mult),
                            [ta, BK[6 + n]], [ta])
                        Vop(lambda e: e.tensor_tensor(out=pre_bf[:T, n * 512:(n + 1) * 512], in0=ta[:T, :],
                                                      in1=t1[:T, n * 512:(n + 1) * 512], op=ALU.add), [ta, t1], [pre_bf])
                    S.append(c3)

                def c4():
                    Top(tr(pre_bf, 8), [pre_bf, ident_b], [BK[6]])
                    Aop(lambda e: e.copy(out=preT[:, :, :T], in_=bank_bf(6)[:, :].rearrange("p (c t) -> p c t", c=8)[:, :, :T]), [BK[6]], [preT])
                S.append(c4)

                def c5():
                    Top(proj(preT, wo, 8), [preT, wo], [BK[6], BK[7]])
                    Vop(lambda e: e.tensor_tensor(out=t1[:T, :], in0=PS[3][:T, :], in1=G1[:T, :], op=ALU.mult), [BK[6], BK[7], G1], [t1])
                S.append(c5)
                S.append(lambda: Vop(lambda e: e.scalar_tensor_tensor(out=xin[:T, :], in0=xin[:T, :], scalar=ALPHA, in1=t1[:T, :],
                                                                      op0=ALU.mult, op1=ALU.add), [xin, t1], [xin]))
                S.append(lambda: [Vop(lambda e, c=c: e.bn_stats(out=stt[:T, c * 6:(c + 1) * 6], in_=xin[:T, c * 512:(c + 1) * 512]), [xin], [stt])
                                  for c in range(2)])

                def c6():
                    Vop(lambda e: e.bn_aggr(out=mv[:T, 0:2], in_=stt[:T, :]), [stt], [mv])
                    Vop(lambda e: e.tensor_scalar_add(out=mv[:T, 2:3], in0=mv[:T, 1:2], scalar1=LN_EPS), [mv], [mv])
                    Gop(lambda e: e.tensor_tensor(out=mv[:T, 2:3], in0=mv[:T, 2:3], in1=negh[:T, :], op=ALU.pow), [mv, negh], [mv])
                    Vop(lambda e: e.scalar_tensor_tensor(out=mv[:T, 3:4], in0=mv[:T, 0:1], scalar=-1.0, in1=mv[:T, 2:3],
                                                         op0=ALU.mult, op1=ALU.mult), [mv], [mv])
                    Aop(lambda e: e.activation(out=xo[:T, :], in_=xin[:T, :], func=AF.Identity, bias=mv[:T, 3:4], scale=mv[:T, 2:3]),
                        [xin, mv], [xo])
                S.append(c6)
                S.append(lambda: Vop(lambda e: e.tensor_tensor(out=xo[:T, :], in0=xo[:T, :], in1=l1g[:T, :], op=ALU.mult), [xo, l1g], [xo]))

                def c7():
                    Vop(lambda e: e.tensor_tensor(out=xo[:T, :], in0=xo[:T, :], in1=l1b[:T, :], op=ALU.add), [xo, l1b], [xo])
                    OP("sync", lambda e: e.dma_start(out=X[r0:r0 + T, :], in_=xo[:T]), r=[xo], w=[dXt[i]], sem=xosem)
                S.append(c7)
                S.append(lambda: Vop(lambda e: e.tensor_tensor(out=t1[:T, :], in0=xo[:T, :], in1=SC2[:T, :], op=ALU.mult), [xo, SC2], [t1]))
                S.append(lambda: Vop(lambda e: e.tensor_tensor(out=h2b[:T, :], in0=t1[:T, :], in1=SH2[:T, :], op=ALU.add), [t1, SH2], [h2b]))

                def c8():
                    Top(tr(h2b, 8), [h2b, ident_b], [BK[6]])
                    Vop(lambda e: e.tensor_copy(out=h2T[:, :, :T], in_=bank_bf(6)[:, :].rearrange("p (c t) -> p c t", c=8)[:, :, :T]), [BK[6]], [h2T])
                S.append(c8)

                def c9():
                    def rmm(e):
                        for kc in range(8):
                            e.matmul(bank(6)[:T, 0:32], lhsT=h2T[:, kc, :T], rhs=wr[:, kc, :], start=(kc == 0), stop=False)
                        return e.matmul(bank(6)[:T, 0:32], lhsT=ones33[:, :T], rhs=br33[:, :], start=False, stop=True)
                    Top(rmm, [h2T, wr, ones33, br33], [BK[6]])
                    Vop(lambda e: e.tensor_copy(out=lg[:T, :], in_=bank(6)[:T, 0:32]), [BK[6]], [lg])
                    Vop(lambda e: e.max(out=top8[:T, :], in_=lg[:T, :]), [lg], [top8])
                    Vop(lambda e: e.tensor_scalar_mul(out=rt[:T, 0:1], in0=top8[:T, 0:1], scalar1=-1.0), [top8], [rt])
                    Aop(lambda e: e.activation(out=rt[:T, 4:8], in_=top8[:T, 0:4], func=AF.Exp, bias=rt[:T, 0:1], scale=1.0,
                                               accum_out=rt[:T, 1:2]), [top8, rt], [rt])
                    Vop(lambda e: e.tensor_scalar(out=Mbf[:T, :], in0=lg[:T, :], scalar1=top8[:T, 3:4], scalar2=None, op0=ALU.is_ge),
                        [lg, top8], [Mbf])
                S.append(c9)

                def c10():
                    def pmm(e):
                        e.matmul(bank(6)[:T, 64:96], lhsT=ustrict[:T, :T], rhs=Mbf[:T, :], start=True, stop=True)
                        return e.matmul(bank(6)[:, 128:160], lhsT=ones_b[:T, :], rhs=Mbf[:T, :], start=True, stop=True)
                    Top(pmm, [ustrict, ones_b, Mbf], [BK[6]])
                    Vop(lambda e: e.reciprocal(out=rt[:T, 2:3], in_=rt[:T, 1:2]), [rt], [rt])
                    Vop(lambda e: e.tensor_tensor(out=pos[:T, :], in0=bank(6)[:T, 64:96], in1=cnt[:T, :], op=ALU.add), [BK[6], cnt], [pos])
                    Vop(lambda e: e.tensor_tensor(out=cnt[:, :], in0=bank(6)[:, 128:160], in1=cnt[:, :], op=ALU.add), [BK[6], cnt], [cnt])
                S.append(c10)

                def c11():
                    Vop(lambda e: e.tensor_scalar(out=oh[:T, :], in0=pos[:T, :], scalar1=float(C), scalar2=BIG, op0=ALU.is_ge, op1=ALU.mult),
                        [pos], [oh])
                    Vop(lambda e: e.tensor_tensor(out=pos[:T, :], in0=pos[:T, :], in1=oh[:T, :], op=ALU.add), [pos, oh], [pos])
                    Vop(lambda e: e.tensor_tensor(out=pos[:T, :], in0=pos[:T, :], in1=iotac[:T, :], op=ALU.add), [pos, iotac], [pos])
                S.append(c11)
                for k in range(4):
                    def c12(k=k):
                        Vop(lambda e: e.tensor_scalar(out=oh[:T, :], in0=lg[:T, :], scalar1=top8[:T, k:k + 1], scalar2=None, op0=ALU.is_equal),
                            [lg, top8], [oh])
                        Vop(lambda e: e.tensor_tensor(out=oh[:T, :], in0=oh[:T, :], in1=pos[:T, :], op=ALU.mult), [oh, pos], [oh])
                        Vop(lambda e: e.reduce_sum(out=destf[:T, k:k + 1], in_=oh[:T, :], axis=AX.X), [oh], [destf])
                    S.append(c12)

                def c13():
                    Vop(lambda e: e.tensor_copy(out=DEST[:T, i, :], in_=destf[:T, :]), [destf], [DEST])
                    Vop(lambda e: e.tensor_scalar(out=rt[:T, 8:12], in0=destf[:T, :], scalar1=float(NSLOT), scalar2=None, op0=ALU.is_lt), [destf], [rt])
                    Vop(lambda e: e.tensor_scalar(out=rt[:T, 4:8], in0=rt[:T, 4:8], scalar1=rt[:T, 2:3], scalar2=None, op0=ALU.mult), [rt], [rt])
                    Vop(lambda e: e.tensor_tensor(out=GATES[:T, i, :], in0=rt[:T, 4:8], in1=rt[:T, 8:12], op=ALU.mult), [rt], [GATES])
                    for k in range(4):
                        OP("gpsimd", lambda e, k=k: e.indirect_dma_start(
                            out=XS, out_offset=bass.IndirectOffsetOnAxis(ap=DEST[:T, i, k:k + 1], axis=0), in_=h2b[:T, :], in_offset=None,
                            bounds_check=bcreg(e), oob_is_err=False), r=[h2b, DEST, dXS], sem=scs[k])
                S.append(c13)
                return S

            def run_merged(A, B):
                na, nb = len(A), len(B)
                ia = ib = 0
                while ia < na or ib < nb:
                    if ib >= nb or (ia < na and ia * nb <= ib * na):
                        A[ia](); ia += 1
                    else:
                        B[ib](); ib += 1

            load_x(0)
            run_merged(steps1(0), [])
            for i in range(NTILE):
                if i + 1 < NTILE:
                    load_x(i + 1)
                    run_merged(steps2(i), steps1(i + 1))
                else:
                    run_merged(steps2(i), [])

    def phase_experts(l):
        with ExitStack() as sp:
            wgu = [sb(f"wgu{j}", [128, 8, 2 * DFF], BF16, sp) for j in range(2)]
            wdn = [sb(f"wdn{j}", [128, 8, D], BF16, sp) for j in range(2)]
            xT = [sb(f"xT{j}", [128, 8, C], BF16, sp) for j in range(2)]
            bdst = [sb(f"bdst{j}", [33, D], F32, sp) for j in range(2)]
            bd33 = [sb(f"bd33{j}", [33, D], BF16, sp) for j in range(2)]
            aT = sb("aT", [128, 8, C], BF16, sp)
            ones33 = sb("ones33e", [33, 128], BF16, sp)
            bgu_r = sb("bgu_r", [32, 2 * DFF], F32, sp)
            bguT = sb("bguT", [128, 16, 32], F32, sp)
            gc = [sb(f"gc{j}", [128, C], F32, sp) for j in range(2)]
            sg = [sb(f"sg{j}", [128, C], F32, sp) for j in range(2)]
            l1 = [sb(f"l1{j}", [128, C], F32, sp) for j in range(2)]
            ysb = [sb(f"ysb{j}", [128, D], F32, sp) for j in range(2)]
            wgs = [sem(f"ewg{l}{j}") for j in range(2)]; wds = [sem(f"ewd{l}{j}") for j in range(2)]
            xts = [sem(f"ext{l}{j}") for j in range(2)]; bds = [sem(f"ebd{l}{j}") for j in range(2)]
            yss = [sem(f"eys{l}{j}") for j in range(2)]; es0 = sem(f"es0{l}")
            Vop(lambda e: e.memset(ones33[:], 1.0), [], [ones33])
            if l == 0:
                print("phase E sbuf remaining", nc.sbuf_bytes_remaining)
            for j in range(2):
                Vop(lambda e, j=j: e.memset(bdst[j][:], 0.0), [], [bdst[j]])
            OP("sync", lambda e: e.dma_start(out=bgu_r[:], in_=b_gu[l]), w=[bgu_r], sem=es0)
            for half in range(2):
                def tb(e, half=half):
                    last = None
                    for c in range(8):
                        fc = half * 8 + c
                        last = e.transpose(PS[half][0:128, c * 32:(c + 1) * 32], bgu_r[:, fc * 128:(fc + 1) * 128], ident_f[:32, :32])
                    return last
                Top(tb, [bgu_r, ident_f], [BK[2 * half]])
                Vop(lambda e, half=half: e.tensor_copy(out=bguT[:, half * 8:(half + 1) * 8, :],
                                                       in_=PS[half][:, 0:256].rearrange("p (c e) -> p c e", c=8)), [BK[2 * half]], [bguT])

            def load_expert(ex):
                j = ex % 2
                cast_load(wgu[j], lambda a, b: wgu[j][:, :, a:b], lambda a, b: w_gu[l, ex][:, a:b].rearrange("(kc p) f -> p kc f", p=128), 2 * DFF, wgs[j])
                cast_load(wdn[j], lambda a, b: wdn[j][:, :, a:b], lambda a, b: w_down[l, ex][:, a:b].rearrange("(kc p) f -> p kc f", p=128), D, wds[j])
                OP("sync", lambda e: [e.dma_start(out=bdst[j][r:r + 1, :], in_=b_down[l, ex:ex + 1, :]) for r in (0, 32)],
                   w=[bdst[j]], sem=bds[j], ndma=2)
                OP("sync", lambda e: e.dma_start_transpose(out=xT[j][:], in_=XS[ex * C:(ex + 1) * C, :]), r=[dXS], w=[xT[j]], sem=xts[j])

            load_expert(0)
            ydma = 0
            for ex in range(NEXP):
                j = ex % 2
                if ex + 1 < NEXP:
                    load_expert(ex + 1)
                Vop(lambda e, j=j: e.tensor_scalar_mul(out=bdst[j][:], in0=bdst[j][:], scalar1=2.0), [bdst[j]], [bdst[j]])
                Vop(lambda e, j=j: e.tensor_copy(out=bd33[j][:], in_=bdst[j][:]), [bdst[j]], [bd33[j]])
                Vop(lambda e, j=j: e.tensor_sub(out=bdst[j][:], in0=bdst[j][:], in1=bd33[j][:]), [bdst[j], bd33[j]], [bdst[j]])
                Vop(lambda e, j=j: e.tensor_copy(out=bd33[j][32:33, :], in_=bdst[j][32:33, :]), [bdst[j]], [bd33[j]])
                chunks = [(c, min(c + 512, C)) for c in range(0, C, 512)]
                for pj in range(8):
                    s = pj % 2
                    pg, pl = PS[2 * s], PS[2 * s + 1]
                    bg_, bl_ = [BK[4 * s], BK[4 * s + 1]], [BK[4 * s + 2], BK[4 * s + 3]]

                    def gu(e, pj=pj, j=j, pg=pg, pl=pl):
                        last = None
                        for (dst, fc) in ((pg, pj), (pl, pj + 8)):
                            for (a, b) in chunks:
                                for kc in range(8):
                                    last = e.matmul(dst[:, a:b], lhsT=wgu[j][:, kc, fc * 128:(fc + 1) * 128], rhs=xT[j][:, kc, a:b],
                                                    start=(kc == 0), stop=(kc == 7))
                        return last
                    Top(gu, [wgu[j], xT[j]], bg_ + bl_)
                    Vop(lambda e, s=s, pg=pg, pj=pj, ex=ex: e.tensor_scalar(out=gc[s][:, :], in0=pg[:, 0:C], scalar1=bguT[:, pj, ex:ex + 1], scalar2=7.0,
                                                                           op0=ALU.add, op1=ALU.min), bg_ + [bguT], [gc[s]])
                    Aop(lambda e, s=s: e.activation(out=sg[s][:, :], in_=gc[s][:, :], func=AF.Tanh, scale=0.851), [gc[s]], [sg[s]])
                    Vop(lambda e, s=s, pl=pl, pj=pj, ex=ex: e.tensor_scalar(out=l1[s][:, :], in0=pl[:, 0:C], scalar1=bguT[:, pj + 8, ex:ex + 1], scalar2=-7.0,
                                                                           op0=ALU.add, op1=ALU.max), bl_ + [bguT], [l1[s]])
                    Vop(lambda e, s=s: e.tensor_scalar(out=l1[s][:, :], in0=l1[s][:, :], scalar1=7.0, scalar2=1.0, op0=ALU.min, op1=ALU.add), [l1[s]], [l1[s]])
                    Vop(lambda e, s=s: e.tensor_tensor(out=gc[s][:, :], in0=gc[s][:, :], in1=l1[s][:, :], op=ALU.mult), [gc[s], l1[s]], [gc[s]])
                    Vop(lambda e, s=s, pj=pj: e.scalar_tensor_tensor(out=aT[:, pj, :], in0=sg[s][:, :], scalar=1.0, in1=gc[s][:, :], op0=ALU.add, op1=ALU.mult),
                        [sg[s], gc[s]], [aT])
                for t in range(CT):
                    pd = t % 4
                    yj = ydma % 2
                    ydma += 1

                    def dn(e, t=t, j=j, pd=pd):
                        last = None
                        for n in range(2):
                            o = PS[pd][:, n * 512:(n + 1) * 512]
                            for fc in range(8):
                                e.matmul(o, lhsT=aT[:, fc, t * 128:(t + 1) * 128], rhs=wdn[j][:, fc, n * 512:(n + 1) * 512], start=(fc == 0), stop=False)
                            last = e.matmul(o, lhsT=ones33[:, :], rhs=bd33[j][:, n * 512:(n + 1) * 512], start=False, stop=True)
                        return last
                    Top(dn, [aT, wdn[j], ones33, bd33[j]], [BK[2 * pd], BK[2 * pd + 1]])
                    Aop(lambda e, pd=pd, yj=yj: e.activation(out=ysb[yj][:, :], in_=PS[pd][:, :], func=AF.Copy, scale=0.5), [BK[2 * pd], BK[2 * pd + 1]], [ysb[yj]])
                    OP("sync", lambda e, ex=ex, t=t, yj=yj: e.dma_start(out=YS[ex * C + t * 128:ex * C + (t + 1) * 128, :], in_=ysb[yj][:, :]),
                       r=[ysb[yj]], sem=yss[yj])

    def phase_combine(l):
        with ExitStack() as sp:
            G2 = sb("G2", [128, D], F32, sp); l2g = sb("l2g", [128, D], F32, sp); l2b = sb("l2b", [128, D], F32, sp)
            x1 = [sb(f"x1{j}", [128, D], F32, sp) for j in range(2)]
            x2b = [sb(f"x2b{j}", [128, D], F32, sp) for j in range(2)]
            xo = [sb(f"xoc{j}", [128, D], F32, sp) for j in range(2)]
            acc = sb("acc", [128, D], F32, sp)
            yg = [[sb(f"yg{j}{k}", [128, D], F32, sp) for k in range(4)] for j in range(2)]
            stt = [sb(f"stt_c{j}", [128, 12], F32, sp) for j in range(2)]; mv = [sb(f"mv_c{j}", [128, 4], F32, sp) for j in range(2)]
            cs = sem(f"cc{l}"); gs = [sem(f"cg{l}{j}") for j in range(2)]; xs_ = [sem(f"cx{l}{j}") for j in range(2)]
            os_ = [sem(f"co{l}{j}") for j in range(2)]
            OP("sync", lambda e: [e.dma_start(out=l2g[:], in_=ln2_g[l:l + 1, :].partition_broadcast(128)),
                                  e.dma_start(out=l2b[:], in_=ln2_b[l:l + 1, :].partition_broadcast(128))], w=[l2g, l2b], sem=cs, ndma=2)
            for j in range(2):
                for k in range(4):
                    Vop(lambda e, j=j, k=k: e.memset(yg[j][k][:], 0.0), [], [yg[j][k]])

            def load_g2(g):
                OP("sync", lambda e: e.dma_start(out=G2[:], in_=MODS[l, g:g + 1, 5 * D:6 * D].partition_broadcast(128)), r=[dMODS], w=[G2], sem=cs)
            load_g2(0)

            def tl(i):
                return 128 if i < NT else DEC

            def prefetch(i):
                T = tl(i)
                j = i % 2
                r0 = i * 128
                OP("sync", lambda e: e.dma_start(out=x1[j][:T], in_=X[r0:r0 + T, :]), r=[dXt[i]], w=[x1[j]], sem=xs_[j])
                OP("gpsimd", lambda e: [e.indirect_dma_start(
                    out=yg[j][k][:, :], out_offset=None, in_=YS, in_offset=bass.IndirectOffsetOnAxis(ap=DEST[:, i, k:k + 1], axis=0),
                    bounds_check=bcreg(e), oob_is_err=False) for k in range(4)], r=[dYS, DEST], w=yg[j], sem=gs[j], ndma=4)

            def S1(i):
                T = tl(i)
                j = i % 2
                if i == NT:
                    load_g2(1)
                Vop(lambda e: e.tensor_scalar(out=acc[:T, :], in0=yg[j][0][:T, :], scalar1=GATES[:T, i, 0:1], scalar2=None, op0=ALU.mult),
                    [yg[j][0], GATES], [acc])
                for k in range(1, 4):
                    Vop(lambda e, k=k: e.scalar_tensor_tensor(out=acc[:T, :], in0=yg[j][k][:T, :], scalar=GATES[:T, i, k:k + 1],
                                                              in1=acc[:T, :], op0=ALU.mult, op1=ALU.add), [yg[j][k], GATES, acc], [acc])
                Vop(lambda e: e.tensor_tensor(out=acc[:T, :], in0=acc[:T, :], in1=G2[:T, :], op=ALU.mult), [acc, G2], [acc])
                Vop(lambda e: e.scalar_tensor_tensor(out=x2b[j][:T, :], in0=x1[j][:T, :], scalar=ALPHA, in1=acc[:T, :], op0=ALU.mult, op1=ALU.add),
                    [x1[j], acc], [x2b[j]])
                ln_stats(x2b[j], stt[j], mv[j], T)

            def S2(i):
                T = tl(i)
                j = i % 2
                r0 = i * 128
                ln_apply(x2b[j], xo[j], mv[j], l2g, l2b, T)
                if l == DEPTH - 1:
                    dst = y_p[r0:r0 + T, :] if i < NT else y_s
                    OP("sync", lambda e: e.dma_start(out=dst, in_=xo[j][:T]), r=[xo[j]], sem=os_[j])
                else:
                    OP("sync", lambda e: e.dma_start(out=X[r0:r0 + T, :], in_=xo[j][:T]), r=[xo[j]], w=[dXt[i]], sem=os_[j])
            prefetch(0)
            if NTILE > 1:
                prefetch(1)
            S1(0)
            for i in range(NTILE):
                if i + 1 < NTILE:
                    S1(i + 1)
                if i + 2 < NTILE:
                    prefetch(i + 2)
                S2(i)

    for l in range(DEPTH):
        phase_mix(l)
        OP("sync", lambda e, l=l: e.dma_start(out=cnt_out[l], in_=cnt[:]), r=[cnt], sem=dsem)
        P.barrier()
        if dbg and l == 0:
            OP("sync", lambda e: [e.dma_start(out=dbgX1, in_=X), e.dma_start(out=dbgD, in_=DEST[:].rearrange("p i k -> p (i k)")),
                                  e.dma_start(out=dbgG, in_=GATES[:].rearrange("p i k -> p (i k)")), e.dma_start(out=dbgXS, in_=XS)], sem=dsem, ndma=4)
            P.barrier()
        phase_experts(l)
        P.barrier()
        if dbg and l == 0:
            OP("sync", lambda e: [e.dma_start(out=dbgYS, in_=YS)], sem=dsem, ndma=1)
            P.barrier()
        phase_combine(l)
        P.barrier()

    block = E(nc.Block())
    P.emit(block, esems)
    st.close()
    return nc


_CACHE = {}
CAP = 768


def make_in_maps(inputs, ncores, NT, C):
    cst = host_consts(C)
    f = lambda a: np.ascontiguousarray(np.asarray(a, dtype=np.float32))
    shared = {
        "rel_table": f(inputs["rel_table"]), "ln0_g": f(inputs["ln0_g"]).reshape(1, D), "ln0_b": f(inputs["ln0_b"]).reshape(1, D),
        "w_ada": f(inputs["w_ada"]), "b_ada": f(inputs["b_ada"]), "w_in": f(inputs["w_in"]), "b_in": f(inputs["b_in"]),
        "sinks": f(inputs["sinks"]), "conv_w": f(inputs["conv_w"]).reshape(DEPTH, 3 * 512),
        "w_oa": f(inputs["w_oa"]), "w_ob": f(inputs["w_ob"]), "w_o": f(inputs["w_o"]),
        "ln1_g": f(inputs["ln1_g"]), "ln1_b": f(inputs["ln1_b"]), "w_router": f(inputs["w_router"]), "b_router": f(inputs["b_router"]),
        "w_gu": f(inputs["w_gu"]), "b_gu": f(inputs["b_gu"]), "w_down": f(inputs["w_down"]), "b_down": f(inputs["b_down"]),
        "ln2_g": f(inputs["ln2_g"]), "ln2_b": f(inputs["ln2_b"]),
    }
    shared.update(cst)
    maps = []
    for c in range(ncores):
        m = dict(shared)
        m["xp"] = f(inputs["x_prompt"][c][:NT * 128])
        m["xsm"] = f(inputs["x_sample"][c])
        m["cvec"] = np.stack([f(inputs["c_prompt"][c]), f(inputs["c_sample"][c])], 0)
        m["ck"] = f(inputs["cache_k"][:, c]).reshape(DEPTH, 128, 128)
        m["cv"] = f(inputs["cache_v"][:, c]).reshape(DEPTH, 128, 128)
        m["sconv"] = f(inputs["state_conv"][:, c])
        maps.append(m)
    return maps


def kernel(**inputs):
    NT = SEQ // 128
    key = (NT, CAP)
    if key not in _CACHE:
        _CACHE[key] = build(NT, CAP)
    nc = _CACHE[key]
    maps = make_in_maps(inputs, NCORES, NT, CAP)
    res = run_bass_kernel_spmd(nc, maps, core_ids=list(range(NCORES)))
    R = res.results
    try:
        cm = np.stack([np.asarray(R[c]["cnt_out"], dtype=np.float32)[:, 0, :] for c in range(NCORES)], 0)
        print("[kernel] expert load per (core, layer): max=%d min=%d cap=%d" % (cm.max(), cm.min(), CAP), flush=True)
        print("[kernel] per core/layer max:", cm.max(axis=2).astype(int).tolist(), flush=True)
    except Exception as ex:
        print("[kernel] cnt diag failed", ex)
    g = lambda name: np.stack([np.asarray(R[c][name], dtype=np.float32) for c in range(NCORES)], 0)
    y_prompt = g("y_p")
    y_sample = g("y_s")

    def kvo(name):
        a = g(name)
        return np.ascontiguousarray(a.transpose(1, 0, 2, 3)).reshape(DEPTH, NCORES, 128, 2, 64)

    def uo(name):
        return np.ascontiguousarray(g(name).transpose(1, 0, 2, 3))
    return (y_prompt, y_sample, kvo("nk_p"), kvo("nv_p"), uo("nu_p"), kvo("nk_s"), kvo("nv_s"), uo("nu_s"))
```

```python
import math
from contextlib import ExitStack

import numpy as np
import concourse.bass as bass
import concourse.mybir as mybir
from concourse.bass_utils import run_bass_kernel_spmd

F32 = mybir.dt.float32
BF16 = mybir.dt.bfloat16
I32 = mybir.dt.int32
U32 = mybir.dt.uint32
AF = mybir.ActivationFunctionType
ALU = mybir.AluOpType
AX = mybir.AxisListType

D = 1024
DEPTH = 2
NEXP = 32
DFF = 1024
INW = 4352
LN_EPS = 1e-5
ALPHA = float((2 * DEPTH) ** 0.25)
NCORES = 8
SEQ = 4096
DEC = 64

ENGINES = ("tensor", "vector", "scalar", "gpsimd", "sync")


class Sem:
    def __init__(self, handle, name):
        self.h = handle
        self.name = name
        self.count = 0


class Buf:
    __slots__ = ("name", "w", "r")

    def __init__(self, name):
        self.name = name
        self.w = {}
        self.r = {}


class Op:
    __slots__ = ("eng", "fn", "deps", "seq", "kind", "sem", "ndma", "val")


class Prog:
    def __init__(self, nc, same_engine_sync=True):
        self.nc = nc
        self.ops = {e: [] for e in ENGINES}
        self.nseq = {e: 0 for e in ENGINES}
        self.same_engine_sync = same_engine_sync
        self.milestones = {e: set() for e in ENGINES}
        self.all_ops = []
        self.ALL = Buf("ALL")

    @staticmethod
    def _merge(dst, key, tok, hard):
        old = dst.get(key)
        if old is None:
            dst[key] = (tok, hard)
        else:
            t = tok if tok[2] > old[0][2] else old[0]
            dst[key] = (t, hard or old[1])

    def op(self, eng, fn, reads=(), writes=(), dma_sem=None, ndma=1, barrier=False):
        o = Op()
        o.eng = eng
        o.fn = fn
        o.kind = 'd' if dma_sem is not None else 'c'
        o.sem = dma_sem
        o.ndma = ndma
        reads = list(reads)
        writes = list(writes)
        if barrier:
            writes.append(self.ALL)
        else:
            reads.append(self.ALL)
        deps = {}
        for b in reads:
            for k, t in b.w.items():
                self._merge(deps, k, t, True)
        for b in writes:
            for k, t in b.w.items():
                self._merge(deps, k, t, True)
            for k, t in b.r.items():
                self._merge(deps, k, t, b is not self.ALL)
        if dma_sem is not None and dma_sem.count > 0:
            self._merge(deps, ('d', id(dma_sem)), ('d', dma_sem, dma_sem.count), True)
        o.deps = deps
        o.seq = self.nseq[eng]
        self.nseq[eng] += 1
        if o.kind == 'd':
            dma_sem.count += 16 * ndma
            o.val = dma_sem.count
            tok = ('d', dma_sem, o.val)
            key = ('d', id(dma_sem))
        else:
            tok = ('c', eng, o.seq)
            key = ('c', eng)
        for b in writes:
            b.w = {key: tok}
            b.r = {}
        for b in reads:
            if b in writes:
                continue
            old = b.r.get(key)
            if old is None or old[2] < tok[2]:
                b.r[key] = tok
        self.ops[eng].append(o)
        self.all_ops.append(o)
        return o

    def barrier(self):
        for e in ENGINES:
            self.op(e, lambda en: None, barrier=True)

    def _skip(self, o, tok, hard):
        e2 = tok[1]
        if e2 != o.eng:
            return False
        if e2 == "tensor":
            return True
        if e2 == "sync":
            return True
        return (not hard) or (not self.same_engine_sync)

    def emit(self, block, esems):
        for o in self.all_ops:
            for key, (tok, hard) in o.deps.items():
                if tok[0] != 'c' or self._skip(o, tok, hard):
                    continue
                self.milestones[tok[1]].add(tok[2])
        rank = {e: {s: i + 1 for i, s in enumerate(sorted(st))} for e, st in self.milestones.items()}

        def make(eng):
            def body(e):
                waited = {}
                for o in self.ops[eng]:
                    for key, (tok, hard) in o.deps.items():
                        if tok[0] == 'c':
                            if self._skip(o, tok, hard):
                                continue
                            semh = esems[tok[1]]
                            val = rank[tok[1]][tok[2]]
                            wk = ('c', tok[1])
                        else:
                            semh = tok[1].h
                            val = tok[2]
                            wk = ('d', id(tok[1]))
                        if waited.get(wk, 0) >= val:
                            continue
                        waited[wk] = val
                        e.wait_ge(semh, val)
                    res = o.fn(e)
                    if o.kind == 'd':
                        if not isinstance(res, (list, tuple)):
                            res = [res]
                        assert len(res) == o.ndma, (len(res), o.ndma)
                        for ins in res:
                            ins.then_inc(o.sem.h, 16)
                    elif o.seq in self.milestones[eng]:
                        if isinstance(res, (list, tuple)):
                            res = res[-1]
                        if res is None:
                            res = e.nop()
                        res.then_inc(esems[eng], 1)
            return body

        for eng in ENGINES:
            if self.ops[eng]:
                getattr(block, eng)(make(eng))


class TB:
    def __init__(self, t, name):
        self.t = t
        self.b = Buf(name)

    def __getitem__(self, k):
        return self.t[k]


def _rel_bucket_np(rel):
    half = 16
    max_exact = 8
    n = np.abs(rel)
    n_f = np.maximum(n, 1).astype(np.float32)
    large = max_exact + (np.log(n_f / np.float32(max_exact)) / np.float32(math.log(128 / max_exact))
                         * np.float32(half - max_exact)).astype(np.int32)
    large = np.minimum(large, half - 1)
    return np.where(rel > 0, half, 0) + np.where(n < max_exact, n, large)


def host_consts(C):
    q = np.arange(128)[:, None]
    kk = np.arange(256)[None, :]
    rel = kk - 128 - q
    bk = _rel_bucket_np(rel)
    oht = np.zeros((32, 256, 128), np.float32)
    for b in range(32):
        oht[b] = (bk == b).T.astype(np.float32)
    valid = np.where(q < 64, kk < 192, kk >= 64)
    mask = np.where(valid, 0.0, -1e30).astype(np.float32)
    ident = np.eye(128, dtype=np.float32)
    ustrict = (np.arange(128)[:, None] < np.arange(128)[None, :]).astype(np.float32)
    iotac = np.tile((np.arange(32) * C).astype(np.float32)[None, :], (128, 1))
    return {"c_oht": oht.reshape(32, 256 * 128), "c_mask": mask, "c_ident": ident,
            "c_ustrict": ustrict, "c_iotac": iotac}


def build(NT, C, dbg=False):
    TP = NT * 128
    TT = TP + DEC
    NTILE = NT + 1
    NSLOT = NEXP * C
    CT = C // 128
    BIG = 1.0e6
    assert C % 128 == 0

    nc = bass.Bass("TRN2", target_bir_lowering=False)

    def din(name, shape, dt=F32):
        return nc.dram_tensor(name, list(shape), dt, kind="ExternalInput").ap()

    def dout(name, shape, dt=F32):
        return nc.dram_tensor(name, list(shape), dt, kind="ExternalOutput").ap()

    def dscr(name, shape, dt=F32):
        return nc.dram_tensor(name, list(shape), dt, kind="Internal").ap()

    xp = din("xp", [TP, D]); xsm = din("xsm", [DEC, D]); cvec = din("cvec", [2, D])
    ck = din("ck", [DEPTH, 128, 128]); cv = din("cv", [DEPTH, 128, 128]); sconv = din("sconv", [DEPTH, 2, 512])
    rel_table = din("rel_table", [32, 8]); ln0_g = din("ln0_g", [1, D]); ln0_b = din("ln0_b", [1, D])
    w_ada = din("w_ada", [DEPTH, D, 6 * D]); b_ada = din("b_ada", [DEPTH, 6 * D])
    w_in = din("w_in", [DEPTH, D, INW]); b_in = din("b_in", [DEPTH, INW])
    sinks = din("sinks", [DEPTH, 8]); conv_w = din("conv_w", [DEPTH, 3 * 512])
    w_oa = din("w_oa", [DEPTH, 512, D]); w_ob = din("w_ob", [DEPTH, 512, D]); w_o = din("w_o", [DEPTH, D, D])
    ln1_g = din("ln1_g", [DEPTH, D]); ln1_b = din("ln1_b", [DEPTH, D])
    w_router = din("w_router", [DEPTH, D, NEXP]); b_router = din("b_router", [DEPTH, NEXP])
    w_gu = din("w_gu", [DEPTH, NEXP, D, 2 * DFF]); b_gu = din("b_gu", [DEPTH, NEXP, 2 * DFF])
    w_down = din("w_down", [DEPTH, NEXP, DFF, D]); b_down = din("b_down", [DEPTH, NEXP, D])
    ln2_g = din("ln2_g", [DEPTH, D]); ln2_b = din("ln2_b", [DEPTH, D])
    c_oht = din("c_oht", [32, 256 * 128]); c_mask = din("c_mask", [128, 256]); c_ident = din("c_ident", [128, 128])
    c_ustrict = din("c_ustrict", [128, 128]); c_iotac = din("c_iotac", [128, 32])

    y_p = dout("y_p", [TP, D]); y_s = dout("y_s", [DEC, D])
    nk_p = dout("nk_p", [DEPTH, 128, 128]); nv_p = dout("nv_p", [DEPTH, 128, 128]); nu_p = dout("nu_p", [DEPTH, 2, 512])
    cnt_out = dout("cnt_out", [DEPTH, 128, 32])
    nk_s = dout("nk_s", [DEPTH, 128, 128]); nv_s = dout("nv_s", [DEPTH, 128, 128]); nu_s = dout("nu_s", [DEPTH, 2, 512])

    if dbg:
        dbgX0 = dout("dbgX0", [TT, D]); dbgM = dout("dbgM", [DEPTH, 2, 6 * D]); dbgX1 = dout("dbgX1", [TT, D]); dbgB = dout("dbgB", [128, 8 * 256])
        dbgD = dout("dbgD", [128, NTILE * 4], I32); dbgG = dout("dbgG", [128, NTILE * 4]); dbgYS = dout("dbgYS", [NSLOT, D]); dbgXS = dout("dbgXS", [NSLOT, D], BF16)
    X = dscr("X", [TT, D]); XS = dscr("XS", [NSLOT, D], BF16); YS = dscr("YS", [NSLOT, D])
    MODS = dscr("MODS", [DEPTH, 2, 6 * D])

    st = ExitStack()
    E = st.enter_context
    nsem = [0]

    def sem(name):
        nsem[0] += 1
        return Sem(E(nc.semaphore(name)), name)

    esems = {e: E(nc.semaphore("es_" + e)) for e in ENGINES}
    P = Prog(nc)

    uniq = [0]

    def sb(name, shape, dt=F32, stack=None):
        uniq[0] += 1
        t = (stack or st).enter_context(nc.sbuf_tensor(f"{name}_{uniq[0]}", list(shape), dt))
        return TB(t, name)


    _bc = {}

    def bcreg(e):
        if "r" not in _bc:
            r = e.alloc_register("bcreg")
            e.reg_mov(r, NSLOT - 1)
            _bc["r"] = r
        return _bc["r"]

    def rw(lst):
        return [x.b for x in lst]

    def OP(eng, fn, r=(), w=(), sem=None, ndma=1):
        return P.op(eng, fn, reads=rw(r), writes=rw(w), dma_sem=sem, ndma=ndma)

    def Vop(fn, r=(), w=()): return OP("vector", fn, r, w)
    def Aop(fn, r=(), w=()): return OP("scalar", fn, r, w)
    def Gop(fn, r=(), w=()): return OP("gpsimd", fn, r, w)
    def Top(fn, r=(), w=()): return OP("tensor", fn, r, w)

    dXt = [TB(None, f"dX{i}") for i in range(NTILE)]; dXS = TB(XS, "dXS"); dYS = TB(YS, "dYS"); dMODS = TB(MODS, "dMODS")

    PS = [E(nc.psum_tensor(f"ps{i}", [128, 1024], F32)) for i in range(4)]
    BK = [TB(PS[k // 2][:, (k % 2) * 512:(k % 2 + 1) * 512], f"bank{k}") for k in range(8)]

    def bank(k):
        return PS[k // 2][:, (k % 2) * 512:(k % 2 + 1) * 512]

    def bank_bf(k):
        return bank(k).bitcast(BF16)

    ident_f = sb("ident_f", [128, 128]); ident_b = sb("ident_b", [128, 128], BF16)
    ustrict = sb("ustrict", [128, 128], BF16); ones_b = sb("ones_b", [128, 128], BF16)
    iotac = sb("iotac", [128, 32]); negh = sb("negh", [128, 1])
    biasT = sb("biasT", [128, 8, 256])
    DEST = sb("DEST", [128, NTILE, 4], I32); GATES = sb("GATES", [128, NTILE, 4])
    cnt = sb("cnt", [128, 32])
    cld = sem("cld")

    def ld_const():
        OP("sync", lambda e: [e.dma_start(out=ident_f[:], in_=c_ident), e.dma_start(out=iotac[:], in_=c_iotac)],
           w=[ident_f, iotac], sem=cld, ndma=2)
        Vop(lambda e: e.tensor_copy(out=ident_b[:], in_=ident_f[:]), [ident_f], [ident_b])
        Vop(lambda e: e.memset(ones_b[:], 1.0), [], [ones_b])
        Vop(lambda e: e.memset(negh[:], -0.5), [], [negh])
        Vop(lambda e: e.memset(DEST[:, :, :], 1000000), [], [DEST])
    ld_const()


    def ln_tile(xin, xout, stt, mv, g_t, b_t, T):
        for c in range(2):
            Vop(lambda e, c=c: e.bn_stats(out=stt[:T, c * 6:(c + 1) * 6], in_=xin[:T, c * 512:(c + 1) * 512]), [xin], [stt])
        Vop(lambda e: e.bn_aggr(out=mv[:T, 0:2], in_=stt[:T, :]), [stt], [mv])
        Vop(lambda e: e.tensor_scalar_add(out=mv[:T, 2:3], in0=mv[:T, 1:2], scalar1=LN_EPS), [mv], [mv])
        Gop(lambda e: e.tensor_tensor(out=mv[:T, 2:3], in0=mv[:T, 2:3], in1=negh[:T, :], op=ALU.pow), [mv, negh], [mv])
        Vop(lambda e: e.scalar_tensor_tensor(out=mv[:T, 3:4], in0=mv[:T, 0:1], scalar=-1.0, in1=mv[:T, 2:3],
                                             op0=ALU.mult, op1=ALU.mult), [mv], [mv])
        Aop(lambda e: e.activation(out=xout[:T, :], in_=xin[:T, :], func=AF.Identity, bias=mv[:T, 3:4], scale=mv[:T, 2:3]),
            [xin, mv], [xout])
        Vop(lambda e: e.tensor_tensor(out=xout[:T, :], in0=xout[:T, :], in1=g_t[:T, :], op=ALU.mult), [xout, g_t], [xout])
        Vop(lambda e: e.tensor_tensor(out=xout[:T, :], in0=xout[:T, :], in1=b_t[:T, :], op=ALU.add), [xout, b_t], [xout])

    def ln_stats(xin, stt, mv, T):
        for c in range(2):
            Vop(lambda e, c=c: e.bn_stats(out=stt[:T, c * 6:(c + 1) * 6], in_=xin[:T, c * 512:(c + 1) * 512]), [xin], [stt])
        Vop(lambda e: e.bn_aggr(out=mv[:T, 0:2], in_=stt[:T, :]), [stt], [mv])
        Vop(lambda e: e.tensor_scalar_add(out=mv[:T, 2:3], in0=mv[:T, 1:2], scalar1=LN_EPS), [mv], [mv])
        Gop(lambda e: e.tensor_tensor(out=mv[:T, 2:3], in0=mv[:T, 2:3], in1=negh[:T, :], op=ALU.pow), [mv, negh], [mv])
        Vop(lambda e: e.scalar_tensor_tensor(out=mv[:T, 3:4], in0=mv[:T, 0:1], scalar=-1.0, in1=mv[:T, 2:3],
                                             op0=ALU.mult, op1=ALU.mult), [mv], [mv])

    def ln_apply(xin, xout, mv, g_t, b_t, T):
        Aop(lambda e: e.activation(out=xout[:T, :], in_=xin[:T, :], func=AF.Identity, bias=mv[:T, 3:4], scale=mv[:T, 2:3]),
            [xin, mv], [xout])
        Vop(lambda e: e.tensor_tensor(out=xout[:T, :], in0=xout[:T, :], in1=g_t[:T, :], op=ALU.mult), [xout, g_t], [xout])
        Vop(lambda e: e.tensor_tensor(out=xout[:T, :], in0=xout[:T, :], in1=b_t[:T, :], op=ALU.add), [xout, b_t], [xout])

    def cast_load(dst, dst_fn, src_fn, ncols, sem_, reads=()):
        pcs = [(c, min(c + 2048, ncols)) for c in range(0, ncols, 2048)]
        OP("gpsimd", lambda e: [e.dma_start(out=dst_fn(a, b), in_=src_fn(a, b)) for (a, b) in pcs],
           r=list(reads), w=[dst], sem=sem_, ndma=len(pcs))

    with ExitStack() as s0:
        oht = sb("oht", [32, 256 * 128], BF16, s0)
        mask_sb = sb("mask_sb", [128, 256], F32, s0)
        OP("sync", lambda e: e.dma_start(out=mask_sb[:], in_=c_mask), w=[mask_sb], sem=cld)
        ust_f = sb("ust_f", [128, 128], F32, s0)
        tab = sb("tab", [32, 8], F32, s0); tab_hi = sb("tab_hi", [32, 8], BF16, s0)
        tab_d = sb("tab_d", [32, 8], F32, s0); tab_lo = sb("tab_lo", [32, 8], BF16, s0)
        s_a = sem("setup_a"); s_b = sem("setup_b")
        cast_load(oht, lambda a, b: oht[:, a:b], lambda a, b: c_oht[:, a:b], 256 * 128, s_a)
        OP("sync", lambda e: [e.dma_start(out=tab[:], in_=rel_table), e.dma_start(out=ust_f[:], in_=c_ustrict)],
           w=[tab, ust_f], sem=s_b, ndma=2)
        Vop(lambda e: e.tensor_copy(out=ustrict[:], in_=ust_f[:]), [ust_f], [ustrict])
        Vop(lambda e: e.tensor_copy(out=tab_hi[:], in_=tab[:]), [tab], [tab_hi])
        Vop(lambda e: e.tensor_sub(out=tab_d[:], in0=tab[:], in1=tab_hi[:]), [tab, tab_hi], [tab_d])
        Vop(lambda e: e.tensor_copy(out=tab_lo[:], in_=tab_d[:]), [tab_d], [tab_lo])
        for half in range(2):
            bks = [BK[4 * half + j] for j in range(2)]

            def mm_bias(e, half=half):
                last = None
                for kl in range(128):
                    kk = half * 128 + kl
                    o = PS[2 * half][:, kl * 8:(kl + 1) * 8]
                    e.matmul(o, lhsT=oht[:, kk * 128:(kk + 1) * 128], rhs=tab_hi[:], start=True, stop=False)
                    last = e.matmul(o, lhsT=oht[:, kk * 128:(kk + 1) * 128], rhs=tab_lo[:], start=False, stop=True)
                return last
            Top(mm_bias, [oht, tab_hi, tab_lo], bks)
            for h in range(8):
                def ev(e, half=half, h=h):
                    src = PS[2 * half][:, :].rearrange("q (kk h) -> q h kk", h=8)
                    return e.tensor_tensor(out=biasT[:, h, half * 128:(half + 1) * 128], in0=src[:, h, :],
                                           in1=mask_sb[:, half * 128:(half + 1) * 128], op=ALU.add)
                Vop(ev, bks + [mask_sb], [biasT])

    P.barrier()
    with ExitStack() as s0:
        cT = sb("cT", [128, 8, 2], F32, s0); cth = sb("cth", [128, 8, 2], F32, s0)
        scT = sb("scT", [128, 8, 2], BF16, s0)
        wada = [sb(f"wada{j}", [128, 8, 2048], BF16, s0) for j in range(2)]
        wsem = [sem(f"wada{j}") for j in range(2)]
        bada = sb("bada", [2, 6 * D], F32, s0)
        modsb = sb("modsb", [2, 6 * D], F32, s0)
        s_c = sem("setup_c"); s_m = sem("setup_m")

        def ld_c(e):
            with nc.allow_non_contiguous_dma(reason="tiny transposed load of c"):
                return [e.dma_start(out=cT[:, :, g], in_=cvec[g].rearrange("(kc p) -> p kc", p=128)) for g in range(2)]
        OP("sync", ld_c, w=[cT], sem=s_c, ndma=2)
        Aop(lambda e: e.activation(out=cth[:], in_=cT[:], func=AF.Tanh, scale=0.5), [cT], [cth])
        Vop(lambda e: e.tensor_scalar(out=cth[:], in0=cth[:], scalar1=0.5, scalar2=0.5, op0=ALU.mult, op1=ALU.add), [cth], [cth])
        Vop(lambda e: e.tensor_tensor(out=scT[:], in0=cth[:], in1=cT[:], op=ALU.mult), [cth, cT], [scT])
        it = 0
        for l in range(DEPTH):
            OP("sync", lambda e, l=l: [e.dma_start(out=bada[g:g + 1, :], in_=b_ada[l:l + 1, :]) for g in range(2)],
               w=[bada], sem=s_c, ndma=2)
            for pc in range(3):
                j = it % 2
                it += 1
                OP("gpsimd", lambda e, l=l, pc=pc, j=j: e.dma_start(
                    out=wada[j][:], in_=w_ada[l][:, pc * 2048:(pc + 1) * 2048].rearrange("(kc p) f -> p kc f", p=128)),
                   w=[wada[j]], sem=wsem[j])
                for n in range(4):
                    bk = BK[n % 2]

                    def mm(e, j=j, n=n):
                        last = None
                        for kc in range(8):
                            last = e.matmul(bank(n % 2)[0:2, :], lhsT=scT[:, kc, :], rhs=wada[j][:, kc, n * 512:(n + 1) * 512],
                                            start=(kc == 0), stop=(kc == 7))
                        return last
                    Top(mm, [scT, wada[j]], [bk])
                    c0 = pc * 2048 + n * 512
                    Vop(lambda e, n=n, c0=c0: e.tensor_tensor(out=modsb[:, c0:c0 + 512], in0=bank(n % 2)[0:2, :],
                                                              in1=bada[:, c0:c0 + 512], op=ALU.add), [bk, bada], [modsb])
            Vop(lambda e: e.tensor_scalar_add(out=modsb[:, D:2 * D], in0=modsb[:, D:2 * D], scalar1=1.0), [modsb], [modsb])
            Vop(lambda e: e.tensor_scalar(out=modsb[:, 2 * D:3 * D], in0=modsb[:, 2 * D:3 * D], scalar1=1.0, scalar2=0.5,
                                          op0=ALU.add, op1=ALU.mult), [modsb], [modsb])
            Vop(lambda e: e.tensor_scalar_add(out=modsb[:, 4 * D:6 * D], in0=modsb[:, 4 * D:6 * D], scalar1=1.0), [modsb], [modsb])
            OP("sync", lambda e, l=l: e.dma_start(out=MODS[l], in_=modsb[:]), r=[modsb], w=[dMODS], sem=s_m)

    P.barrier()
    with ExitStack() as s0:
        s_c = sem("setup_c2")
        g0 = sb("g0", [128, D], F32, s0); b0 = sb("b0", [128, D], F32, s0)
        OP("sync", lambda e: [e.dma_start(out=g0[:], in_=ln0_g.partition_broadcast(128)),
                              e.dma_start(out=b0[:], in_=ln0_b.partition_broadcast(128))], w=[g0, b0], sem=s_c, ndma=2)
        xi = [sb(f"xi{j}", [128, D], F32, s0) for j in range(3)]
        xo = [sb(f"xo{j}", [128, D], F32, s0) for j in range(2)]
        xis = [sem(f"xi{j}") for j in range(3)]; xos = [sem(f"xo{j}") for j in range(2)]
        stt = [sb(f"stt{j}", [128, 12], F32, s0) for j in range(2)]
        mv = [sb(f"mv{j}", [128, 4], F32, s0) for j in range(2)]

        def tl(i):
            return 128 if i < NT else DEC

        def ld0(i):
            T = tl(i)
            src = xp[i * 128:(i + 1) * 128, :] if i < NT else xsm
            OP("sync", lambda e: e.dma_start(out=xi[i % 3][:T], in_=src), w=[xi[i % 3]], sem=xis[i % 3])
        ld0(0)
        if NTILE > 1:
            ld0(1)
        ln_stats(xi[0], stt[0], mv[0], tl(0))
        for i in range(NTILE):
            T = tl(i)
            if i + 2 < NTILE:
                ld0(i + 2)
            if i + 1 < NTILE:
                ln_stats(xi[(i + 1) % 3], stt[(i + 1) % 2], mv[(i + 1) % 2], tl(i + 1))
            ln_apply(xi[i % 3], xo[i % 2], mv[i % 2], g0, b0, T)
            OP("sync", lambda e, T=T, i=i: e.dma_start(out=X[i * 128:i * 128 + T, :], in_=xo[i % 2][:T]),
               r=[xo[i % 2]], w=[dXt[i]], sem=xos[i % 2])
    P.barrier()


    dsem = sem("dbg")
    if dbg:
        OP("sync", lambda e: [e.dma_start(out=dbgX0, in_=X), e.dma_start(out=dbgM, in_=MODS), e.dma_start(out=dbgB, in_=biasT[:].rearrange("p h k -> p (h k)"))], sem=dsem, ndma=3)
        P.barrier()
    SEGS = [(0, 512), (512, 768), (768, 1280), (1280, 1792), (1792, 2304),
            (2304, 2816), (2816, 3328), (3328, 3840), (3840, 4352)]

    def shift_dma(e, dst, src_fn, sh, T):
        n = T - sh
        a = (n // 16) * 16
        if a == n:
            a -= 16
        return [e.dma_start(out=dst[sh:sh + a, :], in_=src_fn(0, a)),
                e.dma_start(out=dst[sh + a:T, :], in_=src_fn(a, n))]

    def phase_mix(l):
        with ExitStack() as sp:
            win = sb("win", [128, 8, INW], BF16, sp)
            woa = sb("woa", [128, 4, D], BF16, sp); wob = sb("wob", [128, 4, D], BF16, sp)
            wo = sb("wo", [128, 8, D], BF16, sp); wr = sb("wr", [128, 8, NEXP], BF16, sp)
            bin33 = sb("bin33", [33, INW], BF16, sp); br33 = sb("br33", [33, NEXP], BF16, sp)
            ones33 = sb("ones33", [33, 128], BF16, sp)
            wsm = sem(f"mw{l}"); csm = sem(f"mc{l}")
            cast_load(win, lambda a, b: win[:, :, a:b], lambda a, b: w_in[l][:, a:b].rearrange("(kc p) f -> p kc f", p=128), INW, wsm)
            cast_load(woa, lambda a, b: woa[:, :, a:b], lambda a, b: w_oa[l][:, a:b].rearrange("(kc p) f -> p kc f", p=128), D, wsm)
            cast_load(wob, lambda a, b: wob[:, :, a:b], lambda a, b: w_ob[l][:, a:b].rearrange("(kc p) f -> p kc f", p=128), D, wsm)
            cast_load(wo, lambda a, b: wo[:, :, a:b], lambda a, b: w_o[l][:, a:b].rearrange("(kc p) f -> p kc f", p=128), D, wsm)
            cast_load(wr, lambda a, b: wr[:, :, a:b], lambda a, b: w_router[l][:, a:b].rearrange("(kc p) f -> p kc f", p=128), NEXP, wsm)
            Vop(lambda e: e.memset(ones33[:], 1.0), [], [ones33])
            with ExitStack() as sq:
                bst = sb("bst", [33, INW + NEXP], F32, sq)
                b33 = sb("b33", [33, INW + NEXP], BF16, sq)
                Vop(lambda e: e.memset(bst[:], 0.0), [], [bst])
                OP("sync", lambda e: [e.dma_start(out=bst[r:r + 1, 0:INW], in_=b_in[l:l + 1, :]) for r in (0, 32)]
                   + [e.dma_start(out=bst[r:r + 1, INW:INW + NEXP], in_=b_router[l:l + 1, :]) for r in (0, 32)],
                   r=[], w=[bst], sem=csm, ndma=4)
                Vop(lambda e: e.tensor_copy(out=b33[:], in_=bst[:]), [bst], [b33])
                Vop(lambda e: e.tensor_sub(out=bst[:], in0=bst[:], in1=b33[:]), [bst, b33], [bst])
                Vop(lambda e: e.tensor_copy(out=b33[32:33, :], in_=bst[32:33, :]), [bst], [b33])
                Vop(lambda e: e.tensor_copy(out=bin33[:], in_=b33[:, 0:INW]), [b33], [bin33])
                Vop(lambda e: e.tensor_copy(out=br33[:], in_=b33[:, INW:INW + NEXP]), [b33], [br33])
            P.barrier()
            SC2 = sb("SC2", [128, D], F32, sp); SH2 = sb("SH2", [128, D], F32, sp); G1 = sb("G1", [128, D], F32, sp)
            l1g = sb("l1g", [128, D], F32, sp); l1b = sb("l1b", [128, D], F32, sp)
            cw = sb("cw", [128, 1536], F32, sp); sinkb = sb("sinkb", [128, 8], F32, sp)
            SC1c = sb("SC1c", [128, 8], F32, sp); SH1c = sb("SH1c", [128, 8], F32, sp)
            OP("sync", lambda e: [e.dma_start(out=l1g[:], in_=ln1_g[l:l + 1, :].partition_broadcast(128)),
                                  e.dma_start(out=l1b[:], in_=ln1_b[l:l + 1, :].partition_broadcast(128)),
                                  e.dma_start(out=cw[:], in_=conv_w[l:l + 1, :].partition_broadcast(128)),
                                  e.dma_start(out=sinkb[:], in_=sinks[l:l + 1, :].partition_broadcast(128))],
               w=[l1g, l1b, cw, sinkb], sem=csm, ndma=4)

            def load_mods(g):
                def f(e):
                    r = [e.dma_start(out=SC2[:], in_=MODS[l, g:g + 1, 4 * D:5 * D].partition_broadcast(128)),
                         e.dma_start(out=SH2[:], in_=MODS[l, g:g + 1, 3 * D:4 * D].partition_broadcast(128)),
                         e.dma_start(out=G1[:], in_=MODS[l, g:g + 1, 2 * D:3 * D].partition_broadcast(128))]
                    with nc.allow_non_contiguous_dma(reason="tiny per-partition mod vectors"):
                        r.append(e.dma_start(out=SC1c[:], in_=MODS[l, g, D:2 * D].rearrange("(kc p) -> p kc", p=128)))
                        r.append(e.dma_start(out=SH1c[:], in_=MODS[l, g, 0:D].rearrange("(kc p) -> p kc", p=128)))
                    return r
                OP("sync", f, r=[dMODS], w=[SC2, SH2, G1, SC1c, SH1c], sem=csm, ndma=5)
            load_mods(0)

            SC1s = sb("SC1s", [128, 8], F32, sp); SH1s = sb("SH1s", [128, 8], F32, sp)

            def ld_s(e):
                with nc.allow_non_contiguous_dma(reason="tiny per-partition mod vectors"):
                    return [e.dma_start(out=SC1s[:], in_=MODS[l, 1, D:2 * D].rearrange("(kc p) -> p kc", p=128)),
                            e.dma_start(out=SH1s[:], in_=MODS[l, 1, 0:D].rearrange("(kc p) -> p kc", p=128))]
            OP("sync", ld_s, r=[dMODS], w=[SC1s, SH1s], sem=csm, ndma=2)
            xin2 = [sb(f"xin{j}", [128, D], F32, sp) for j in range(2)]
            xsem2 = [sem(f"mx{l}{j}") for j in range(2)]; xosem = sem(f"mxo{l}")
            hT2 = [sb(f"hT{j}", [128, 8, 128], BF16, sp) for j in range(2)]
            q_bf = sb("q_bf", [128, 512], BF16, sp); kv_f = sb("kv_f", [128, 256], F32, sp)
            k_bf = sb("k_bf", [128, 128], BF16, sp)
            kT2 = sb("kT2", [128, 3, 128], BF16, sp); v2 = sb("v2", [128, 3, 128], BF16, sp)
            qT = sb("qT", [128, 4, 128], BF16, sp)
            S_sb = sb("S_sb", [128, 4, 256], F32, sp); Eb = sb("Eb", [128, 4, 256], BF16, sp)
            PT = sb("PT", [128, 4, 2, 128], BF16, sp)
            sm = sb("sm", [128, 16], F32, sp)
            rden = sb("rden", [128, 8], F32, sp)
            ya_bf = sb("ya_bf", [128, 512], BF16, sp); yaT = sb("yaT", [128, 4, 128], BF16, sp)
            cb = sb("cb", [128, 512], F32, sp); cc = sb("cc", [128, 512], F32, sp)
            u2 = sb("u2", [128, 3, 512], F32, sp); ush1 = sb("ush1", [128, 512], F32, sp); ush2 = sb("ush2", [128, 512], F32, sp)
            ct = sb("ct", [128, 512], F32, sp)
            yb_bf = sb("yb_bf", [128, 512], BF16, sp); ybT = sb("ybT", [128, 4, 128], BF16, sp)
            ta = sb("ta", [128, 512], F32, sp); t1 = sb("t1", [128, D], F32, sp)
            pre_bf = sb("pre_bf", [128, D], BF16, sp); preT = sb("preT", [128, 8, 128], BF16, sp)
            h2b = pre_bf; h2T = preT
            xo = sb("xo_m", [128, D], F32, sp)
            stt = sb("stt_m", [128, 12], F32, sp); mv = sb("mv_m", [128, 4], F32, sp)
            lg = sb("lg", [128, 32], F32, sp); top8 = sb("top8", [128, 8], F32, sp)
            rt = sb("rt", [128, 16], F32, sp); Mbf = sb("Mbf", [128, 32], BF16, sp)
            pos = sb("pos", [128, 32], F32, sp); oh = sb("oh", [128, 32], F32, sp); destf = sb("destf", [128, 4], F32, sp)
            ck_bf = sb("ck_bf", [128, 128], BF16, sp)
            ckf = kv_f
            shs = sem(f"msh{l}"); scs = [sem(f"msc{l}{k}") for k in range(4)]; kos = sem(f"mko{l}"); cks = sem(f"mck{l}")
            Vop(lambda e: e.memset(cnt[:], 0.0), [], [cnt])
            Vop(lambda e: e.memset(pre_bf[:], 0.0), [], [pre_bf])
            zsem = sem(f"mz{l}")
            NZ = 8
            rows = NSLOT // NZ
            OP("sync", lambda e: [e.dma_start(out=XS[z * rows:(z + 1) * rows, :].rearrange("(p r) d -> p r d", p=128),
                                              in_=pre_bf[:, None, :].to_broadcast([128, rows // 128, D])) for z in range(NZ)],
               r=[pre_bf], w=[dXS], sem=zsem, ndma=NZ)
            if l == 0:
                print("phase M sbuf remaining", nc.sbuf_bytes_remaining)

            def tinfo(i):
                T = 128 if i < NT else DEC
                return T, (i == NT), i % 3, (i - 1) % 3, i * 128

            ya2 = [ya_bf, sb("ya_bf1", [128, 512], BF16, sp)]
            yb2 = [yb_bf, sb("yb_bf1", [128, 512], BF16, sp)]
            ct2 = cc

            def zseg(si, bk, T, hT):
                c0, c1 = SEGS[si]

                def f(e):
                    for kc in range(8):
                        e.matmul(bank(bk)[:T, 0:c1 - c0], lhsT=hT[:, kc, :T], rhs=win[:, kc, c0:c1], start=(kc == 0), stop=False)
                    return e.matmul(bank(bk)[:T, 0:c1 - c0], lhsT=ones33[:, :T], rhs=bin33[:, c0:c1], start=False, stop=True)
                Top(f, [hT, win, ones33, bin33], [BK[bk]])

            def load_x(i):
                T, samp, cur, prv, r0 = tinfo(i)
                xin = xin2[i % 2]
                OP("sync", lambda e: e.dma_start(out=xin[:T], in_=X[r0:r0 + T, :]), r=[dXt[i]], w=[xin], sem=xsem2[i % 2])

            def steps1(i):
                T, samp, cur, prv, r0 = tinfo(i)
                xin = xin2[i % 2]; hT = hT2[i % 2]
                ya_o = ya2[i % 2]; yb_o = yb2[i % 2]
                sc_, sh_ = (SC1s, SH1s) if samp else (SC1c, SH1c)
                k0 = 128 if i == 0 else 0
                k1 = 128 + T
                has_prev = (i > 0)
                if samp:
                    prv = (i + 1) % 3
                S = []

                def a0():
                    def xT_f(e):
                        last = None
                        for kc in range(8):
                            last = e.transpose(PS[0][:, kc * 128:kc * 128 + T], xin[:T, kc * 128:(kc + 1) * 128], ident_f[:T, :T])
                        return last
                    Top(xT_f, [xin, ident_f], [BK[0], BK[1]])
                    for kc in range(8):
                        Aop(lambda e, kc=kc: e.activation(out=hT[:, kc, :T], in_=PS[0][:, kc * 128:kc * 128 + T], func=AF.Identity,
                                                          bias=sh_[:, kc:kc + 1], scale=sc_[:, kc:kc + 1]),
                            [BK[0], BK[1], sc_, sh_], [hT])
                S.append(a0)

                def a1():
                    zseg(0, 0, T, hT)
                    Vop(lambda e: e.tensor_copy(out=q_bf[:T, :].rearrange("t (g kv d) -> t kv g d", g=4, kv=2, d=64),
                                                in_=bank(0)[:T, :].rearrange("t (kv g d) -> t kv g d", kv=2, g=4, d=64)),
                        [BK[0]], [q_bf])
                S.append(a1)

                def a2():
                    zseg(1, 1, T, hT)
                    Vop(lambda e: e.tensor_copy(out=kv_f[:T, :], in_=bank(1)[:T, 0:256]), [BK[1]], [kv_f])
                    Aop(lambda e: e.copy(out=k_bf[:T, :], in_=kv_f[:T, 0:128]), [kv_f], [k_bf])
                    Aop(lambda e: e.copy(out=v2[:T, cur, :], in_=kv_f[:T, 128:256]), [kv_f], [v2])
                    if i == NT - 1:
                        OP("sync", lambda e: [e.dma_start(out=nk_p[l], in_=kv_f[:, 0:128]), e.dma_start(out=nv_p[l], in_=kv_f[:, 128:256])],
                           r=[kv_f], sem=kos, ndma=2)
                    if samp:
                        OP("sync", lambda e: [e.dma_start(out=nk_s[l, 64:128, :], in_=kv_f[:64, 0:128]),
                                              e.dma_start(out=nv_s[l, 64:128, :], in_=kv_f[:64, 128:256])],
                           r=[kv_f], sem=kos, ndma=2)
                S.append(a2)

                def a3():
                    zseg(2, 0, T, hT)
                    Vop(lambda e: e.tensor_copy(out=cb[:T, :], in_=bank(0)[:T, :]), [BK[0]], [cb])
                S.append(a3)

                def a4():
                    zseg(3, 1, T, hT)
                    Aop(lambda e: e.copy(out=cc[:T, :], in_=bank(1)[:T, :]), [BK[1]], [cc])
                S.append(a4)

                def a5():
                    zseg(4, 0, T, hT)
                    Vop(lambda e: e.tensor_tensor(out=u2[:T, cur, :], in0=bank(0)[:T, :], in1=cc[:T, :], op=ALU.mult), [BK[0], cc], [u2])
                    if i == 0:
                        Vop(lambda e: e.memset(ush1[0:1, :], 0.0), [], [ush1])
                        Vop(lambda e: e.memset(ush2[0:2, :], 0.0), [], [ush2])
                        hist_f = None
                    elif samp:
                        hist_f = lambda e: [e.dma_start(out=ush1[0:1, :], in_=sconv[l, 1:2, :]), e.dma_start(out=ush2[0:2, :], in_=sconv[l, 0:2, :])]
                    else:
                        pu = (i - 1) % 3
                        hist_f = lambda e: [e.dma_start(out=ush1[0:1, :], in_=u2[127:128, pu, :]),
                                            e.dma_start(out=ush2[0:2, :], in_=u2[126:128, pu, :])]

                    def shf(e):
                        r = shift_dma(e, ush1, lambda a, b: u2[a:b, cur, :], 1, T) + shift_dma(e, ush2, lambda a, b: u2[a:b, cur, :], 2, T)
                        if hist_f is not None:
                            r += hist_f(e)
                        return r
                    OP("sync", shf, r=[u2], w=[ush1, ush2], sem=shs, ndma=4 + (0 if hist_f is None else 2))
                    if i == NT - 1:
                        OP("sync", lambda e: e.dma_start(out=nu_p[l], in_=u2[126:128, cur, :]), r=[u2], sem=kos)
                    if samp:
                        OP("sync", lambda e: e.dma_start(out=nu_s[l], in_=u2[62:64, cur, :]), r=[u2], sem=kos)
                S.append(a5)

                def b0():
                    def qkT(e):
                        for g in range(4):
                            e.transpose(bank_bf(4)[:, g * 128:g * 128 + T], q_bf[:T, g * 128:(g + 1) * 128], ident_b[:T, :T])
                        return e.transpose(bank_bf(4)[:, 512:512 + T], k_bf[:T, :], ident_b[:T, :T])
                    Top(qkT, [q_bf, k_bf, ident_b], [BK[4]])
                    Vop(lambda e: e.tensor_copy(out=qT[:, :, :T], in_=bank_bf(4)[:, 0:512].rearrange("p (g t) -> p g t", g=4)[:, :, :T]),
                        [BK[4]], [qT])
                    Aop(lambda e: e.copy(out=kT2[:, cur, :T], in_=bank_bf(4)[:, 512:512 + T]), [BK[4]], [kT2])
                    if samp:
                        ck32 = lambda: S_sb[:, 0, :]
                        OP("sync", lambda e: [e.dma_start(out=ck32()[:, 0:128], in_=ck[l]), e.dma_start(out=S_sb[:, 1, 0:128], in_=cv[l])],
                           w=[S_sb], sem=cks, ndma=2)
                        Vop(lambda e: e.tensor_copy(out=ck_bf[:], in_=S_sb[:, 0, 0:128]), [S_sb], [ck_bf])
                        Vop(lambda e: e.tensor_copy(out=v2[:, prv, :], in_=S_sb[:, 1, 0:128]), [S_sb], [v2])
                        Top(lambda e: e.transpose(bank_bf(4)[:, 0:128], ck_bf[:], ident_b[:]), [ck_bf, ident_b], [BK[4]])
                        Vop(lambda e: e.tensor_copy(out=kT2[:, prv, :], in_=bank_bf(4)[:, 0:128]), [BK[4]], [kT2])
                        OP("sync", lambda e: [e.dma_start(out=nk_s[l, 0:64, :], in_=S_sb[64:128, 0, 0:128]),
                                              e.dma_start(out=nv_s[l, 0:64, :], in_=S_sb[64:128, 1, 0:128])],
                           r=[S_sb], sem=kos, ndma=2)
                S.append(b0)

                for kvh in range(2):
                    pr = slice(kvh * 64, (kvh + 1) * 64)

                    def b_s(kvh=kvh, pr=pr):
                        def smm(e):
                            last = None
                            for g in range(4):
                                if has_prev:
                                    e.matmul(PS[1][:T, g * 256:g * 256 + 128], lhsT=qT[pr, g, :T], rhs=kT2[pr, prv, :], start=True, stop=True)
                                last = e.matmul(PS[1][:T, g * 256 + 128:g * 256 + 128 + T], lhsT=qT[pr, g, :T], rhs=kT2[pr, cur, :T],
                                                start=True, stop=True)
                            return last
                        Top(smm, [qT, kT2], [BK[2], BK[3]])
                        S3 = lambda: PS[1][:, :].rearrange("q (g k) -> q g k", g=4)
                        Vop(lambda e: e.scalar_tensor_tensor(
                            out=S_sb[:T, :, k0:k1], in0=S3()[:T, :, k0:k1], scalar=0.125, in1=biasT[:T, kvh * 4:kvh * 4 + 4, k0:k1],
                            op0=ALU.mult, op1=ALU.add), [BK[2], BK[3], biasT], [S_sb])
                    S.append(b_s)

                    def b_m(kvh=kvh):
                        Vop(lambda e: e.reduce_max(out=sm[:T, 0:4], in_=S_sb[:T, :, k0:k1], axis=AX.X), [S_sb], [sm])
                        Vop(lambda e: e.tensor_tensor(out=sm[:T, 0:4], in0=sm[:T, 0:4], in1=sinkb[:T, kvh * 4:kvh * 4 + 4], op=ALU.max),
                            [sm, sinkb], [sm])
                        Vop(lambda e: e.tensor_scalar_mul(out=sm[:T, 4:8], in0=sm[:T, 0:4], scalar1=-1.0), [sm], [sm])
                        for g in range(4):
                            Aop(lambda e, g=g: e.activation(out=Eb[:T, g, k0:k1], in_=S_sb[:T, g, k0:k1], func=AF.Exp,
                                                            bias=sm[:T, 4 + g:5 + g], scale=1.0, accum_out=sm[:T, 8 + g:9 + g]),
                                [S_sb, sm], [Eb, sm])
                    S.append(b_m)

                    def b_d(kvh=kvh):
                        Vop(lambda e: e.tensor_tensor(out=sm[:T, 12:16], in0=sinkb[:T, kvh * 4:kvh * 4 + 4], in1=sm[:T, 4:8], op=ALU.add),
                            [sm, sinkb], [sm])
                        Aop(lambda e: e.activation(out=sm[:T, 12:16], in_=sm[:T, 12:16], func=AF.Exp), [sm], [sm])
                        Vop(lambda e: e.tensor_tensor(out=sm[:T, 12:16], in0=sm[:T, 12:16], in1=sm[:T, 8:12], op=ALU.add), [sm], [sm])
                        Vop(lambda e: e.reciprocal(out=rden[:T, kvh * 4:kvh * 4 + 4], in_=sm[:T, 12:16]), [sm], [rden])
                    S.append(b_d)

                    def b_p(kvh=kvh):
                        def ptT(e):
                            last = None
                            for g in range(4):
                                if has_prev:
                                    e.transpose(bank_bf(4)[:, (g * 2) * 128:(g * 2) * 128 + T], Eb[:T, g, 0:128], ident_b[:T, :T])
                                last = e.transpose(bank_bf(4)[:T, (g * 2 + 1) * 128:(g * 2 + 1) * 128 + T], Eb[:T, g, 128:128 + T], ident_b[:T, :T])
                            return last
                        Top(ptT, [Eb, ident_b], [BK[4]])
                        if has_prev:
                            Aop(lambda e: e.copy(out=PT[:, :, 0, :T],
                                                 in_=bank_bf(4)[:, :].rearrange("p (g b t) -> p g b t", g=4, b=2)[:, :, 0, :T]), [BK[4]], [PT])
                        Vop(lambda e: e.tensor_copy(out=PT[:T, :, 1, :T],
                                                    in_=bank_bf(4)[:, :].rearrange("p (g b t) -> p g b t", g=4, b=2)[:T, :, 1, :T]), [BK[4]], [PT])
                    S.append(b_p)

                    def b_o(kvh=kvh):
                        def omm(e):
                            last = None
                            for g in range(4):
                                h = kvh * 4 + g
                                o = bank(5)[:T, h * 64:(h + 1) * 64]
                                if has_prev:
                                    e.matmul(o, lhsT=PT[:, g, 0, :T], rhs=v2[:, prv, kvh * 64:(kvh + 1) * 64], start=True, stop=False)
                                last = e.matmul(o, lhsT=PT[:T, g, 1, :T], rhs=v2[:T, cur, kvh * 64:(kvh + 1) * 64], start=(not has_prev), stop=True)
                            return last
                        Top(omm, [PT, v2], [BK[5]])
                    S.append(b_o)
                    if kvh == 0:
                        def b_c1():
                            Vop(lambda e: e.tensor_tensor(out=ct[:T, :], in0=u2[:T, cur, :], in1=cw[:T, 1024:1536], op=ALU.mult), [u2, cw], [ct])
                            Vop(lambda e: e.tensor_tensor(out=ct2[:T, :], in0=ush1[:T, :], in1=cw[:T, 512:1024], op=ALU.mult), [ush1, cw], [ct2])
                            Vop(lambda e: e.tensor_tensor(out=ct[:T, :], in0=ct[:T, :], in1=ct2[:T, :], op=ALU.add), [ct, ct2], [ct])
                        S.append(b_c1)

                        def b_c2():
                            Vop(lambda e: e.tensor_tensor(out=ct2[:T, :], in0=ush2[:T, :], in1=cw[:T, 0:512], op=ALU.mult), [ush2, cw], [ct2])
                            Vop(lambda e: e.tensor_tensor(out=ct[:T, :], in0=ct[:T, :], in1=ct2[:T, :], op=ALU.add), [ct, ct2], [ct])
                            Vop(lambda e: e.tensor_tensor(out=yb_o[:T, :], in0=ct[:T, :], in1=cb[:T, :], op=ALU.mult), [ct, cb], [yb_o])
                        S.append(b_c2)

                def b_n():
                    Vop(lambda e: e.tensor_tensor(out=ya_o[:T, :].rearrange("t (h d) -> t h d", h=8),
                                                  in0=bank(5)[:T, :].rearrange("t (h d) -> t h d", h=8),
                                                  in1=rden[:T, :].unsqueeze(2).to_broadcast([T, 8, 64]), op=ALU.mult), [BK[5], rden], [ya_o])
                S.append(b_n)
                return S

            def steps2(i):
                T, samp, cur, prv, r0 = tinfo(i)
                xin = xin2[i % 2]; hT = hT2[i % 2]
                ya_i = ya2[i % 2]; yb_i = yb2[i % 2]
                S = []

                def tr(src, nch):
                    def f(e):
                        last = None
                        for c in range(nch):
                            last = e.transpose(bank_bf(6)[:, c * 128:c * 128 + T], src[:T, c * 128:(c + 1) * 128], ident_b[:T, :T])
                        return last
                    return f

                def proj(srcT, w, nkc):
                    def f(e):
                        last = None
                        for n in range(2):
                            for c in range(nkc):
                                last = e.matmul(PS[3][:T, n * 512:(n + 1) * 512], lhsT=srcT[:, c, :T], rhs=w[:, c, n * 512:(n + 1) * 512],
                                                start=(c == 0), stop=(c == nkc - 1))
                        return last
                    return f

                def c0():
                    if samp:
                        load_mods(1)
                    Top(tr(ya_i, 4), [ya_i, ident_b], [BK[6]])
                    Aop(lambda e: e.copy(out=yaT[:, :, :T], in_=bank_bf(6)[:, 0:512].rearrange("p (c t) -> p c t", c=4)[:, :, :T]), [BK[6]], [yaT])
                S.append(c0)

                def c1():
                    Top(tr(yb_i, 4), [yb_i, ident_b], [BK[6]])
                    Vop(lambda e: e.tensor_copy(out=ybT[:, :, :T], in_=bank_bf(6)[:, 0:512].rearrange("p (c t) -> p c t", c=4)[:, :, :T]), [BK[6]], [ybT])
                S.append(c1)
                S.append(lambda: Top(proj(yaT, woa, 4), [yaT, woa], [BK[6], BK[7]]))
                for n in range(2):
                    def c2(n=n):
                        zseg(5 + n, n, T, hT)
                        Aop(lambda e: e.activation(out=ta[:T, :], in_=bank(n)[:T, :], func=AF.Tanh, scale=0.5), [BK[n]], [ta])
                        Vop(lambda e: e.scalar_tensor_tensor(out=t1[:T, n * 512:(n + 1) * 512], in0=ta[:T, :], scalar=1.0,
                                                             in1=PS[3][:T, n * 512:(n + 1) * 512], op0=ALU.add, op1=ALU.mult),
                            [ta, BK[6 + n]], [t1])
                    S.append(c2)
                S.append(lambda: Top(proj(ybT, wob, 4), [ybT, wob], [BK[6], BK[7]]))
                for n in range(2):
                    def c3(n=n):
                        zseg(7 + n, n, T, hT)
                        Aop(lambda e: e.activation(out=ta[:T, :], in_=bank(n)[:T, :], func=AF.Tanh, scale=0.5), [BK[n]], [ta])
                        Vop(lambda e: e.scalar_tensor_tensor(out=ta[:T, :], in0=ta[:T, :], scalar=1.0,
                                                             in1=PS[3][:T, n * 512:(n + 1) * 512], op0=ALU.add, op1=ALU.mult),
                            [ta, BK[6 + n]], [ta])
                        Vop(lambda e: e.tensor_tensor(out=pre_bf[:T, n * 512:(n + 1) * 512], in0=ta[:T, :],
                                                      in1=t1[:T, n * 512:(n + 1) * 512], op=ALU.add), [ta, t1], [pre_bf])
                    S.append(c3)

                def c4():
                    Top(tr(pre_bf, 8), [pre_bf, ident_b], [BK[6]])
                    Aop(lambda e: e.copy(out=preT[:, :, :T], in_=bank_bf(6)[:, :].rearrange("p (c t) -> p c t", c=8)[:, :, :T]), [BK[6]], [preT])
                S.append(c4)

                def c5():
                    Top(proj(preT, wo, 8), [preT, wo], [BK[6], BK[7]])
                    Vop(lambda e: e.tensor_tensor(out=t1[:T, :], in0=PS[3][:T, :], in1=G1[:T, :], op=ALU.mult), [BK[6], BK[7], G1], [t1])
                S.append(c5)
                S.append(lambda: Vop(lambda e: e.scalar_tensor_tensor(out=xin[:T, :], in0=xin[:T, :], scalar=ALPHA, in1=t1[:T, :],
                                                                      op0=ALU.mult, op1=ALU.add), [xin, t1], [xin]))
                S.append(lambda: [Vop(lambda e, c=c: e.bn_stats(out=stt[:T, c * 6:(c + 1) * 6], in_=xin[:T, c * 512:(c + 1) * 512]), [xin], [stt])
                                  for c in range(2)])

                def c6():
                    Vop(lambda e: e.bn_aggr(out=mv[:T, 0:2], in_=stt[:T, :]), [stt], [mv])
                    Vop(lambda e: e.tensor_scalar_add(out=mv[:T, 2:3], in0=mv[:T, 1:2], scalar1=LN_EPS), [mv], [mv])
                    Gop(lambda e: e.tensor_tensor(out=mv[:T, 2:3], in0=mv[:T, 2:3], in1=negh[:T, :], op=ALU.pow), [mv, negh], [mv])
                    Vop(lambda e: e.scalar_tensor_tensor(out=mv[:T, 3:4], in0=mv[:T, 0:1], scalar=-1.0, in1=mv[:T, 2:3],
                                                         op0=ALU.mult, op1=ALU.mult), [mv], [mv])
                    Aop(lambda e: e.activation(out=xo[:T, :], in_=xin[:T, :], func=AF.Identity, bias=mv[:T, 3:4], scale=mv[:T, 2:3]),
                        [xin, mv], [xo])
                S.append(c6)
                S.append(lambda: Vop(lambda e: e.tensor_tensor(out=xo[:T, :], in0=xo[:T, :], in1=l1g[:T, :], op=ALU.mult), [xo, l1g], [xo]))

                def c7():
                    Vop(lambda e: e.tensor_tensor(out=xo[:T, :], in0=xo[:T, :], in1=l1b[:T, :], op=ALU.add), [xo, l1b], [xo])
                    OP("sync", lambda e: e.dma_start(out=X[r0:r0 + T, :], in_=xo[:T]), r=[xo], w=[dXt[i]], sem=xosem)
                S.append(c7)
                S.append(lambda: Vop(lambda e: e.tensor_tensor(out=t1[:T, :], in0=xo[:T, :], in1=SC2[:T, :], op=ALU.mult), [xo, SC2], [t1]))
                S.append(lambda: Vop(lambda e: e.tensor_tensor(out=h2b[:T, :], in0=t1[:T, :], in1=SH2[:T, :], op=ALU.add), [t1, SH2], [h2b]))

                def c8():
                    Top(tr(h2b, 8), [h2b, ident_b], [BK[6]])
                    Vop(lambda e: e.tensor_copy(out=h2T[:, :, :T], in_=bank_bf(6)[:, :].rearrange("p (c t) -> p c t", c=8)[:, :, :T]), [BK[6]], [h2T])
                S.append(c8)

                def c9():
                    def rmm(e):
                        for kc in range(8):
                            e.matmul(bank(6)[:T, 0:32], lhsT=h2T[:, kc, :T], rhs=wr[:, kc, :], start=(kc == 0), stop=False)
                        return e.matmul(bank(6)[:T, 0:32], lhsT=ones33[:, :T], rhs=br33[:, :], start=False, stop=True)
                    Top(rmm, [h2T, wr, ones33, br33], [BK[6]])
                    Vop(lambda e: e.tensor_copy(out=lg[:T, :], in_=bank(6)[:T, 0:32]), [BK[6]], [lg])
                    Vop(lambda e: e.max(out=top8[:T, :], in_=lg[:T, :]), [lg], [top8])
                    Vop(lambda e: e.tensor_scalar_mul(out=rt[:T, 0:1], in0=top8[:T, 0:1], scalar1=-1.0), [top8], [rt])
                    Aop(lambda e: e.activation(out=rt[:T, 4:8], in_=top8[:T, 0:4], func=AF.Exp, bias=rt[:T, 0:1], scale=1.0,
                                               accum_out=rt[:T, 1:2]), [top8, rt], [rt])
                    Vop(lambda e: e.tensor_scalar(out=Mbf[:T, :], in0=lg[:T, :], scalar1=top8[:T, 3:4], scalar2=None, op0=ALU.is_ge),
                        [lg, top8], [Mbf])
                S.append(c9)

                def c10():
                    def pmm(e):
                        e.matmul(bank(6)[:T, 64:96], lhsT=ustrict[:T, :T], rhs=Mbf[:T, :], start=True, stop=True)
                        return e.matmul(bank(6)[:, 128:160], lhsT=ones_b[:T, :], rhs=Mbf[:T, :], start=True, stop=True)
                    Top(pmm, [ustrict, ones_b, Mbf], [BK[6]])
                    Vop(lambda e: e.reciprocal(out=rt[:T, 2:3], in_=rt[:T, 1:2]), [rt], [rt])
                    Vop(lambda e: e.tensor_tensor(out=pos[:T, :], in0=bank(6)[:T, 64:96], in1=cnt[:T, :], op=ALU.add), [BK[6], cnt], [pos])
                    Vop(lambda e: e.tensor_tensor(out=cnt[:, :], in0=bank(6)[:, 128:160], in1=cnt[:, :], op=ALU.add), [BK[6], cnt], [cnt])
                S.append(c10)

                def c11():
                    Vop(lambda e: e.tensor_scalar(out=oh[:T, :], in0=pos[:T, :], scalar1=float(C), scalar2=BIG, op0=ALU.is_ge, op1=ALU.mult),
                        [pos], [oh])
                    Vop(lambda e: e.tensor_tensor(out=pos[:T, :], in0=pos[:T, :], in1=oh[:T, :], op=ALU.add), [pos, oh], [pos])
                    Vop(lambda e: e.tensor_tensor(out=pos[:T, :], in0=pos[:T, :], in1=iotac[:T, :], op=ALU.add), [pos, iotac], [pos])
                S.append(c11)
                for k in range(4):
                    def c12(k=k):
                        Vop(lambda e: e.tensor_scalar(out=oh[:T, :], in0=lg[:T, :], scalar1=top8[:T, k:k + 1], scalar2=None, op0=ALU.is_equal),
                            [lg, top8], [oh])
                        Vop(lambda e: e.tensor_tensor(out=oh[:T, :], in0=oh[:T, :], in1=pos[:T, :], op=ALU.mult), [oh, pos], [oh])
                        Vop(lambda e: e.reduce_sum(out=destf[:T, k:k + 1], in_=oh[:T, :], axis=AX.X), [oh], [destf])
                    S.append(c12)

                def c13():
                    Vop(lambda e: e.tensor_copy(out=DEST[:T, i, :], in_=destf[:T, :]), [destf], [DEST])
                    Vop(lambda e: e.tensor_scalar(out=rt[:T, 8:12], in0=destf[:T, :], scalar1=float(NSLOT), scalar2=None, op0=ALU.is_lt), [destf], [rt])
                    Vop(lambda e: e.tensor_scalar(out=rt[:T, 4:8], in0=rt[:T, 4:8], scalar1=rt[:T, 2:3], scalar2=None, op0=ALU.mult), [rt], [rt])
                    Vop(lambda e: e.tensor_tensor(out=GATES[:T, i, :], in0=rt[:T, 4:8], in1=rt[:T, 8:12], op=ALU.mult), [rt], [GATES])
                    for k in range(4):
                        OP("gpsimd", lambda e, k=k: e.indirect_dma_start(
                            out=XS, out_offset=bass.IndirectOffsetOnAxis(ap=DEST[:T, i, k:k + 1], axis=0), in_=h2b[:T, :], in_offset=None,
                            bounds_check=bcreg(e), oob_is_err=False), r=[h2b, DEST, dXS], sem=scs[k])
                S.append(c13)
                return S

            def run_merged(A, B):
                na, nb = len(A), len(B)
                ia = ib = 0
                while ia < na or ib < nb:
                    if ib >= nb or (ia < na and ia * nb <= ib * na):
                        A[ia](); ia += 1
                    else:
                        B[ib](); ib += 1

            load_x(0)
            run_merged(steps1(0), [])
            for i in range(NTILE):
                if i + 1 < NTILE:
                    load_x(i + 1)
                    run_merged(steps2(i), steps1(i + 1))
                else:
                    run_merged(steps2(i), [])

    def phase_experts(l):
        with ExitStack() as sp:
            wgu = [sb(f"wgu{j}", [128, 8, 2 * DFF], BF16, sp) for j in range(2)]
            wdn = [sb(f"wdn{j}", [128, 8, D], BF16, sp) for j in range(2)]
            xT = [sb(f"xT{j}", [128, 8, C], BF16, sp) for j in range(2)]
            bdst = [sb(f"bdst{j}", [33, D], F32, sp) for j in range(2)]
            bd33 = [sb(f"bd33{j}", [33, D], BF16, sp) for j in range(2)]
            aT = sb("aT", [128, 8, C], BF16, sp)
            ones33 = sb("ones33e", [33, 128], BF16, sp)
            bgu_r = sb("bgu_r", [32, 2 * DFF], F32, sp)
            bguT = sb("bguT", [128, 16, 32], F32, sp)
            gc = [sb(f"gc{j}", [128, C], F32, sp) for j in range(2)]
            sg = [sb(f"sg{j}", [128, C], F32, sp) for j in range(2)]
            l1 = [sb(f"l1{j}", [128, C], F32, sp) for j in range(2)]
            ysb = [sb(f"ysb{j}", [128, D], F32, sp) for j in range(2)]
            wgs = [sem(f"ewg{l}{j}") for j in range(2)]; wds = [sem(f"ewd{l}{j}") for j in range(2)]
            xts = [sem(f"ext{l}{j}") for j in range(2)]; bds = [sem(f"ebd{l}{j}") for j in range(2)]
            yss = [sem(f"eys{l}{j}") for j in range(2)]; es0 = sem(f"es0{l}")
            Vop(lambda e: e.memset(ones33[:], 1.0), [], [ones33])
            if l == 0:
                print("phase E sbuf remaining", nc.sbuf_bytes_remaining)
            for j in range(2):
                Vop(lambda e, j=j: e.memset(bdst[j][:], 0.0), [], [bdst[j]])
            OP("sync", lambda e: e.dma_start(out=bgu_r[:], in_=b_gu[l]), w=[bgu_r], sem=es0)
            for half in range(2):
                def tb(e, half=half):
                    last = None
                    for c in range(8):
                        fc = half * 8 + c
                        last = e.transpose(PS[half][0:128, c * 32:(c + 1) * 32], bgu_r[:, fc * 128:(fc + 1) * 128], ident_f[:32, :32])
                    return last
                Top(tb, [bgu_r, ident_f], [BK[2 * half]])
                Vop(lambda e, half=half: e.tensor_copy(out=bguT[:, half * 8:(half + 1) * 8, :],
                                                       in_=PS[half][:, 0:256].rearrange("p (c e) -> p c e", c=8)), [BK[2 * half]], [bguT])

            def load_expert(ex):
                j = ex % 2
                cast_load(wgu[j], lambda a, b: wgu[j][:, :, a:b], lambda a, b: w_gu[l, ex][:, a:b].rearrange("(kc p) f -> p kc f", p=128), 2 * DFF, wgs[j])
                cast_load(wdn[j], lambda a, b: wdn[j][:, :, a:b], lambda a, b: w_down[l, ex][:, a:b].rearrange("(kc p) f -> p kc f", p=128), D, wds[j])
                OP("sync", lambda e: [e.dma_start(out=bdst[j][r:r + 1, :], in_=b_down[l, ex:ex + 1, :]) for r in (0, 32)],
                   w=[bdst[j]], sem=bds[j], ndma=2)
                OP("sync", lambda e: e.dma_start_transpose(out=xT[j][:], in_=XS[ex * C:(ex + 1) * C, :]), r=[dXS], w=[xT[j]], sem=xts[j])

            load_expert(0)
            ydma = 0
            for ex in range(NEXP):
                j = ex % 2
                if ex + 1 < NEXP:
                    load_expert(ex + 1)
                Vop(lambda e, j=j: e.tensor_scalar_mul(out=bdst[j][:], in0=bdst[j][:], scalar1=2.0), [bdst[j]], [bdst[j]])
                Vop(lambda e, j=j: e.tensor_copy(out=bd33[j][:], in_=bdst[j][:]), [bdst[j]], [bd33[j]])
                Vop(lambda e, j=j: e.tensor_sub(out=bdst[j][:], in0=bdst[j][:], in1=bd33[j][:]), [bdst[j], bd33[j]], [bdst[j]])
                Vop(lambda e, j=j: e.tensor_copy(out=bd33[j][32:33, :], in_=bdst[j][32:33, :]), [bdst[j]], [bd33[j]])
                chunks = [(c, min(c + 512, C)) for c in range(0, C, 512)]
                for pj in range(8):
                    s = pj % 2
                    pg, pl = PS[2 * s], PS[2 * s + 1]
                    bg_, bl_ = [BK[4 * s], BK[4 * s + 1]], [BK[4 * s + 2], BK[4 * s + 3]]

                    def gu(e, pj=pj, j=j, pg=pg, pl=pl):
                        last = None
                        for (dst, fc) in ((pg, pj), (pl, pj + 8)):
                            for (a, b) in chunks:
                                for kc in range(8):
                                    last = e.matmul(dst[:, a:b], lhsT=wgu[j][:, kc, fc * 128:(fc + 1) * 128], rhs=xT[j][:, kc, a:b],
                                                    start=(kc == 0), stop=(kc == 7))
                        return last
                    Top(gu, [wgu[j], xT[j]], bg_ + bl_)
                    Vop(lambda e, s=s, pg=pg, pj=pj, ex=ex: e.tensor_scalar(out=gc[s][:, :], in0=pg[:, 0:C], scalar1=bguT[:, pj, ex:ex + 1], scalar2=7.0,
                                                                           op0=ALU.add, op1=ALU.min), bg_ + [bguT], [gc[s]])
                    Aop(lambda e, s=s: e.activation(out=sg[s][:, :], in_=gc[s][:, :], func=AF.Tanh, scale=0.851), [gc[s]], [sg[s]])
                    Vop(lambda e, s=s, pl=pl, pj=pj, ex=ex: e.tensor_scalar(out=l1[s][:, :], in0=pl[:, 0:C], scalar1=bguT[:, pj + 8, ex:ex + 1], scalar2=-7.0,
                                                                           op0=ALU.add, op1=ALU.max), bl_ + [bguT], [l1[s]])
                    Vop(lambda e, s=s: e.tensor_scalar(out=l1[s][:, :], in0=l1[s][:, :], scalar1=7.0, scalar2=1.0, op0=ALU.min, op1=ALU.add), [l1[s]], [l1[s]])
                    Vop(lambda e, s=s: e.tensor_tensor(out=gc[s][:, :], in0=gc[s][:, :], in1=l1[s][:, :], op=ALU.mult), [gc[s], l1[s]], [gc[s]])
                    Vop(lambda e, s=s, pj=pj: e.scalar_tensor_tensor(out=aT[:, pj, :], in0=sg[s][:, :], scalar=1.0, in1=gc[s][:, :], op0=ALU.add, op1=ALU.mult),
                        [sg[s], gc[s]], [aT])
                for t in range(CT):
                    pd = t % 4
                    yj = ydma % 2
                    ydma += 1

                    def dn(e, t=t, j=j, pd=pd):
                        last = None
                        for n in range(2):
                            o = PS[pd][:, n * 512:(n + 1) * 512]
                            for fc in range(8):
                                e.matmul(o, lhsT=aT[:, fc, t * 128:(t + 1) * 128], rhs=wdn[j][:, fc, n * 512:(n + 1) * 512], start=(fc == 0), stop=False)
                            last = e.matmul(o, lhsT=ones33[:, :], rhs=bd33[j][:, n * 512:(n + 1) * 512], start=False, stop=True)
                        return last
                    Top(dn, [aT, wdn[j], ones33, bd33[j]], [BK[2 * pd], BK[2 * pd + 1]])
                    Aop(lambda e, pd=pd, yj=yj: e.activation(out=ysb[yj][:, :], in_=PS[pd][:, :], func=AF.Copy, scale=0.5), [BK[2 * pd], BK[2 * pd + 1]], [ysb[yj]])
                    OP("sync", lambda e, ex=ex, t=t, yj=yj: e.dma_start(out=YS[ex * C + t * 128:ex * C + (t + 1) * 128, :], in_=ysb[yj][:, :]),
                       r=[ysb[yj]], sem=yss[yj])

    def phase_combine(l):
        with ExitStack() as sp:
            G2 = sb("G2", [128, D], F32, sp); l2g = sb("l2g", [128, D], F32, sp); l2b = sb("l2b", [128, D], F32, sp)
            x1 = [sb(f"x1{j}", [128, D], F32, sp) for j in range(2)]
            x2b = [sb(f"x2b{j}", [128, D], F32, sp) for j in range(2)]
            xo = [sb(f"xoc{j}", [128, D], F32, sp) for j in range(2)]
            acc = sb("acc", [128, D], F32, sp)
            yg = [[sb(f"yg{j}{k}", [128, D], F32, sp) for k in range(4)] for j in range(2)]
            stt = [sb(f"stt_c{j}", [128, 12], F32, sp) for j in range(2)]; mv = [sb(f"mv_c{j}", [128, 4], F32, sp) for j in range(2)]
            cs = sem(f"cc{l}"); gs = [sem(f"cg{l}{j}") for j in range(2)]; xs_ = [sem(f"cx{l}{j}") for j in range(2)]
            os_ = [sem(f"co{l}{j}") for j in range(2)]
            OP("sync", lambda e: [e.dma_start(out=l2g[:], in_=ln2_g[l:l + 1, :].partition_broadcast(128)),
                                  e.dma_start(out=l2b[:], in_=ln2_b[l:l + 1, :].partition_broadcast(128))], w=[l2g, l2b], sem=cs, ndma=2)
            for j in range(2):
                for k in range(4):
                    Vop(lambda e, j=j, k=k: e.memset(yg[j][k][:], 0.0), [], [yg[j][k]])

            def load_g2(g):
                OP("sync", lambda e: e.dma_start(out=G2[:], in_=MODS[l, g:g + 1, 5 * D:6 * D].partition_broadcast(128)), r=[dMODS], w=[G2], sem=cs)
            load_g2(0)

            def tl(i):
                return 128 if i < NT else DEC

            def prefetch(i):
                T = tl(i)
                j = i % 2
                r0 = i * 128
                OP("sync", lambda e: e.dma_start(out=x1[j][:T], in_=X[r0:r0 + T, :]), r=[dXt[i]], w=[x1[j]], sem=xs_[j])
                OP("gpsimd", lambda e: [e.indirect_dma_start(
                    out=yg[j][k][:, :], out_offset=None, in_=YS, in_offset=bass.IndirectOffsetOnAxis(ap=DEST[:, i, k:k + 1], axis=0),
                    bounds_check=bcreg(e), oob_is_err=False) for k in range(4)], r=[dYS, DEST], w=yg[j], sem=gs[j], ndma=4)

            def S1(i):
                T = tl(i)
                j = i % 2
                if i == NT:
                    load_g2(1)
                Vop(lambda e: e.tensor_scalar(out=acc[:T, :], in0=yg[j][0][:T, :], scalar1=GATES[:T, i, 0:1], scalar2=None, op0=ALU.mult),
                    [yg[j][0], GATES], [acc])
                for k in range(1, 4):
                    Vop(lambda e, k=k: e.scalar_tensor_tensor(out=acc[:T, :], in0=yg[j][k][:T, :], scalar=GATES[:T, i, k:k + 1],
                                                              in1=acc[:T, :], op0=ALU.mult, op1=ALU.add), [yg[j][k], GATES, acc], [acc])
                Vop(lambda e: e.tensor_tensor(out=acc[:T, :], in0=acc[:T, :], in1=G2[:T, :], op=ALU.mult), [acc, G2], [acc])
                Vop(lambda e: e.scalar_tensor_tensor(out=x2b[j][:T, :], in0=x1[j][:T, :], scalar=ALPHA, in1=acc[:T, :], op0=ALU.mult, op1=ALU.add),
                    [x1[j], acc], [x2b[j]])
                ln_stats(x2b[j], stt[j], mv[j], T)

            def S2(i):
                T = tl(i)
                j = i % 2
                r0 = i * 128
                ln_apply(x2b[j], xo[j], mv[j], l2g, l2b, T)
                if l == DEPTH - 1:
                    dst = y_p[r0:r0 + T, :] if i < NT else y_s
                    OP("sync", lambda e: e.dma_start(out=dst, in_=xo[j][:T]), r=[xo[j]], sem=os_[j])
                else:
                    OP("sync", lambda e: e.dma_start(out=X[r0:r0 + T, :], in_=xo[j][:T]), r=[xo[j]], w=[dXt[i]], sem=os_[j])
            prefetch(0)
            if NTILE > 1:
                prefetch(1)
            S1(0)
            for i in range(NTILE):
                if i + 2 < NTILE:
                    prefetch(i + 2)
                if i + 1 < NTILE:
                    S1(i + 1)
                S2(i)

    for l in range(DEPTH):
        phase_mix(l)
        OP("sync", lambda e, l=l: e.dma_start(out=cnt_out[l], in_=cnt[:]), r=[cnt], sem=dsem)
        P.barrier()
        if dbg and l == 0:
            OP("sync", lambda e: [e.dma_start(out=dbgX1, in_=X), e.dma_start(out=dbgD, in_=DEST[:].rearrange("p i k -> p (i k)")),
                                  e.dma_start(out=dbgG, in_=GATES[:].rearrange("p i k -> p (i k)")), e.dma_start(out=dbgXS, in_=XS)], sem=dsem, ndma=4)
            P.barrier()
        phase_experts(l)
        P.barrier()
        if dbg and l == 0:
            OP("sync", lambda e: [e.dma_start(out=dbgYS, in_=YS)], sem=dsem, ndma=1)
            P.barrier()
        phase_combine(l)
        P.barrier()

    block = E(nc.Block())
    P.emit(block, esems)
    st.close()
    return nc


_CACHE = {}
CAP = 768


def make_in_maps(inputs, ncores, NT, C):
    cst = host_consts(C)
    f = lambda a: np.ascontiguousarray(np.asarray(a, dtype=np.float32))
    shared = {
        "rel_table": f(inputs["rel_table"]), "ln0_g": f(inputs["ln0_g"]).reshape(1, D), "ln0_b": f(inputs["ln0_b"]).reshape(1, D),
        "w_ada": f(inputs["w_ada"]), "b_ada": f(inputs["b_ada"]), "w_in": f(inputs["w_in"]), "b_in": f(inputs["b_in"]),
        "sinks": f(inputs["sinks"]), "conv_w": f(inputs["conv_w"]).reshape(DEPTH, 3 * 512),
        "w_oa": f(inputs["w_oa"]), "w_ob": f(inputs["w_ob"]), "w_o": f(inputs["w_o"]),
        "ln1_g": f(inputs["ln1_g"]), "ln1_b": f(inputs["ln1_b"]), "w_router": f(inputs["w_router"]), "b_router": f(inputs["b_router"]),
        "w_gu": f(inputs["w_gu"]), "b_gu": f(inputs["b_gu"]), "w_down": f(inputs["w_down"]), "b_down": f(inputs["b_down"]),
        "ln2_g": f(inputs["ln2_g"]), "ln2_b": f(inputs["ln2_b"]),
    }
    shared.update(cst)
    maps = []
    for c in range(ncores):
        m = dict(shared)
        m["xp"] = f(inputs["x_prompt"][c][:NT * 128])
        m["xsm"] = f(inputs["x_sample"][c])
        m["cvec"] = np.stack([f(inputs["c_prompt"][c]), f(inputs["c_sample"][c])], 0)
        m["ck"] = f(inputs["cache_k"][:, c]).reshape(DEPTH, 128, 128)
        m["cv"] = f(inputs["cache_v"][:, c]).reshape(DEPTH, 128, 128)
        m["sconv"] = f(inputs["state_conv"][:, c])
        maps.append(m)
    return maps


def kernel(**inputs):
    NT = SEQ // 128
    key = (NT, CAP)
    if key not in _CACHE:
        _CACHE[key] = build(NT, CAP)
    nc = _CACHE[key]
    maps = make_in_maps(inputs, NCORES, NT, CAP)
    res = run_bass_kernel_spmd(nc, maps, core_ids=list(range(NCORES)))
    R = res.results
    try:
        cm = np.stack([np.asarray(R[c]["cnt_out"], dtype=np.float32)[:, 0, :] for c in range(NCORES)], 0)
        print("[kernel] expert load per (core, layer): max=%d min=%d cap=%d" % (cm.max(), cm.min(), CAP), flush=True)
        print("[kernel] per core/layer max:", cm.max(axis=2).astype(int).tolist(), flush=True)
    except Exception as ex:
        print("[kernel] cnt diag failed", ex)
    g = lambda name: np.stack([np.asarray(R[c][name], dtype=np.float32) for c in range(NCORES)], 0)
    y_prompt = g("y_p")
    y_sample = g("y_s")

    def kvo(name):
        a = g(name)
        return np.ascontiguousarray(a.transpose(1, 0, 2, 3)).reshape(DEPTH, NCORES, 128, 2, 64)

    def uo(name):
        return np.ascontiguousarray(g(name).transpose(1, 0, 2, 3))
    return (y_prompt, y_sample, kvo("nk_p"), kvo("nv_p"), uo("nu_p"), kvo("nk_s"), kvo("nv_s"), uo("nu_s"))
```

```python
import math
from contextlib import ExitStack

import numpy as np
import concourse.bass as bass
import concourse.mybir as mybir
from concourse.bass_utils import run_bass_kernel_spmd

F32 = mybir.dt.float32
BF16 = mybir.dt.bfloat16
I32 = mybir.dt.int32
U32 = mybir.dt.uint32
AF = mybir.ActivationFunctionType
ALU = mybir.AluOpType
AX = mybir.AxisListType

D = 1024
DEPTH = 2
NEXP = 32
DFF = 1024
INW = 4352
LN_EPS = 1e-5
ALPHA = float((2 * DEPTH) ** 0.25)
NCORES = 8
SEQ = 4096
DEC = 64

ENGINES = ("tensor", "vector", "scalar", "gpsimd", "sync")


class Sem:
    def __init__(self, handle, name):
        self.h = handle
        self.name = name
        self.count = 0


class Buf:
    __slots__ = ("name", "w", "r")

    def __init__(self, name):
        self.name = name
        self.w = {}
        self.r = {}


class Op:
    __slots__ = ("eng", "fn", "deps", "seq", "kind", "sem", "ndma", "val")


class Prog:
    def __init__(self, nc, same_engine_sync=True):
        self.nc = nc
        self.ops = {e: [] for e in ENGINES}
        self.nseq = {e: 0 for e in ENGINES}
        self.same_engine_sync = same_engine_sync
        self.milestones = {e: set() for e in ENGINES}
        self.all_ops = []
        self.ALL = Buf("ALL")

    @staticmethod
    def _merge(dst, key, tok, hard):
        old = dst.get(key)
        if old is None:
            dst[key] = (tok, hard)
        else:
            t = tok if tok[2] > old[0][2] else old[0]
            dst[key] = (t, hard or old[1])

    def op(self, eng, fn, reads=(), writes=(), dma_sem=None, ndma=1, barrier=False):
        o = Op()
        o.eng = eng
        o.fn = fn
        o.kind = 'd' if dma_sem is not None else 'c'
        o.sem = dma_sem
        o.ndma = ndma
        reads = list(reads)
        writes = list(writes)
        if barrier:
            writes.append(self.ALL)
        else:
            reads.append(self.ALL)
        deps = {}
        for b in reads:
            for k, t in b.w.items():
                self._merge(deps, k, t, True)
        for b in writes:
            for k, t in b.w.items():
                self._merge(deps, k, t, True)
            for k, t in b.r.items():
                self._merge(deps, k, t, b is not self.ALL)
        if dma_sem is not None and dma_sem.count > 0:
            self._merge(deps, ('d', id(dma_sem)), ('d', dma_sem, dma_sem.count), True)
        o.deps = deps
        o.seq = self.nseq[eng]
        self.nseq[eng] += 1
        if o.kind == 'd':
            dma_sem.count += 16 * ndma
            o.val = dma_sem.count
            tok = ('d', dma_sem, o.val)
            key = ('d', id(dma_sem))
        else:
            tok = ('c', eng, o.seq)
            key = ('c', eng)
        for b in writes:
            b.w = {key: tok}
            b.r = {}
        for b in reads:
            if b in writes:
                continue
            old = b.r.get(key)
            if old is None or old[2] < tok[2]:
                b.r[key] = tok
        self.ops[eng].append(o)
        self.all_ops.append(o)
        return o

    def barrier(self):
        for e in ENGINES:
            self.op(e, lambda en: None, barrier=True)

    def _skip(self, o, tok, hard):
        e2 = tok[1]
        if e2 != o.eng:
            return False
        if e2 == "tensor":
            return True
        if e2 == "sync":
            return True
        return (not hard) or (not self.same_engine_sync)

    def emit(self, block, esems):
        for o in self.all_ops:
            for key, (tok, hard) in o.deps.items():
                if tok[0] != 'c' or self._skip(o, tok, hard):
                    continue
                self.milestones[tok[1]].add(tok[2])
        rank = {e: {s: i + 1 for i, s in enumerate(sorted(st))} for e, st in self.milestones.items()}

        def make(eng):
            def body(e):
                waited = {}
                for o in self.ops[eng]:
                    for key, (tok, hard) in o.deps.items():
                        if tok[0] == 'c':
                            if self._skip(o, tok, hard):
                                continue
                            semh = esems[tok[1]]
                            val = rank[tok[1]][tok[2]]
                            wk = ('c', tok[1])
                        else:
                            semh = tok[1].h
                            val = tok[2]
                            wk = ('d', id(tok[1]))
                        if waited.get(wk, 0) >= val:
                            continue
                        waited[wk] = val
                        e.wait_ge(semh, val)
                    res = o.fn(e)
                    if o.kind == 'd':
                        if not isinstance(res, (list, tuple)):
                            res = [res]
                        assert len(res) == o.ndma, (len(res), o.ndma)
                        for ins in res:
                            ins.then_inc(o.sem.h, 16)
                    elif o.seq in self.milestones[eng]:
                        if isinstance(res, (list, tuple)):
                            res = res[-1]
                        if res is None:
                            res = e.nop()
                        res.then_inc(esems[eng], 1)
            return body

        for eng in ENGINES:
            if self.ops[eng]:
                getattr(block, eng)(make(eng))


class TB:
    def __init__(self, t, name):
        self.t = t
        self.b = Buf(name)

    def __getitem__(self, k):
        return self.t[k]


def _rel_bucket_np(rel):
    half = 16
    max_exact = 8
    n = np.abs(rel)
    n_f = np.maximum(n, 1).astype(np.float32)
    large = max_exact + (np.log(n_f / np.float32(max_exact)) / np.float32(math.log(128 / max_exact))
                         * np.float32(half - max_exact)).astype(np.int32)
    large = np.minimum(large, half - 1)
    return np.where(rel > 0, half, 0) + np.where(n < max_exact, n, large)


def host_consts(C):
    q = np.arange(128)[:, None]
    kk = np.arange(256)[None, :]
    rel = kk - 128 - q
    bk = _rel_bucket_np(rel)
    oht = np.zeros((32, 256, 128), np.float32)
    for b in range(32):
        oht[b] = (bk == b).T.astype(np.float32)
    valid = np.where(q < 64, kk < 192, kk >= 64)
    mask = np.where(valid, 0.0, -1e30).astype(np.float32)
    ident = np.eye(128, dtype=np.float32)
    ustrict = (np.arange(128)[:, None] < np.arange(128)[None, :]).astype(np.float32)
    iotac = np.tile((np.arange(32) * C).astype(np.float32)[None, :], (128, 1))
    return {"c_oht": oht.reshape(32, 256 * 128), "c_mask": mask, "c_ident": ident,
            "c_ustrict": ustrict, "c_iotac": iotac}


def build(NT, C, dbg=False):
    TP = NT * 128
    TT = TP + DEC
    NTILE = NT + 1
    NSLOT = NEXP * C
    CT = C // 128
    BIG = 1.0e6
    assert C % 128 == 0

    nc = bass.Bass("TRN2", target_bir_lowering=False)

    def din(name, shape, dt=F32):
        return nc.dram_tensor(name, list(shape), dt, kind="ExternalInput").ap()

    def dout(name, shape, dt=F32):
        return nc.dram_tensor(name, list(shape), dt, kind="ExternalOutput").ap()

    def dscr(name, shape, dt=F32):
        return nc.dram_tensor(name, list(shape), dt, kind="Internal").ap()

    xp = din("xp", [TP, D]); xsm = din("xsm", [DEC, D]); cvec = din("cvec", [2, D])
    ck = din("ck", [DEPTH, 128, 128]); cv = din("cv", [DEPTH, 128, 128]); sconv = din("sconv", [DEPTH, 2, 512])
    rel_table = din("rel_table", [32, 8]); ln0_g = din("ln0_g", [1, D]); ln0_b = din("ln0_b", [1, D])
    w_ada = din("w_ada", [DEPTH, D, 6 * D]); b_ada = din("b_ada", [DEPTH, 6 * D])
    w_in = din("w_in", [DEPTH, D, INW]); b_in = din("b_in", [DEPTH, INW])
    sinks = din("sinks", [DEPTH, 8]); conv_w = din("conv_w", [DEPTH, 3 * 512])
    w_oa = din("w_oa", [DEPTH, 512, D]); w_ob = din("w_ob", [DEPTH, 512, D]); w_o = din("w_o", [DEPTH, D, D])
    ln1_g = din("ln1_g", [DEPTH, D]); ln1_b = din("ln1_b", [DEPTH, D])
    w_router = din("w_router", [DEPTH, D, NEXP]); b_router = din("b_router", [DEPTH, NEXP])
    w_gu = din("w_gu", [DEPTH, NEXP, D, 2 * DFF]); b_gu = din("b_gu", [DEPTH, NEXP, 2 * DFF])
    w_down = din("w_down", [DEPTH, NEXP, DFF, D]); b_down = din("b_down", [DEPTH, NEXP, D])
    ln2_g = din("ln2_g", [DEPTH, D]); ln2_b = din("ln2_b", [DEPTH, D])
    c_oht = din("c_oht", [32, 256 * 128]); c_mask = din("c_mask", [128, 256]); c_ident = din("c_ident", [128, 128])
    c_ustrict = din("c_ustrict", [128, 128]); c_iotac = din("c_iotac", [128, 32])

    y_p = dout("y_p", [TP, D]); y_s = dout("y_s", [DEC, D])
    nk_p = dout("nk_p", [DEPTH, 128, 128]); nv_p = dout("nv_p", [DEPTH, 128, 128]); nu_p = dout("nu_p", [DEPTH, 2, 512])
    cnt_out = dout("cnt_out", [DEPTH, 128, 32])
    nk_s = dout("nk_s", [DEPTH, 128, 128]); nv_s = dout("nv_s", [DEPTH, 128, 128]); nu_s = dout("nu_s", [DEPTH, 2, 512])

    if dbg:
        dbgX0 = dout("dbgX0", [TT, D]); dbgM = dout("dbgM", [DEPTH, 2, 6 * D]); dbgX1 = dout("dbgX1", [TT, D]); dbgB = dout("dbgB", [128, 8 * 256])
        dbgD = dout("dbgD", [128, NTILE * 4], I32); dbgG = dout("dbgG", [128, NTILE * 4]); dbgYS = dout("dbgYS", [NSLOT, D]); dbgXS = dout("dbgXS", [NSLOT, D], BF16)
    X = dscr("X", [TT, D]); XS = dscr("XS", [NSLOT, D], BF16); YS = dscr("YS", [NSLOT, D])
    MODS = dscr("MODS", [DEPTH, 2, 6 * D])

    st = ExitStack()
    E = st.enter_context
    nsem = [0]

    def sem(name):
        nsem[0] += 1
        return Sem(E(nc.semaphore(name)), name)

    esems = {e: E(nc.semaphore("es_" + e)) for e in ENGINES}
    P = Prog(nc)

    uniq = [0]

    def sb(name, shape, dt=F32, stack=None):
        uniq[0] += 1
        t = (stack or st).enter_context(nc.sbuf_tensor(f"{name}_{uniq[0]}", list(shape), dt))
        return TB(t, name)


    _bc = {}

    def bcreg(e):
        if "r" not in _bc:
            r = e.alloc_register("bcreg")
            e.reg_mov(r, NSLOT - 1)
            _bc["r"] = r
        return _bc["r"]

    def rw(lst):
        return [x.b for x in lst]

    def OP(eng, fn, r=(), w=(), sem=None, ndma=1):
        return P.op(eng, fn, reads=rw(r), writes=rw(w), dma_sem=sem, ndma=ndma)

    def Vop(fn, r=(), w=()): return OP("vector", fn, r, w)
    def Aop(fn, r=(), w=()): return OP("scalar", fn, r, w)
    def Gop(fn, r=(), w=()): return OP("gpsimd", fn, r, w)
    def Top(fn, r=(), w=()): return OP("tensor", fn, r, w)

    dXt = [TB(None, f"dX{i}") for i in range(NTILE)]; dXS = TB(XS, "dXS"); dYS = TB(YS, "dYS"); dMODS = TB(MODS, "dMODS")

    PS = [E(nc.psum_tensor(f"ps{i}", [128, 1024], F32)) for i in range(4)]
    BK = [TB(PS[k // 2][:, (k % 2) * 512:(k % 2 + 1) * 512], f"bank{k}") for k in range(8)]

    def bank(k):
        return PS[k // 2][:, (k % 2) * 512:(k % 2 + 1) * 512]

    def bank_bf(k):
        return bank(k).bitcast(BF16)

    ident_f = sb("ident_f", [128, 128]); ident_b = sb("ident_b", [128, 128], BF16)
    ustrict = sb("ustrict", [128, 128], BF16); ones_b = sb("ones_b", [128, 128], BF16)
    iotac = sb("iotac", [128, 32]); negh = sb("negh", [128, 1])
    biasT = sb("biasT", [128, 8, 256])
    DEST = sb("DEST", [128, NTILE, 4], I32); GATES = sb("GATES", [128, NTILE, 4])
    cnt = sb("cnt", [128, 32])
    cld = sem("cld")

    def ld_const():
        OP("sync", lambda e: [e.dma_start(out=ident_f[:], in_=c_ident), e.dma_start(out=iotac[:], in_=c_iotac)],
           w=[ident_f, iotac], sem=cld, ndma=2)
        Vop(lambda e: e.tensor_copy(out=ident_b[:], in_=ident_f[:]), [ident_f], [ident_b])
        Vop(lambda e: e.memset(ones_b[:], 1.0), [], [ones_b])
        Vop(lambda e: e.memset(negh[:], -0.5), [], [negh])
        Vop(lambda e: e.memset(DEST[:, :, :], 1000000), [], [DEST])
    ld_const()


    def ln_tile(xin, xout, stt, mv, g_t, b_t, T):
        for c in range(2):
            Vop(lambda e, c=c: e.bn_stats(out=stt[:T, c * 6:(c + 1) * 6], in_=xin[:T, c * 512:(c + 1) * 512]), [xin], [stt])
        Vop(lambda e: e.bn_aggr(out=mv[:T, 0:2], in_=stt[:T, :]), [stt], [mv])
        Vop(lambda e: e.tensor_scalar_add(out=mv[:T, 2:3], in0=mv[:T, 1:2], scalar1=LN_EPS), [mv], [mv])
        Gop(lambda e: e.tensor_tensor(out=mv[:T, 2:3], in0=mv[:T, 2:3], in1=negh[:T, :], op=ALU.pow), [mv, negh], [mv])
        Vop(lambda e: e.scalar_tensor_tensor(out=mv[:T, 3:4], in0=mv[:T, 0:1], scalar=-1.0, in1=mv[:T, 2:3],
                                             op0=ALU.mult, op1=ALU.mult), [mv], [mv])
        Aop(lambda e: e.activation(out=xout[:T, :], in_=xin[:T, :], func=AF.Identity, bias=mv[:T, 3:4], scale=mv[:T, 2:3]),
            [xin, mv], [xout])
        Vop(lambda e: e.tensor_tensor(out=xout[:T, :], in0=xout[:T, :], in1=g_t[:T, :], op=ALU.mult), [xout, g_t], [xout])
        Vop(lambda e: e.tensor_tensor(out=xout[:T, :], in0=xout[:T, :], in1=b_t[:T, :], op=ALU.add), [xout, b_t], [xout])

    def ln_stats(xin, stt, mv, T):
        for c in range(2):
            Vop(lambda e, c=c: e.bn_stats(out=stt[:T, c * 6:(c + 1) * 6], in_=xin[:T, c * 512:(c + 1) * 512]), [xin], [stt])
        Vop(lambda e: e.bn_aggr(out=mv[:T, 0:2], in_=stt[:T, :]), [stt], [mv])
        Vop(lambda e: e.tensor_scalar_add(out=mv[:T, 2:3], in0=mv[:T, 1:2], scalar1=LN_EPS), [mv], [mv])
        Gop(lambda e: e.tensor_tensor(out=mv[:T, 2:3], in0=mv[:T, 2:3], in1=negh[:T, :], op=ALU.pow), [mv, negh], [mv])
        Vop(lambda e: e.scalar_tensor_tensor(out=mv[:T, 3:4], in0=mv[:T, 0:1], scalar=-1.0, in1=mv[:T, 2:3],
                                             op0=ALU.mult, op1=ALU.mult), [mv], [mv])

    def ln_apply(xin, xout, mv, g_t, b_t, T):
        Aop(lambda e: e.activation(out=xout[:T, :], in_=xin[:T, :], func=AF.Identity, bias=mv[:T, 3:4], scale=mv[:T, 2:3]),
            [xin, mv], [xout])
        Vop(lambda e: e.tensor_tensor(out=xout[:T, :], in0=xout[:T, :], in1=g_t[:T, :], op=ALU.mult), [xout, g_t], [xout])
        Vop(lambda e: e.tensor_tensor(out=xout[:T, :], in0=xout[:T, :], in1=b_t[:T, :], op=ALU.add), [xout, b_t], [xout])

    def cast_load(dst, dst_fn, src_fn, ncols, sem_, reads=()):
        pcs = [(c, min(c + 2048, ncols)) for c in range(0, ncols, 2048)]
        OP("gpsimd", lambda e: [e.dma_start(out=dst_fn(a, b), in_=src_fn(a, b)) for (a, b) in pcs],
           r=list(reads), w=[dst], sem=sem_, ndma=len(pcs))

    with ExitStack() as s0:
        oht = sb("oht", [32, 256 * 128], BF16, s0)
        mask_sb = sb("mask_sb", [128, 256], F32, s0)
        OP("sync", lambda e: e.dma_start(out=mask_sb[:], in_=c_mask), w=[mask_sb], sem=cld)
        ust_f = sb("ust_f", [128, 128], F32, s0)
        tab = sb("tab", [32, 8], F32, s0); tab_hi = sb("tab_hi", [32, 8], BF16, s0)
        tab_d = sb("tab_d", [32, 8], F32, s0); tab_lo = sb("tab_lo", [32, 8], BF16, s0)
        s_a = sem("setup_a"); s_b = sem("setup_b")
        cast_load(oht, lambda a, b: oht[:, a:b], lambda a, b: c_oht[:, a:b], 256 * 128, s_a)
        OP("sync", lambda e: [e.dma_start(out=tab[:], in_=rel_table), e.dma_start(out=ust_f[:], in_=c_ustrict)],
           w=[tab, ust_f], sem=s_b, ndma=2)
        Vop(lambda e: e.tensor_copy(out=ustrict[:], in_=ust_f[:]), [ust_f], [ustrict])
        Vop(lambda e: e.tensor_copy(out=tab_hi[:], in_=tab[:]), [tab], [tab_hi])
        Vop(lambda e: e.tensor_sub(out=tab_d[:], in0=tab[:], in1=tab_hi[:]), [tab, tab_hi], [tab_d])
        Vop(lambda e: e.tensor_copy(out=tab_lo[:], in_=tab_d[:]), [tab_d], [tab_lo])
        for half in range(2):
            bks = [BK[4 * half + j] for j in range(2)]

            def mm_bias(e, half=half):
                last = None
                for kl in range(128):
                    kk = half * 128 + kl
                    o = PS[2 * half][:, kl * 8:(kl + 1) * 8]
                    e.matmul(o, lhsT=oht[:, kk * 128:(kk + 1) * 128], rhs=tab_hi[:], start=True, stop=False)
                    last = e.matmul(o, lhsT=oht[:, kk * 128:(kk + 1) * 128], rhs=tab_lo[:], start=False, stop=True)
                return last
            Top(mm_bias, [oht, tab_hi, tab_lo], bks)
            for h in range(8):
                def ev(e, half=half, h=h):
                    src = PS[2 * half][:, :].rearrange("q (kk h) -> q h kk", h=8)
                    return e.tensor_tensor(out=biasT[:, h, half * 128:(half + 1) * 128], in0=src[:, h, :],
                                           in1=mask_sb[:, half * 128:(half + 1) * 128], op=ALU.add)
                Vop(ev, bks + [mask_sb], [biasT])

    P.barrier()
    with ExitStack() as s0:
        cT = sb("cT", [128, 8, 2], F32, s0); cth = sb("cth", [128, 8, 2], F32, s0)
        scT = sb("scT", [128, 8, 2], BF16, s0)
        wada = [sb(f"wada{j}", [128, 8, 2048], BF16, s0) for j in range(2)]
        wsem = [sem(f"wada{j}") for j in range(2)]
        bada = sb("bada", [2, 6 * D], F32, s0)
        modsb = sb("modsb", [2, 6 * D], F32, s0)
        s_c = sem("setup_c"); s_m = sem("setup_m")

        def ld_c(e):
            with nc.allow_non_contiguous_dma(reason="tiny transposed load of c"):
                return [e.dma_start(out=cT[:, :, g], in_=cvec[g].rearrange("(kc p) -> p kc", p=128)) for g in range(2)]
        OP("sync", ld_c, w=[cT], sem=s_c, ndma=2)
        Aop(lambda e: e.activation(out=cth[:], in_=cT[:], func=AF.Tanh, scale=0.5), [cT], [cth])
        Vop(lambda e: e.tensor_scalar(out=cth[:], in0=cth[:], scalar1=0.5, scalar2=0.5, op0=ALU.mult, op1=ALU.add), [cth], [cth])
        Vop(lambda e: e.tensor_tensor(out=scT[:], in0=cth[:], in1=cT[:], op=ALU.mult), [cth, cT], [scT])
        it = 0
        for l in range(DEPTH):
            OP("sync", lambda e, l=l: [e.dma_start(out=bada[g:g + 1, :], in_=b_ada[l:l + 1, :]) for g in range(2)],
               w=[bada], sem=s_c, ndma=2)
            for pc in range(3):
                j = it % 2
                it += 1
                OP("gpsimd", lambda e, l=l, pc=pc, j=j: e.dma_start(
                    out=wada[j][:], in_=w_ada[l][:, pc * 2048:(pc + 1) * 2048].rearrange("(kc p) f -> p kc f", p=128)),
                   w=[wada[j]], sem=wsem[j])
                for n in range(4):
                    bk = BK[n % 2]

                    def mm(e, j=j, n=n):
                        last = None
                        for kc in range(8):
                            last = e.matmul(bank(n % 2)[0:2, :], lhsT=scT[:, kc, :], rhs=wada[j][:, kc, n * 512:(n + 1) * 512],
                                            start=(kc == 0), stop=(kc == 7))
                        return last
                    Top(mm, [scT, wada[j]], [bk])
                    c0 = pc * 2048 + n * 512
                    Vop(lambda e, n=n, c0=c0: e.tensor_tensor(out=modsb[:, c0:c0 + 512], in0=bank(n % 2)[0:2, :],
                                                              in1=bada[:, c0:c0 + 512], op=ALU.add), [bk, bada], [modsb])
            Vop(lambda e: e.tensor_scalar_add(out=modsb[:, D:2 * D], in0=modsb[:, D:2 * D], scalar1=1.0), [modsb], [modsb])
            Vop(lambda e: e.tensor_scalar(out=modsb[:, 2 * D:3 * D], in0=modsb[:, 2 * D:3 * D], scalar1=1.0, scalar2=0.5,
                                          op0=ALU.add, op1=ALU.mult), [modsb], [modsb])
            Vop(lambda e: e.tensor_scalar_add(out=modsb[:, 4 * D:6 * D], in0=modsb[:, 4 * D:6 * D], scalar1=1.0), [modsb], [modsb])
            OP("sync", lambda e, l=l: e.dma_start(out=MODS[l], in_=modsb[:]), r=[modsb], w=[dMODS], sem=s_m)

    P.barrier()
    with ExitStack() as s0:
        s_c = sem("setup_c2")
        g0 = sb("g0", [128, D], F32, s0); b0 = sb("b0", [128, D], F32, s0)
        OP("sync", lambda e: [e.dma_start(out=g0[:], in_=ln0_g.partition_broadcast(128)),
                              e.dma_start(out=b0[:], in_=ln0_b.partition_broadcast(128))], w=[g0, b0], sem=s_c, ndma=2)
        xi = [sb(f"xi{j}", [128, D], F32, s0) for j in range(3)]
        xo = [sb(f"xo{j}", [128, D], F32, s0) for j in range(2)]
        xis = [sem(f"xi{j}") for j in range(3)]; xos = [sem(f"xo{j}") for j in range(2)]
        stt = [sb(f"stt{j}", [128, 12], F32, s0) for j in range(2)]
        mv = [sb(f"mv{j}", [128, 4], F32, s0) for j in range(2)]

        def tl(i):
            return 128 if i < NT else DEC

        def ld0(i):
            T = tl(i)
            src = xp[i * 128:(i + 1) * 128, :] if i < NT else xsm
            OP("sync", lambda e: e.dma_start(out=xi[i % 3][:T], in_=src), w=[xi[i % 3]], sem=xis[i % 3])
        ld0(0)
        if NTILE > 1:
            ld0(1)
        ln_stats(xi[0], stt[0], mv[0], tl(0))
        for i in range(NTILE):
            T = tl(i)
            if i + 2 < NTILE:
                ld0(i + 2)
            if i + 1 < NTILE:
                ln_stats(xi[(i + 1) % 3], stt[(i + 1) % 2], mv[(i + 1) % 2], tl(i + 1))
            ln_apply(xi[i % 3], xo[i % 2], mv[i % 2], g0, b0, T)
            OP("sync", lambda e, T=T, i=i: e.dma_start(out=X[i * 128:i * 128 + T, :], in_=xo[i % 2][:T]),
               r=[xo[i % 2]], w=[dXt[i]], sem=xos[i % 2])
    P.barrier()


    dsem = sem("dbg")
    if dbg:
        OP("sync", lambda e: [e.dma_start(out=dbgX0, in_=X), e.dma_start(out=dbgM, in_=MODS), e.dma_start(out=dbgB, in_=biasT[:].rearrange("p h k -> p (h k)"))], sem=dsem, ndma=3)
        P.barrier()
    SEGS = [(0, 512), (512, 768), (768, 1280), (1280, 1792), (1792, 2304),
            (2304, 2816), (2816, 3328), (3328, 3840), (3840, 4352)]

    def shift_dma(e, dst, src_fn, sh, T):
        n = T - sh
        a = (n // 16) * 16
        if a == n:
            a -= 16
        return [e.dma_start(out=dst[sh:sh + a, :], in_=src_fn(0, a)),
                e.dma_start(out=dst[sh + a:T, :], in_=src_fn(a, n))]

    def phase_mix(l):
        with ExitStack() as sp:
            win = sb("win", [128, 8, INW], BF16, sp)
            woa = sb("woa", [128, 4, D], BF16, sp); wob = sb("wob", [128, 4, D], BF16, sp)
            wo = sb("wo", [128, 8, D], BF16, sp); wr = sb("wr", [128, 8, NEXP], BF16, sp)
            bin33 = sb("bin33", [33, INW], BF16, sp); br33 = sb("br33", [33, NEXP], BF16, sp)
            ones33 = sb("ones33", [33, 128], BF16, sp)
            wsm = sem(f"mw{l}"); csm = sem(f"mc{l}")
            cast_load(win, lambda a, b: win[:, :, a:b], lambda a, b: w_in[l][:, a:b].rearrange("(kc p) f -> p kc f", p=128), INW, wsm)
            cast_load(woa, lambda a, b: woa[:, :, a:b], lambda a, b: w_oa[l][:, a:b].rearrange("(kc p) f -> p kc f", p=128), D, wsm)
            cast_load(wob, lambda a, b: wob[:, :, a:b], lambda a, b: w_ob[l][:, a:b].rearrange("(kc p) f -> p kc f", p=128), D, wsm)
            cast_load(wo, lambda a, b: wo[:, :, a:b], lambda a, b: w_o[l][:, a:b].rearrange("(kc p) f -> p kc f", p=128), D, wsm)
            cast_load(wr, lambda a, b: wr[:, :, a:b], lambda a, b: w_router[l][:, a:b].rearrange("(kc p) f -> p kc f", p=128), NEXP, wsm)
            Vop(lambda e: e.memset(ones33[:], 1.0), [], [ones33])
            with ExitStack() as sq:
                bst = sb("bst", [33, INW + NEXP], F32, sq)
                b33 = sb("b33", [33, INW + NEXP], BF16, sq)
                Vop(lambda e: e.memset(bst[:], 0.0), [], [bst])
                OP("sync", lambda e: [e.dma_start(out=bst[r:r + 1, 0:INW], in_=b_in[l:l + 1, :]) for r in (0, 32)]
                   + [e.dma_start(out=bst[r:r + 1, INW:INW + NEXP], in_=b_router[l:l + 1, :]) for r in (0, 32)],
                   r=[], w=[bst], sem=csm, ndma=4)
                Vop(lambda e: e.tensor_copy(out=b33[:], in_=bst[:]), [bst], [b33])
                Vop(lambda e: e.tensor_sub(out=bst[:], in0=bst[:], in1=b33[:]), [bst, b33], [bst])
                Vop(lambda e: e.tensor_copy(out=b33[32:33, :], in_=bst[32:33, :]), [bst], [b33])
                Vop(lambda e: e.tensor_copy(out=bin33[:], in_=b33[:, 0:INW]), [b33], [bin33])
                Vop(lambda e: e.tensor_copy(out=br33[:], in_=b33[:, INW:INW + NEXP]), [b33], [br33])
            P.barrier()
            SC2 = sb("SC2", [128, D], F32, sp); SH2 = sb("SH2", [128, D], F32, sp); G1 = sb("G1", [128, D], F32, sp)
            l1g = sb("l1g", [128, D], F32, sp); l1b = sb("l1b", [128, D], F32, sp)
            cw = sb("cw", [128, 1536], F32, sp); sinkb = sb("sinkb", [128, 8], F32, sp)
            SC1c = sb("SC1c", [128, 8], F32, sp); SH1c = sb("SH1c", [128, 8], F32, sp)
            OP("sync", lambda e: [e.dma_start(out=l1g[:], in_=ln1_g[l:l + 1, :].partition_broadcast(128)),
                                  e.dma_start(out=l1b[:], in_=ln1_b[l:l + 1, :].partition_broadcast(128)),
                                  e.dma_start(out=cw[:], in_=conv_w[l:l + 1, :].partition_broadcast(128)),
                                  e.dma_start(out=sinkb[:], in_=sinks[l:l + 1, :].partition_broadcast(128))],
               w=[l1g, l1b, cw, sinkb], sem=csm, ndma=4)

            def load_mods(g):
                def f(e):
                    r = [e.dma_start(out=SC2[:], in_=MODS[l, g:g + 1, 4 * D:5 * D].partition_broadcast(128)),
                         e.dma_start(out=SH2[:], in_=MODS[l, g:g + 1, 3 * D:4 * D].partition_broadcast(128)),
                         e.dma_start(out=G1[:], in_=MODS[l, g:g + 1, 2 * D:3 * D].partition_broadcast(128))]
                    with nc.allow_non_contiguous_dma(reason="tiny per-partition mod vectors"):
                        r.append(e.dma_start(out=SC1c[:], in_=MODS[l, g, D:2 * D].rearrange("(kc p) -> p kc", p=128)))
                        r.append(e.dma_start(out=SH1c[:], in_=MODS[l, g, 0:D].rearrange("(kc p) -> p kc", p=128)))
                    return r
                OP("sync", f, r=[dMODS], w=[SC2, SH2, G1, SC1c, SH1c], sem=csm, ndma=5)
            load_mods(0)

            SC1s = sb("SC1s", [128, 8], F32, sp); SH1s = sb("SH1s", [128, 8], F32, sp)

            def ld_s(e):
                with nc.allow_non_contiguous_dma(reason="tiny per-partition mod vectors"):
                    return [e.dma_start(out=SC1s[:], in_=MODS[l, 1, D:2 * D].rearrange("(kc p) -> p kc", p=128)),
                            e.dma_start(out=SH1s[:], in_=MODS[l, 1, 0:D].rearrange("(kc p) -> p kc", p=128))]
            OP("sync", ld_s, r=[dMODS], w=[SC1s, SH1s], sem=csm, ndma=2)
            xin2 = [sb(f"xin{j}", [128, D], F32, sp) for j in range(2)]
            xsem2 = [sem(f"mx{l}{j}") for j in range(2)]; xosem = sem(f"mxo{l}")
            hT2 = [sb(f"hT{j}", [128, 8, 128], BF16, sp) for j in range(2)]
            q_bf = sb("q_bf", [128, 512], BF16, sp); kv_f = sb("kv_f", [128, 256], F32, sp)
            k_bf = sb("k_bf", [128, 128], BF16, sp)
            kT2 = sb("kT2", [128, 3, 128], BF16, sp); v2 = sb("v2", [128, 3, 128], BF16, sp)
            qT = sb("qT", [128, 4, 128], BF16, sp)
            S_sb = sb("S_sb", [128, 4, 256], F32, sp); Eb = sb("Eb", [128, 4, 256], BF16, sp)
            PT = sb("PT", [128, 4, 2, 128], BF16, sp)
            sm = sb("sm", [128, 16], F32, sp)
            rden = sb("rden", [128, 8], F32, sp)
            ya_bf = sb("ya_bf", [128, 512], BF16, sp); yaT = sb("yaT", [128, 4, 128], BF16, sp)
            cb = sb("cb", [128, 512], F32, sp); cc = sb("cc", [128, 512], F32, sp)
            u2 = sb("u2", [128, 3, 512], F32, sp); ush1 = sb("ush1", [128, 512], F32, sp); ush2 = sb("ush2", [128, 512], F32, sp)
            ct = sb("ct", [128, 512], F32, sp)
            yb_bf = sb("yb_bf", [128, 512], BF16, sp); ybT = sb("ybT", [128, 4, 128], BF16, sp)
            ta = sb("ta", [128, 512], F32, sp); t1 = sb("t1", [128, D], F32, sp)
            pre_bf = sb("pre_bf", [128, D], BF16, sp); preT = sb("preT", [128, 8, 128], BF16, sp)
            h2b = pre_bf; h2T = preT
            xo = sb("xo_m", [128, D], F32, sp)
            stt = sb("stt_m", [128, 12], F32, sp); mv = sb("mv_m", [128, 4], F32, sp)
            lg = sb("lg", [128, 32], F32, sp); top8 = sb("top8", [128, 8], F32, sp)
            rt = sb("rt", [128, 16], F32, sp); Mbf = sb("Mbf", [128, 32], BF16, sp)
            pos = sb("pos", [128, 32], F32, sp); oh = sb("oh", [128, 32], F32, sp); destf = sb("destf", [128, 4], F32, sp)
            ck_bf = sb("ck_bf", [128, 128], BF16, sp)
            ckf = kv_f
            shs = sem(f"msh{l}"); scs = [sem(f"msc{l}{k}") for k in range(4)]; kos = sem(f"mko{l}"); cks = sem(f"mck{l}")
            Vop(lambda e: e.memset(cnt[:], 0.0), [], [cnt])
            Vop(lambda e: e.memset(pre_bf[:], 0.0), [], [pre_bf])
            zsem = sem(f"mz{l}")
            NZ = 8
            rows = NSLOT // NZ
            OP("sync", lambda e: [e.dma_start(out=XS[z * rows:(z + 1) * rows, :].rearrange("(p r) d -> p r d", p=128),
                                              in_=pre_bf[:, None, :].to_broadcast([128, rows // 128, D])) for z in range(NZ)],
               r=[pre_bf], w=[dXS], sem=zsem, ndma=NZ)
            if l == 0:
                print("phase M sbuf remaining", nc.sbuf_bytes_remaining)

            def tinfo(i):
                T = 128 if i < NT else DEC
                return T, (i == NT), i % 3, (i - 1) % 3, i * 128

            ya2 = [ya_bf, sb("ya_bf1", [128, 512], BF16, sp)]
            yb2 = [yb_bf, sb("yb_bf1", [128, 512], BF16, sp)]
            ct2 = cc

            def zseg(si, bk, T, hT):
                c0, c1 = SEGS[si]

                def f(e):
                    for kc in range(8):
                        e.matmul(bank(bk)[:T, 0:c1 - c0], lhsT=hT[:, kc, :T], rhs=win[:, kc, c0:c1], start=(kc == 0), stop=False)
                    return e.matmul(bank(bk)[:T, 0:c1 - c0], lhsT=ones33[:, :T], rhs=bin33[:, c0:c1], start=False, stop=True)
                Top(f, [hT, win, ones33, bin33], [BK[bk]])

            def load_x(i):
                T, samp, cur, prv, r0 = tinfo(i)
                xin = xin2[i % 2]
                OP("sync", lambda e: e.dma_start(out=xin[:T], in_=X[r0:r0 + T, :]), r=[dXt[i]], w=[xin], sem=xsem2[i % 2])

            def steps1(i):
                T, samp, cur, prv, r0 = tinfo(i)
                xin = xin2[i % 2]; hT = hT2[i % 2]
                ya_o = ya2[i % 2]; yb_o = yb2[i % 2]
                sc_, sh_ = (SC1s, SH1s) if samp else (SC1c, SH1c)
                k0 = 128 if i == 0 else 0
                k1 = 128 + T
                has_prev = (i > 0)
                if samp:
                    prv = (i + 1) % 3
                S = []

                def a0():
                    def xT_f(e):
                        last = None
                        for kc in range(8):
                            last = e.transpose(PS[0][:, kc * 128:kc * 128 + T], xin[:T, kc * 128:(kc + 1) * 128], ident_f[:T, :T])
                        return last
                    Top(xT_f, [xin, ident_f], [BK[0], BK[1]])
                    for kc in range(8):
                        Aop(lambda e, kc=kc: e.activation(out=hT[:, kc, :T], in_=PS[0][:, kc * 128:kc * 128 + T], func=AF.Identity,
                                                          bias=sh_[:, kc:kc + 1], scale=sc_[:, kc:kc + 1]),
                            [BK[0], BK[1], sc_, sh_], [hT])
                S.append(a0)

                def a1():
                    zseg(0, 0, T, hT)
                    Vop(lambda e: e.tensor_copy(out=q_bf[:T, :].rearrange("t (g kv d) -> t kv g d", g=4, kv=2, d=64),
                                                in_=bank(0)[:T, :].rearrange("t (kv g d) -> t kv g d", kv=2, g=4, d=64)),
                        [BK[0]], [q_bf])
                S.append(a1)

                def a2():
                    zseg(1, 1, T, hT)
                    Vop(lambda e: e.tensor_copy(out=kv_f[:T, :], in_=bank(1)[:T, 0:256]), [BK[1]], [kv_f])
                    Aop(lambda e: e.copy(out=k_bf[:T, :], in_=kv_f[:T, 0:128]), [kv_f], [k_bf])
                    Aop(lambda e: e.copy(out=v2[:T, cur, :], in_=kv_f[:T, 128:256]), [kv_f], [v2])
                    if i == NT - 1:
                        OP("sync", lambda e: [e.dma_start(out=nk_p[l], in_=kv_f[:, 0:128]), e.dma_start(out=nv_p[l], in_=kv_f[:, 128:256])],
                           r=[kv_f], sem=kos, ndma=2)
                    if samp:
                        OP("sync", lambda e: [e.dma_start(out=nk_s[l, 64:128, :], in_=kv_f[:64, 0:128]),
                                              e.dma_start(out=nv_s[l, 64:128, :], in_=kv_f[:64, 128:256])],
                           r=[kv_f], sem=kos, ndma=2)
                S.append(a2)

                def a3():
                    zseg(2, 0, T, hT)
                    Vop(lambda e: e.tensor_copy(out=cb[:T, :], in_=bank(0)[:T, :]), [BK[0]], [cb])
                S.append(a3)

                def a4():
                    zseg(3, 1, T, hT)
                    Aop(lambda e: e.copy(out=cc[:T, :], in_=bank(1)[:T, :]), [BK[1]], [cc])
                S.append(a4)

                def a5():
                    zseg(4, 0, T, hT)
                    Vop(lambda e: e.tensor_tensor(out=u2[:T, cur, :], in0=bank(0)[:T, :], in1=cc[:T, :], op=ALU.mult), [BK[0], cc], [u2])
                    if i == 0:
                        Vop(lambda e: e.memset(ush1[0:1, :], 0.0), [], [ush1])
                        Vop(lambda e: e.memset(ush2[0:2, :], 0.0), [], [ush2])
                        hist_f = None
                    elif samp:
                        hist_f = lambda e: [e.dma_start(out=ush1[0:1, :], in_=sconv[l, 1:2, :]), e.dma_start(out=ush2[0:2, :], in_=sconv[l, 0:2, :])]
                    else:
                        pu = (i - 1) % 3
                        hist_f = lambda e: [e.dma_start(out=ush1[0:1, :], in_=u2[127:128, pu, :]),
                                            e.dma_start(out=ush2[0:2, :], in_=u2[126:128, pu, :])]

                    def shf(e):
                        r = shift_dma(e, ush1, lambda a, b: u2[a:b, cur, :], 1, T) + shift_dma(e, ush2, lambda a, b: u2[a:b, cur, :], 2, T)
                        if hist_f is not None:
                            r += hist_f(e)
                        return r
                    OP("sync", shf, r=[u2], w=[ush1, ush2], sem=shs, ndma=4 + (0 if hist_f is None else 2))
                    if i == NT - 1:
                        OP("sync", lambda e: e.dma_start(out=nu_p[l], in_=u2[126:128, cur, :]), r=[u2], sem=kos)
                    if samp:
                        OP("sync", lambda e: e.dma_start(out=nu_s[l], in_=u2[62:64, cur, :]), r=[u2], sem=kos)
                S.append(a5)

                def b0():
                    def qkT(e):
                        for g in range(4):
                            e.transpose(bank_bf(4)[:, g * 128:g * 128 + T], q_bf[:T, g * 128:(g + 1) * 128], ident_b[:T, :T])
                        return e.transpose(bank_bf(4)[:, 512:512 + T], k_bf[:T, :], ident_b[:T, :T])
                    Top(qkT, [q_bf, k_bf, ident_b], [BK[4]])
                    Vop(lambda e: e.tensor_copy(out=qT[:, :, :T], in_=bank_bf(4)[:, 0:512].rearrange("p (g t) -> p g t", g=4)[:, :, :T]),
                        [BK[4]], [qT])
                    Aop(lambda e: e.copy(out=kT2[:, cur, :T], in_=bank_bf(4)[:, 512:512 + T]), [BK[4]], [kT2])
                    if samp:
                        ck32 = lambda: S_sb[:, 0, :]
                        OP("sync", lambda e: [e.dma_start(out=ck32()[:, 0:128], in_=ck[l]), e.dma_start(out=S_sb[:, 1, 0:128], in_=cv[l])],
                           w=[S_sb], sem=cks, ndma=2)
                        Vop(lambda e: e.tensor_copy(out=ck_bf[:], in_=S_sb[:, 0, 0:128]), [S_sb], [ck_bf])
                        Vop(lambda e: e.tensor_copy(out=v2[:, prv, :], in_=S_sb[:, 1, 0:128]), [S_sb], [v2])
                        Top(lambda e: e.transpose(bank_bf(4)[:, 0:128], ck_bf[:], ident_b[:]), [ck_bf, ident_b], [BK[4]])
                        Vop(lambda e: e.tensor_copy(out=kT2[:, prv, :], in_=bank_bf(4)[:, 0:128]), [BK[4]], [kT2])
                        OP("sync", lambda e: [e.dma_start(out=nk_s[l, 0:64, :], in_=S_sb[64:128, 0, 0:128]),
                                              e.dma_start(out=nv_s[l, 0:64, :], in_=S_sb[64:128, 1, 0:128])],
                           r=[S_sb], sem=kos, ndma=2)
                S.append(b0)

                for kvh in range(2):
                    pr = slice(kvh * 64, (kvh + 1) * 64)

                    def b_s(kvh=kvh, pr=pr):
                        def smm(e):
                            last = None
                            for g in range(4):
                                if has_prev:
                                    e.matmul(PS[1][:T, g * 256:g * 256 + 128], lhsT=qT[pr, g, :T], rhs=kT2[pr, prv, :], start=True, stop=True)
                                last = e.matmul(PS[1][:T, g * 256 + 128:g * 256 + 128 + T], lhsT=qT[pr, g, :T], rhs=kT2[pr, cur, :T],
                                                start=True, stop=True)
                            return last
                        Top(smm, [qT, kT2], [BK[2], BK[3]])
                        S3 = lambda: PS[1][:, :].rearrange("q (g k) -> q g k", g=4)
                        Vop(lambda e: e.scalar_tensor_tensor(
                            out=S_sb[:T, :, k0:k1], in0=S3()[:T, :, k0:k1], scalar=0.125, in1=biasT[:T, kvh * 4:kvh * 4 + 4, k0:k1],
                            op0=ALU.mult, op1=ALU.add), [BK[2], BK[3], biasT], [S_sb])
                    S.append(b_s)

                    def b_m(kvh=kvh):
                        Vop(lambda e: e.reduce_max(out=sm[:T, 0:4], in_=S_sb[:T, :, k0:k1], axis=AX.X), [S_sb], [sm])
                        Vop(lambda e: e.tensor_tensor(out=sm[:T, 0:4], in0=sm[:T, 0:4], in1=sinkb[:T, kvh * 4:kvh * 4 + 4], op=ALU.max),
                            [sm, sinkb], [sm])
                        Vop(lambda e: e.tensor_scalar_mul(out=sm[:T, 4:8], in0=sm[:T, 0:4], scalar1=-1.0), [sm], [sm])
                        for g in range(4):
                            Aop(lambda e, g=g: e.activation(out=Eb[:T, g, k0:k1], in_=S_sb[:T, g, k0:k1], func=AF.Exp,
                                                            bias=sm[:T, 4 + g:5 + g], scale=1.0, accum_out=sm[:T, 8 + g:9 + g]),
                                [S_sb, sm], [Eb, sm])
                    S.append(b_m)

                    def b_d(kvh=kvh):
                        Vop(lambda e: e.tensor_tensor(out=sm[:T, 12:16], in0=sinkb[:T, kvh * 4:kvh * 4 + 4], in1=sm[:T, 4:8], op=ALU.add),
                            [sm, sinkb], [sm])
                        Aop(lambda e: e.activation(out=sm[:T, 12:16], in_=sm[:T, 12:16], func=AF.Exp), [sm], [sm])
                        Vop(lambda e: e.tensor_tensor(out=sm[:T, 12:16], in0=sm[:T, 12:16], in1=sm[:T, 8:12], op=ALU.add), [sm], [sm])
                        Vop(lambda e: e.reciprocal(out=rden[:T, kvh * 4:kvh * 4 + 4], in_=sm[:T, 12:16]), [sm], [rden])
                    S.append(b_d)

                    def b_p(kvh=kvh):
                        def ptT(e):
                            last = None
                            for g in range(4):
                                if has_prev:
                                    e.transpose(bank_bf(4)[:, (g * 2) * 128:(g * 2) * 128 + T], Eb[:T, g, 0:128], ident_b[:T, :T])
                                last = e.transpose(bank_bf(4)[:T, (g * 2 + 1) * 128:(g * 2 + 1) * 128 + T], Eb[:T, g, 128:128 + T], ident_b[:T, :T])
                            return last
                        Top(ptT, [Eb, ident_b], [BK[4]])
                        if has_prev:
                            Aop(lambda e: e.copy(out=PT[:, :, 0, :T],
                                                 in_=bank_bf(4)[:, :].rearrange("p (g b t) -> p g b t", g=4, b=2)[:, :, 0, :T]), [BK[4]], [PT])
                        Vop(lambda e: e.tensor_copy(out=PT[:T, :, 1, :T],
                                                    in_=bank_bf(4)[:, :].rearrange("p (g b t) -> p g b t", g=4, b=2)[:T, :, 1, :T]), [BK[4]], [PT])
                    S.append(b_p)

                    def b_o(kvh=kvh):
                        def omm(e):
                            last = None
                            for g in range(4):
                                h = kvh * 4 + g
                                o = bank(5)[:T, h * 64:(h + 1) * 64]
                                if has_prev:
                                    e.matmul(o, lhsT=PT[:, g, 0, :T], rhs=v2[:, prv, kvh * 64:(kvh + 1) * 64], start=True, stop=False)
                                last = e.matmul(o, lhsT=PT[:T, g, 1, :T], rhs=v2[:T, cur, kvh * 64:(kvh + 1) * 64], start=(not has_prev), stop=True)
                            return last
                        Top(omm, [PT, v2], [BK[5]])
                    S.append(b_o)
                    if kvh == 0:
                        def b_c1():
                            Vop(lambda e: e.tensor_tensor(out=ct[:T, :], in0=u2[:T, cur, :], in1=cw[:T, 1024:1536], op=ALU.mult), [u2, cw], [ct])
                            Vop(lambda e: e.tensor_tensor(out=ct2[:T, :], in0=ush1[:T, :], in1=cw[:T, 512:1024], op=ALU.mult), [ush1, cw], [ct2])
                            Vop(lambda e: e.tensor_tensor(out=ct[:T, :], in0=ct[:T, :], in1=ct2[:T, :], op=ALU.add), [ct, ct2], [ct])
                        S.append(b_c1)

                        def b_c2():
                            Vop(lambda e: e.tensor_tensor(out=ct2[:T, :], in0=ush2[:T, :], in1=cw[:T, 0:512], op=ALU.mult), [ush2, cw], [ct2])
                            Vop(lambda e: e.tensor_tensor(out=ct[:T, :], in0=ct[:T, :], in1=ct2[:T, :], op=ALU.add), [ct, ct2], [ct])
                            Vop(lambda e: e.tensor_tensor(out=yb_o[:T, :], in0=ct[:T, :], in1=cb[:T, :], op=ALU.mult), [ct, cb], [yb_o])
                        S.append(b_c2)

                def b_n():
                    Vop(lambda e: e.tensor_tensor(out=ya_o[:T, :].rearrange("t (h d) -> t h d", h=8),
                                                  in0=bank(5)[:T, :].rearrange("t (h d) -> t h d", h=8),
                                                  in1=rden[:T, :].unsqueeze(2).to_broadcast([T, 8, 64]), op=ALU.mult), [BK[5], rden], [ya_o])
                S.append(b_n)
                return S[:6], S[6:]

            def steps2(i):
                T, samp, cur, prv, r0 = tinfo(i)
                xin = xin2[i % 2]; hT = hT2[i % 2]
                ya_i = ya2[i % 2]; yb_i = yb2[i % 2]
                S = []

                def tr(src, nch):
                    def f(e):
                        last = None
                        for c in range(nch):
                            last = e.transpose(bank_bf(6)[:, c * 128:c * 128 + T], src[:T, c * 128:(c + 1) * 128], ident_b[:T, :T])
                        return last
                    return f

                def proj(srcT, w, nkc):
                    def f(e):
                        last = None
                        for n in range(2):
                            for c in range(nkc):
                                last = e.matmul(PS[3][:T, n * 512:(n + 1) * 512], lhsT=srcT[:, c, :T], rhs=w[:, c, n * 512:(n + 1) * 512],
                                                start=(c == 0), stop=(c == nkc - 1))
                        return last
                    return f

                def c0():
                    if samp:
                        load_mods(1)
                    Top(tr(ya_i, 4), [ya_i, ident_b], [BK[6]])
                    Aop(lambda e: e.copy(out=yaT[:, :, :T], in_=bank_bf(6)[:, 0:512].rearrange("p (c t) -> p c t", c=4)[:, :, :T]), [BK[6]], [yaT])
                S.append(c0)

                def c1():
                    Top(tr(yb_i, 4), [yb_i, ident_b], [BK[6]])
                    Vop(lambda e: e.tensor_copy(out=ybT[:, :, :T], in_=bank_bf(6)[:, 0:512].rearrange("p (c t) -> p c t", c=4)[:, :, :T]), [BK[6]], [ybT])
                S.append(c1)
                S.append(lambda: Top(proj(yaT, woa, 4), [yaT, woa], [BK[6], BK[7]]))
                for n in range(2):
                    def c2(n=n):
                        zseg(5 + n, n, T, hT)
                        Aop(lambda e: e.activation(out=ta[:T, :], in_=bank(n)[:T, :], func=AF.Tanh, scale=0.5), [BK[n]], [ta])
                        Vop(lambda e: e.scalar_tensor_tensor(out=t1[:T, n * 512:(n + 1) * 512], in0=ta[:T, :], scalar=1.0,
                                                             in1=PS[3][:T, n * 512:(n + 1) * 512], op0=ALU.add, op1=ALU.mult),
                            [ta, BK[6 + n]], [t1])
                    S.append(c2)
                S.append(lambda: Top(proj(ybT, wob, 4), [ybT, wob], [BK[6], BK[7]]))
                for n in range(2):
                    def c3(n=n):
                        zseg(7 + n, n, T, hT)
                        Aop(lambda e: e.activation(out=ta[:T, :], in_=bank(n)[:T, :], func=AF.Tanh, scale=0.5), [BK[n]], [ta])
                        Vop(lambda e: e.scalar_tensor_tensor(out=ta[:T, :], in0=ta[:T, :], scalar=1.0,
                                                             in1=PS[3][:T, n * 512:(n + 1) * 512], op0=ALU.add, op1=ALU.mult),
                            [ta, BK[6 + n]], [ta])
                        Vop(lambda e: e.tensor_tensor(out=pre_bf[:T, n * 512:(n + 1) * 512], in0=ta[:T, :],
                                                      in1=t1[:T, n * 512:(n + 1) * 512], op=ALU.add), [ta, t1], [pre_bf])
                    S.append(c3)

                def c4():
                    Top(tr(pre_bf, 8), [pre_bf, ident_b], [BK[6]])
                    Aop(lambda e: e.copy(out=preT[:, :, :T], in_=bank_bf(6)[:, :].rearrange("p (c t) -> p c t", c=8)[:, :, :T]), [BK[6]], [preT])
                S.append(c4)

                def c5():
                    Top(proj(preT, wo, 8), [preT, wo], [BK[6], BK[7]])
                    Vop(lambda e: e.tensor_tensor(out=t1[:T, :], in0=PS[3][:T, :], in1=G1[:T, :], op=ALU.mult), [BK[6], BK[7], G1], [t1])
                S.append(c5)
                S.append(lambda: Vop(lambda e: e.scalar_tensor_tensor(out=xin[:T, :], in0=xin[:T, :], scalar=ALPHA, in1=t1[:T, :],
                                                                      op0=ALU.mult, op1=ALU.add), [xin, t1], [xin]))
                NHEAD = len(S)
                S.append(lambda: [Vop(lambda e, c=c: e.bn_stats(out=stt[:T, c * 6:(c + 1) * 6], in_=xin[:T, c * 512:(c + 1) * 512]), [xin], [stt])
                                  for c in range(2)])

                def c6():
                    Vop(lambda e: e.bn_aggr(out=mv[:T, 0:2], in_=stt[:T, :]), [stt], [mv])
                    Vop(lambda e: e.tensor_scalar_add(out=mv[:T, 2:3], in0=mv[:T, 1:2], scalar1=LN_EPS), [mv], [mv])
                    Gop(lambda e: e.tensor_tensor(out=mv[:T, 2:3], in0=mv[:T, 2:3], in1=negh[:T, :], op=ALU.pow), [mv, negh], [mv])
                    Vop(lambda e: e.scalar_tensor_tensor(out=mv[:T, 3:4], in0=mv[:T, 0:1], scalar=-1.0, in1=mv[:T, 2:3],
                                                         op0=ALU.mult, op1=ALU.mult), [mv], [mv])
                    Aop(lambda e: e.activation(out=xo[:T, :], in_=xin[:T, :], func=AF.Identity, bias=mv[:T, 3:4], scale=mv[:T, 2:3]),
                        [xin, mv], [xo])
                S.append(c6)
                S.append(lambda: Vop(lambda e: e.tensor_tensor(out=xo[:T, :], in0=xo[:T, :], in1=l1g[:T, :], op=ALU.mult), [xo, l1g], [xo]))

                def c7():
                    Vop(lambda e: e.tensor_tensor(out=xo[:T, :], in0=xo[:T, :], in1=l1b[:T, :], op=ALU.add), [xo, l1b], [xo])
                    OP("sync", lambda e: e.dma_start(out=X[r0:r0 + T, :], in_=xo[:T]), r=[xo], w=[dXt[i]], sem=xosem)
                S.append(c7)
                S.append(lambda: Vop(lambda e: e.tensor_tensor(out=t1[:T, :], in0=xo[:T, :], in1=SC2[:T, :], op=ALU.mult), [xo, SC2], [t1]))
                S.append(lambda: Vop(lambda e: e.tensor_tensor(out=h2b[:T, :], in0=t1[:T, :], in1=SH2[:T, :], op=ALU.add), [t1, SH2], [h2b]))

                def c8():
                    Top(tr(h2b, 8), [h2b, ident_b], [BK[6]])
                    Vop(lambda e: e.tensor_copy(out=h2T[:, :, :T], in_=bank_bf(6)[:, :].rearrange("p (c t) -> p c t", c=8)[:, :, :T]), [BK[6]], [h2T])
                S.append(c8)

                def c9():
                    def rmm(e):
                        for kc in range(8):
                            e.matmul(bank(6)[:T, 0:32], lhsT=h2T[:, kc, :T], rhs=wr[:, kc, :], start=(kc == 0), stop=False)
                        return e.matmul(bank(6)[:T, 0:32], lhsT=ones33[:, :T], rhs=br33[:, :], start=False, stop=True)
                    Top(rmm, [h2T, wr, ones33, br33], [BK[6]])
                    Vop(lambda e: e.tensor_copy(out=lg[:T, :], in_=bank(6)[:T, 0:32]), [BK[6]], [lg])
                    Vop(lambda e: e.max(out=top8[:T, :], in_=lg[:T, :]), [lg], [top8])
                    Vop(lambda e: e.tensor_scalar_mul(out=rt[:T, 0:1], in0=top8[:T, 0:1], scalar1=-1.0), [top8], [rt])
                    Aop(lambda e: e.activation(out=rt[:T, 4:8], in_=top8[:T, 0:4], func=AF.Exp, bias=rt[:T, 0:1], scale=1.0,
                                               accum_out=rt[:T, 1:2]), [top8, rt], [rt])
                    Vop(lambda e: e.tensor_scalar(out=Mbf[:T, :], in0=lg[:T, :], scalar1=top8[:T, 3:4], scalar2=None, op0=ALU.is_ge),
                        [lg, top8], [Mbf])
                S.append(c9)

                def c10():
                    def pmm(e):
                        e.matmul(bank(6)[:T, 64:96], lhsT=ustrict[:T, :T], rhs=Mbf[:T, :], start=True, stop=True)
                        return e.matmul(bank(6)[:, 128:160], lhsT=ones_b[:T, :], rhs=Mbf[:T, :], start=True, stop=True)
                    Top(pmm, [ustrict, ones_b, Mbf], [BK[6]])
                    Vop(lambda e: e.reciprocal(out=rt[:T, 2:3], in_=rt[:T, 1:2]), [rt], [rt])
                    Vop(lambda e: e.tensor_tensor(out=pos[:T, :], in0=bank(6)[:T, 64:96], in1=cnt[:T, :], op=ALU.add), [BK[6], cnt], [pos])
                    Vop(lambda e: e.tensor_tensor(out=cnt[:, :], in0=bank(6)[:, 128:160], in1=cnt[:, :], op=ALU.add), [BK[6], cnt], [cnt])
                S.append(c10)

                def c11():
                    Vop(lambda e: e.tensor_scalar(out=oh[:T, :], in0=pos[:T, :], scalar1=float(C), scalar2=BIG, op0=ALU.is_ge, op1=ALU.mult),
                        [pos], [oh])
                    Vop(lambda e: e.tensor_tensor(out=pos[:T, :], in0=pos[:T, :], in1=oh[:T, :], op=ALU.add), [pos, oh], [pos])
                    Vop(lambda e: e.tensor_tensor(out=pos[:T, :], in0=pos[:T, :], in1=iotac[:T, :], op=ALU.add), [pos, iotac], [pos])
                S.append(c11)
                for k in range(4):
                    def c12(k=k):
                        Vop(lambda e: e.tensor_scalar(out=oh[:T, :], in0=lg[:T, :], scalar1=top8[:T, k:k + 1], scalar2=None, op0=ALU.is_equal),
                            [lg, top8], [oh])
                        Vop(lambda e: e.tensor_tensor(out=oh[:T, :], in0=oh[:T, :], in1=pos[:T, :], op=ALU.mult), [oh, pos], [oh])
                        Vop(lambda e: e.reduce_sum(out=destf[:T, k:k + 1], in_=oh[:T, :], axis=AX.X), [oh], [destf])
                    S.append(c12)

                def c13():
                    Vop(lambda e: e.tensor_copy(out=DEST[:T, i, :], in_=destf[:T, :]), [destf], [DEST])
                    Vop(lambda e: e.tensor_scalar(out=rt[:T, 8:12], in0=destf[:T, :], scalar1=float(NSLOT), scalar2=None, op0=ALU.is_lt), [destf], [rt])
                    Vop(lambda e: e.tensor_scalar(out=rt[:T, 4:8], in0=rt[:T, 4:8], scalar1=rt[:T, 2:3], scalar2=None, op0=ALU.mult), [rt], [rt])
                    Vop(lambda e: e.tensor_tensor(out=GATES[:T, i, :], in0=rt[:T, 4:8], in1=rt[:T, 8:12], op=ALU.mult), [rt], [GATES])
                    for k in range(4):
                        OP("gpsimd", lambda e, k=k: e.indirect_dma_start(
                            out=XS, out_offset=bass.IndirectOffsetOnAxis(ap=DEST[:T, i, k:k + 1], axis=0), in_=h2b[:T, :], in_offset=None,
                            bounds_check=bcreg(e), oob_is_err=False), r=[h2b, DEST, dXS], sem=scs[k])
                S.append(c13)
                return S[:NHEAD], S[NHEAD:]

            def run_merged(A, B):
                na, nb = len(A), len(B)
                ia = ib = 0
                while ia < na or ib < nb:
                    if ib >= nb or (ia < na and ia * nb <= ib * na):
                        A[ia](); ia += 1
                    else:
                        B[ib](); ib += 1

            load_x(0)
            A0, at0 = steps1(0)
            run_merged(A0 + at0, [])
            for i in range(NTILE):
                head2, tail2 = steps2(i)
                run_merged(head2, [])
                if i + 1 < NTILE:
                    load_x(i + 1)
                    A1, at1 = steps1(i + 1)
                    run_merged(tail2, A1 + at1)
                else:
                    run_merged(tail2, [])

    def phase_experts(l):
        with ExitStack() as sp:
            wgu = [sb(f"wgu{j}", [128, 8, 2 * DFF], BF16, sp) for j in range(2)]
            wdn = [sb(f"wdn{j}", [128, 8, D], BF16, sp) for j in range(2)]
            xT = [sb(f"xT{j}", [128, 8, C], BF16, sp) for j in range(2)]
            bdst = [sb(f"bdst{j}", [33, D], F32, sp) for j in range(2)]
            bd33 = [sb(f"bd33{j}", [33, D], BF16, sp) for j in range(2)]
            aT = sb("aT", [128, 8, C], BF16, sp)
            ones33 = sb("ones33e", [33, 128], BF16, sp)
            bgu_r = sb("bgu_r", [32, 2 * DFF], F32, sp)
            bguT = sb("bguT", [128, 16, 32], F32, sp)
            gc = [sb(f"gc{j}", [128, C], F32, sp) for j in range(2)]
            sg = [sb(f"sg{j}", [128, C], F32, sp) for j in range(2)]
            l1 = [sb(f"l1{j}", [128, C], F32, sp) for j in range(2)]
            ysb = [sb(f"ysb{j}", [128, D], F32, sp) for j in range(2)]
            wgs = [sem(f"ewg{l}{j}") for j in range(2)]; wds = [sem(f"ewd{l}{j}") for j in range(2)]
            xts = [sem(f"ext{l}{j}") for j in range(2)]; bds = [sem(f"ebd{l}{j}") for j in range(2)]
            yss = [sem(f"eys{l}{j}") for j in range(2)]; es0 = sem(f"es0{l}")
            Vop(lambda e: e.memset(ones33[:], 1.0), [], [ones33])
            if l == 0:
                print("phase E sbuf remaining", nc.sbuf_bytes_remaining)
            for j in range(2):
                Vop(lambda e, j=j: e.memset(bdst[j][:], 0.0), [], [bdst[j]])
            OP("sync", lambda e: e.dma_start(out=bgu_r[:], in_=b_gu[l]), w=[bgu_r], sem=es0)
            for half in range(2):
                def tb(e, half=half):
                    last = None
                    for c in range(8):
                        fc = half * 8 + c
                        last = e.transpose(PS[half][0:128, c * 32:(c + 1) * 32], bgu_r[:, fc * 128:(fc + 1) * 128], ident_f[:32, :32])
                    return last
                Top(tb, [bgu_r, ident_f], [BK[2 * half]])
                Vop(lambda e, half=half: e.tensor_copy(out=bguT[:, half * 8:(half + 1) * 8, :],
                                                       in_=PS[half][:, 0:256].rearrange("p (c e) -> p c e", c=8)), [BK[2 * half]], [bguT])

            def load_expert(ex):
                j = ex % 2
                cast_load(wgu[j], lambda a, b: wgu[j][:, :, a:b], lambda a, b: w_gu[l, ex][:, a:b].rearrange("(kc p) f -> p kc f", p=128), 2 * DFF, wgs[j])
                cast_load(wdn[j], lambda a, b: wdn[j][:, :, a:b], lambda a, b: w_down[l, ex][:, a:b].rearrange("(kc p) f -> p kc f", p=128), D, wds[j])
                OP("sync", lambda e: [e.dma_start(out=bdst[j][r:r + 1, :], in_=b_down[l, ex:ex + 1, :]) for r in (0, 32)],
                   w=[bdst[j]], sem=bds[j], ndma=2)
                OP("sync", lambda e: e.dma_start_transpose(out=xT[j][:], in_=XS[ex * C:(ex + 1) * C, :]), r=[dXS], w=[xT[j]], sem=xts[j])

            load_expert(0)
            ydma = 0
            for ex in range(NEXP):
                j = ex % 2
                if ex + 1 < NEXP:
                    load_expert(ex + 1)
                Vop(lambda e, j=j: e.tensor_scalar_mul(out=bdst[j][:], in0=bdst[j][:], scalar1=2.0), [bdst[j]], [bdst[j]])
                Vop(lambda e, j=j: e.tensor_copy(out=bd33[j][:], in_=bdst[j][:]), [bdst[j]], [bd33[j]])
                Vop(lambda e, j=j: e.tensor_sub(out=bdst[j][:], in0=bdst[j][:], in1=bd33[j][:]), [bdst[j], bd33[j]], [bdst[j]])
                Vop(lambda e, j=j: e.tensor_copy(out=bd33[j][32:33, :], in_=bdst[j][32:33, :]), [bdst[j]], [bd33[j]])
                chunks = [(c, min(c + 512, C)) for c in range(0, C, 512)]
                for pj in range(8):
                    s = pj % 2
                    pg, pl = PS[2 * s], PS[2 * s + 1]
                    bg_, bl_ = [BK[4 * s], BK[4 * s + 1]], [BK[4 * s + 2], BK[4 * s + 3]]

                    def gu(e, pj=pj, j=j, pg=pg, pl=pl):
                        last = None
                        for (dst, fc) in ((pg, pj), (pl, pj + 8)):
                            for (a, b) in chunks:
                                for kc in range(8):
                                    last = e.matmul(dst[:, a:b], lhsT=wgu[j][:, kc, fc * 128:(fc + 1) * 128], rhs=xT[j][:, kc, a:b],
                                                    start=(kc == 0), stop=(kc == 7))
                        return last
                    Top(gu, [wgu[j], xT[j]], bg_ + bl_)
                    Vop(lambda e, s=s, pg=pg, pj=pj, ex=ex: e.tensor_scalar(out=gc[s][:, :], in0=pg[:, 0:C], scalar1=bguT[:, pj, ex:ex + 1], scalar2=7.0,
                                                                           op0=ALU.add, op1=ALU.min), bg_ + [bguT], [gc[s]])
                    Aop(lambda e, s=s: e.activation(out=sg[s][:, :], in_=gc[s][:, :], func=AF.Tanh, scale=0.851), [gc[s]], [sg[s]])
                    Vop(lambda e, s=s, pl=pl, pj=pj, ex=ex: e.tensor_scalar(out=l1[s][:, :], in0=pl[:, 0:C], scalar1=bguT[:, pj + 8, ex:ex + 1], scalar2=-7.0,
                                                                           op0=ALU.add, op1=ALU.max), bl_ + [bguT], [l1[s]])
                    Vop(lambda e, s=s: e.tensor_scalar(out=l1[s][:, :], in0=l1[s][:, :], scalar1=7.0, scalar2=1.0, op0=ALU.min, op1=ALU.add), [l1[s]], [l1[s]])
                    Vop(lambda e, s=s: e.tensor_tensor(out=gc[s][:, :], in0=gc[s][:, :], in1=l1[s][:, :], op=ALU.mult), [gc[s], l1[s]], [gc[s]])
                    Vop(lambda e, s=s, pj=pj: e.scalar_tensor_tensor(out=aT[:, pj, :], in0=sg[s][:, :], scalar=1.0, in1=gc[s][:, :], op0=ALU.add, op1=ALU.mult),
                        [sg[s], gc[s]], [aT])
                for t in range(CT):
                    pd = t % 4
                    yj = ydma % 2
                    ydma += 1

                    def dn(e, t=t, j=j, pd=pd):
                        last = None
                        for n in range(2):
                            o = PS[pd][:, n * 512:(n + 1) * 512]
                            for fc in range(8):
                                e.matmul(o, lhsT=aT[:, fc, t * 128:(t + 1) * 128], rhs=wdn[j][:, fc, n * 512:(n + 1) * 512], start=(fc == 0), stop=False)
                            last = e.matmul(o, lhsT=ones33[:, :], rhs=bd33[j][:, n * 512:(n + 1) * 512], start=False, stop=True)
                        return last
                    Top(dn, [aT, wdn[j], ones33, bd33[j]], [BK[2 * pd], BK[2 * pd + 1]])
                    Aop(lambda e, pd=pd, yj=yj: e.activation(out=ysb[yj][:, :], in_=PS[pd][:, :], func=AF.Copy, scale=0.5), [BK[2 * pd], BK[2 * pd + 1]], [ysb[yj]])
                    OP("sync", lambda e, ex=ex, t=t, yj=yj: e.dma_start(out=YS[ex * C + t * 128:ex * C + (t + 1) * 128, :], in_=ysb[yj][:, :]),
                       r=[ysb[yj]], sem=yss[yj])

    def phase_combine(l):
        with ExitStack() as sp:
            G2 = sb("G2", [128, D], F32, sp); l2g = sb("l2g", [128, D], F32, sp); l2b = sb("l2b", [128, D], F32, sp)
            x1 = [sb(f"x1{j}", [128, D], F32, sp) for j in range(2)]
            x2b = [sb(f"x2b{j}", [128, D], F32, sp) for j in range(2)]
            xo = [sb(f"xoc{j}", [128, D], F32, sp) for j in range(2)]
            acc = sb("acc", [128, D], F32, sp)
            yg = [[sb(f"yg{j}{k}", [128, D], F32, sp) for k in range(4)] for j in range(2)]
            stt = [sb(f"stt_c{j}", [128, 12], F32, sp) for j in range(2)]; mv = [sb(f"mv_c{j}", [128, 4], F32, sp) for j in range(2)]
            cs = sem(f"cc{l}"); gs = [sem(f"cg{l}{j}") for j in range(2)]; xs_ = [sem(f"cx{l}{j}") for j in range(2)]
            os_ = [sem(f"co{l}{j}") for j in range(2)]
            OP("sync", lambda e: [e.dma_start(out=l2g[:], in_=ln2_g[l:l + 1, :].partition_broadcast(128)),
                                  e.dma_start(out=l2b[:], in_=ln2_b[l:l + 1, :].partition_broadcast(128))], w=[l2g, l2b], sem=cs, ndma=2)
            for j in range(2):
                for k in range(4):
                    Vop(lambda e, j=j, k=k: e.memset(yg[j][k][:], 0.0), [], [yg[j][k]])

            def load_g2(g):
                OP("sync", lambda e: e.dma_start(out=G2[:], in_=MODS[l, g:g + 1, 5 * D:6 * D].partition_broadcast(128)), r=[dMODS], w=[G2], sem=cs)
            load_g2(0)

            def tl(i):
                return 128 if i < NT else DEC

            def prefetch(i):
                T = tl(i)
                j = i % 2
                r0 = i * 128
                OP("sync", lambda e: e.dma_start(out=x1[j][:T], in_=X[r0:r0 + T, :]), r=[dXt[i]], w=[x1[j]], sem=xs_[j])
                OP("gpsimd", lambda e: [e.indirect_dma_start(
                    out=yg[j][k][:, :], out_offset=None, in_=YS, in_offset=bass.IndirectOffsetOnAxis(ap=DEST[:, i, k:k + 1], axis=0),
                    bounds_check=bcreg(e), oob_is_err=False) for k in range(4)], r=[dYS, DEST], w=yg[j], sem=gs[j], ndma=4)

            def S1(i):
                T = tl(i)
                j = i % 2
                if i == NT:
                    load_g2(1)
                Vop(lambda e: e.tensor_scalar(out=acc[:T, :], in0=yg[j][0][:T, :], scalar1=GATES[:T, i, 0:1], scalar2=None, op0=ALU.mult),
                    [yg[j][0], GATES], [acc])
                for k in range(1, 4):
                    Vop(lambda e, k=k: e.scalar_tensor_tensor(out=acc[:T, :], in0=yg[j][k][:T, :], scalar=GATES[:T, i, k:k + 1],
                                                              in1=acc[:T, :], op0=ALU.mult, op1=ALU.add), [yg[j][k], GATES, acc], [acc])
                Vop(lambda e: e.tensor_tensor(out=acc[:T, :], in0=acc[:T, :], in1=G2[:T, :], op=ALU.mult), [acc, G2], [acc])
                Vop(lambda e: e.scalar_tensor_tensor(out=x2b[j][:T, :], in0=x1[j][:T, :], scalar=ALPHA, in1=acc[:T, :], op0=ALU.mult, op1=ALU.add),
                    [x1[j], acc], [x2b[j]])
                ln_stats(x2b[j], stt[j], mv[j], T)

            def S2(i):
                T = tl(i)
                j = i % 2
                r0 = i * 128
                ln_apply(x2b[j], xo[j], mv[j], l2g, l2b, T)
                if l == DEPTH - 1:
                    dst = y_p[r0:r0 + T, :] if i < NT else y_s
                    OP("sync", lambda e: e.dma_start(out=dst, in_=xo[j][:T]), r=[xo[j]], sem=os_[j])
                else:
                    OP("sync", lambda e: e.dma_start(out=X[r0:r0 + T, :], in_=xo[j][:T]), r=[xo[j]], w=[dXt[i]], sem=os_[j])
            prefetch(0)
            if NTILE > 1:
                prefetch(1)
            S1(0)
            for i in range(NTILE):
                if i + 2 < NTILE:
                    prefetch(i + 2)
                if i + 1 < NTILE:
                    S1(i + 1)
                S2(i)

    for l in range(DEPTH):
        phase_mix(l)
        OP("sync", lambda e, l=l: e.dma_start(out=cnt_out[l], in_=cnt[:]), r=[cnt], sem=dsem)
        P.barrier()
        if dbg and l == 0:
            OP("sync", lambda e: [e.dma_start(out=dbgX1, in_=X), e.dma_start(out=dbgD, in_=DEST[:].rearrange("p i k -> p (i k)")),
                                  e.dma_start(out=dbgG, in_=GATES[:].rearrange("p i k -> p (i k)")), e.dma_start(out=dbgXS, in_=XS)], sem=dsem, ndma=4)
            P.barrier()
        phase_experts(l)
        P.barrier()
        if dbg and l == 0:
            OP("sync", lambda e: [e.dma_start(out=dbgYS, in_=YS)], sem=dsem, ndma=1)
            P.barrier()
        phase_combine(l)
        P.barrier()

    block = E(nc.Block())
    P.emit(block, esems)
    st.close()
    return nc


_CACHE = {}
CAP = 768


def make_in_maps(inputs, ncores, NT, C):
    cst = host_consts(C)
    f = lambda a: np.ascontiguousarray(np.asarray(a, dtype=np.float32))
    shared = {
        "rel_table": f(inputs["rel_table"]), "ln0_g": f(inputs["ln0_g"]).reshape(1, D), "ln0_b": f(inputs["ln0_b"]).reshape(1, D),
        "w_ada": f(inputs["w_ada"]), "b_ada": f(inputs["b_ada"]), "w_in": f(inputs["w_in"]), "b_in": f(inputs["b_in"]),
        "sinks": f(inputs["sinks"]), "conv_w": f(inputs["conv_w"]).reshape(DEPTH, 3 * 512),
        "w_oa": f(inputs["w_oa"]), "w_ob": f(inputs["w_ob"]), "w_o": f(inputs["w_o"]),
        "ln1_g": f(inputs["ln1_g"]), "ln1_b": f(inputs["ln1_b"]), "w_router": f(inputs["w_router"]), "b_router": f(inputs["b_router"]),
        "w_gu": f(inputs["w_gu"]), "b_gu": f(inputs["b_gu"]), "w_down": f(inputs["w_down"]), "b_down": f(inputs["b_down"]),
        "ln2_g": f(inputs["ln2_g"]), "ln2_b": f(inputs["ln2_b"]),
    }
    shared.update(cst)
    maps = []
    for c in range(ncores):
        m = dict(shared)
        m["xp"] = f(inputs["x_prompt"][c][:NT * 128])
        m["xsm"] = f(inputs["x_sample"][c])
        m["cvec"] = np.stack([f(inputs["c_prompt"][c]), f(inputs["c_sample"][c])], 0)
        m["ck"] = f(inputs["cache_k"][:, c]).reshape(DEPTH, 128, 128)
        m["cv"] = f(inputs["cache_v"][:, c]).reshape(DEPTH, 128, 128)
        m["sconv"] = f(inputs["state_conv"][:, c])
        maps.append(m)
    return maps


def kernel(**inputs):
    NT = SEQ // 128
    key = (NT, CAP)
    if key not in _CACHE:
        _CACHE[key] = build(NT, CAP)
    nc = _CACHE[key]
    maps = make_in_maps(inputs, NCORES, NT, CAP)
    res = run_bass_kernel_spmd(nc, maps, core_ids=list(range(NCORES)))
    R = res.results
    try:
        cm = np.stack([np.asarray(R[c]["cnt_out"], dtype=np.float32)[:, 0, :] for c in range(NCORES)], 0)
        print("[kernel] expert load per (core, layer): max=%d min=%d cap=%d" % (cm.max(), cm.min(), CAP), flush=True)
        print("[kernel] per core/layer max:", cm.max(axis=2).astype(int).tolist(), flush=True)
    except Exception as ex:
        print("[kernel] cnt diag failed", ex)
    g = lambda name: np.stack([np.asarray(R[c][name], dtype=np.float32) for c in range(NCORES)], 0)
    y_prompt = g("y_p")
    y_sample = g("y_s")

    def kvo(name):
        a = g(name)
        return np.ascontiguousarray(a.transpose(1, 0, 2, 3)).reshape(DEPTH, NCORES, 128, 2, 64)

    def uo(name):
        return np.ascontiguousarray(g(name).transpose(1, 0, 2, 3))
    return (y_prompt, y_sample, kvo("nk_p"), kvo("nv_p"), uo("nu_p"), kvo("nk_s"), kvo("nv_s"), uo("nu_s"))
```
